# Optimizing a Trainium2 kernel written in Bass

```python
import jax, jax.numpy as jnp
from jax import lax
import numpy as np

D_MODEL = 1024
BATCH = 8
SEQ = 2048
DEPTH = 2
DEC_BATCH = 128
DEC_SEQ = 4
PAST_LEN = 16384
PAGE_SIZE = 128

N_MIXERS = 2
N_LRU_LAYERS = (DEPTH + 1) // 2
N_CONF_LAYERS = DEPTH // 2
D_RNN = (D_MODEL * 5) // 4
N_RNN_HEADS = 10
RNN_HEAD_DIM = D_RNN // N_RNN_HEADS
LRU_CONV_W = 4
LRU_C = 8.0
D_CONF = D_MODEL
CONF_CONV_W = 31
N_PEER_HEADS = 8
N_KEYS = 128
N_EXPERTS = N_KEYS * N_KEYS
D_KEY = 256
HALF_KEY = D_KEY // 2
TOPK_HALF = 16
TOPK = 16
PEER_BLOCK = 256
DEEPNORM_ALPHA = (2 * DEPTH) ** 0.25
DEEPNORM_BETA = (8 * DEPTH) ** -0.25
LN_EPS = 1e-5

kernel_name = "hybrid_rglru_conformer_peer_step"


def layer_norm(x, g, b):
    xf = x.astype(jnp.float32)
    mu = jnp.mean(xf, axis=-1, keepdims=True)
    var = jnp.mean(jnp.square(xf - mu), axis=-1, keepdims=True)
    return ((xf - mu) * lax.rsqrt(var + LN_EPS)).astype(x.dtype) * g + b


def causal_depthwise_conv(buf, x, w, b):
    xp = jnp.concatenate([buf.astype(x.dtype), x], axis=1)
    c = x.shape[-1]
    y = lax.conv_general_dilated(xp, w[:, None, :].astype(x.dtype), (1,), 'VALID',
                                 dimension_numbers=('NWC', 'WIO', 'NWC'),
                                 feature_group_count=c) + b
    return y, xp[:, -(w.shape[0] - 1):]


def rg_lru(xc, h0, w_a, b_a, w_i, b_i, lam):
    bsz, L, _ = xc.shape
    xh = xc.reshape(bsz, L, N_RNN_HEADS, RNN_HEAD_DIM)
    r = jax.nn.sigmoid(jnp.einsum('blhi,hij->blhj', xh, w_a).reshape(bsz, L, D_RNN) + b_a)
    ig = jax.nn.sigmoid(jnp.einsum('blhi,hij->blhj', xh, w_i).reshape(bsz, L, D_RNN) + b_i)
    log_a = (-LRU_C * r.astype(jnp.float32)) * jax.nn.softplus(-lam.astype(jnp.float32))
    a = jnp.exp(log_a)
    bterm = jnp.sqrt(-jnp.expm1(2.0 * log_a)) * (ig * xc).astype(jnp.float32)

    def combine(left, right):
        a1, b1 = left
        a2, b2 = right
        return a1 * a2, a2 * b1 + b2

    a_cum, b_cum = lax.associative_scan(combine, (a, bterm), axis=1)
    h = a_cum * h0.astype(jnp.float32)[:, None, :] + b_cum
    return h.astype(xc.dtype), h[:, -1].astype(h0.dtype)


def lru_mixer(h_in, h0, conv0, w_in, conv_w, conv_b, w_a, b_a, w_i, b_i, lam, w_out):
    proj = h_in @ w_in
    xb, gb = proj[..., :D_RNN], proj[..., D_RNN:]
    xc, conv_new = causal_depthwise_conv(conv0, xb, conv_w, conv_b)
    y, h_new = rg_lru(xc, h0, w_a, b_a, w_i, b_i, lam)
    out = (y * jax.nn.gelu(gb, approximate=False)) @ w_out
    return out, h_new, conv_new


def conformer_mixer(h_in, buf0, w_pw1, b_pw1, dw_w, dw_b, ln_g, ln_b, w_pw2, b_pw2):
    p = h_in @ w_pw1 + b_pw1
    u = p[..., :D_CONF] * jax.nn.sigmoid(p[..., D_CONF:])
    d, buf_new = causal_depthwise_conv(buf0, u, dw_w, dw_b)
    d = jax.nn.silu(layer_norm(d, ln_g, ln_b))
    return d @ w_pw2 + b_pw2, buf_new


def peer(xm, w_q, keys, u_tab, v_tab):
    T, D = xm.shape
    nb = -(-T // PEER_BLOCK)
    xs = jnp.pad(xm, ((0, nb * PEER_BLOCK - T), (0, 0))).reshape(nb, PEER_BLOCK, D)

    def block(xb):
        q = (xb @ w_q).reshape(PEER_BLOCK, N_PEER_HEADS, 2, HALF_KEY)
        s = jnp.einsum('thpd,hpnd->thpn', q, keys).astype(jnp.float32)
        v1, i1 = lax.top_k(s[:, :, 0], TOPK_HALF)
        v2, i2 = lax.top_k(s[:, :, 1], TOPK_HALF)
        cand = (v1[..., :, None] + v2[..., None, :]).reshape(PEER_BLOCK, N_PEER_HEADS, TOPK_HALF * TOPK_HALF)
        cand_idx = (i1[..., :, None] * N_KEYS + i2[..., None, :]).reshape(PEER_BLOCK, N_PEER_HEADS, TOPK_HALF * TOPK_HALF)
        top_s, pos = lax.top_k(cand, TOPK)
        expert = jnp.take_along_axis(cand_idx, pos, axis=-1)
        g = jax.nn.softmax(top_s, axis=-1).astype(xb.dtype)
        u = u_tab[expert]
        v = v_tab[expert]
        act = jax.nn.gelu(jnp.einsum('td,thkd->thk', xb, u), approximate=False)
        return jnp.einsum('thk,thkd->td', g * act, v)

    return lax.map(block, xs).reshape(nb * PEER_BLOCK, D)[:T]


def trunk(x, c, lru_h0, lru_conv0, conf_conv0, p):
    bsz, L, D = x.shape
    new_h, new_lconv, new_cconv = [], [], []
    for i in range(DEPTH):
        mod = jax.nn.silu(c) @ p['w_ada'][i] + p['b_ada'][i]
        sh_m, sc_m, g_m, sh_f, sc_f, g_f = jnp.split(mod[:, None, :], 6, axis=-1)
        hm = x * (1 + sc_m) + sh_m
        j = i // N_MIXERS
        if i % N_MIXERS == 0:
            out, hn, cn = lru_mixer(hm, lru_h0[j], lru_conv0[j], p['lru_w_in'][j], p['lru_conv_w'][j],
                                    p['lru_conv_b'][j], p['lru_w_a'][j], p['lru_b_a'][j], p['lru_w_i'][j],
                                    p['lru_b_i'][j], p['lru_lambda'][j], p['lru_w_out'][j])
            new_h.append(hn)
            new_lconv.append(cn)
        else:
            out, cn = conformer_mixer(hm, conf_conv0[j], p['conf_w_pw1'][j], p['conf_b_pw1'][j],
                                      p['conf_dw_w'][j], p['conf_dw_b'][j], p['conf_ln_g'][j],
                                      p['conf_ln_b'][j], p['conf_w_pw2'][j], p['conf_b_pw2'][j])
            new_cconv.append(cn)
        x = layer_norm(DEEPNORM_ALPHA * x + g_m * out, p['ln_mix_g'][i], p['ln_mix_b'][i])
        hf = x * (1 + sc_f) + sh_f
        f = peer(hf.reshape(bsz * L, D), p['peer_w_q'][i], p['peer_keys'][i],
                 p['peer_u'][i], p['peer_v'][i]).reshape(bsz, L, D)
        x = layer_norm(DEEPNORM_ALPHA * x + g_f * f, p['ln_ffn_g'][i], p['ln_ffn_b'][i])
    return x, jnp.stack(new_h), jnp.stack(new_lconv), jnp.stack(new_cconv)


def setup_inputs(seed: int = 0) -> dict:
    key = jax.random.key(seed)
    ks = jax.random.split(key, 40)
    nrm = lambda k, shape, s: jax.random.normal(k, shape, jnp.float32) * s
    D = D_MODEL
    a0 = jax.random.uniform(ks[20], (N_LRU_LAYERS, D_RNN), jnp.float32, 0.9, 0.999)
    p0 = a0 ** (1.0 / LRU_C)
    return {
        'x_prompt': nrm(ks[0], (BATCH, SEQ, D), 1.0),
        'x_sample': nrm(ks[1], (DEC_BATCH, DEC_SEQ, D), 1.0),
        'state_lru_h': nrm(ks[2], (N_LRU_LAYERS, DEC_BATCH, D_RNN), 0.5),
        'state_lru_conv': nrm(ks[3], (N_LRU_LAYERS, DEC_BATCH, LRU_CONV_W - 1, D_RNN), 1.0),
        'state_conf_conv': nrm(ks[4], (N_CONF_LAYERS, DEC_BATCH, CONF_CONV_W - 1, D_CONF), 0.5),
        'c_prompt': nrm(ks[5], (BATCH, D), 1.0),
        'c_sample': nrm(ks[6], (DEC_BATCH, D), 1.0),
        'w_ada': nrm(ks[7], (DEPTH, D, 6 * D), 0.5 * D ** -0.5),
        'b_ada': nrm(ks[8], (DEPTH, 6 * D), 0.01),
        'ln_mix_g': 1.0 + nrm(ks[9], (DEPTH, D), 0.05),
        'ln_mix_b': nrm(ks[10], (DEPTH, D), 0.01),
        'ln_ffn_g': 1.0 + nrm(ks[11], (DEPTH, D), 0.05),
        'ln_ffn_b': nrm(ks[12], (DEPTH, D), 0.01),
        'lru_w_in': nrm(ks[13], (N_LRU_LAYERS, D, 2 * D_RNN), D ** -0.5),
        'lru_conv_w': nrm(ks[14], (N_LRU_LAYERS, LRU_CONV_W, D_RNN), LRU_CONV_W ** -0.5),
        'lru_conv_b': nrm(ks[15], (N_LRU_LAYERS, D_RNN), 0.01),
        'lru_w_a': nrm(ks[16], (N_LRU_LAYERS, N_RNN_HEADS, RNN_HEAD_DIM, RNN_HEAD_DIM), RNN_HEAD_DIM ** -0.5),
        'lru_b_a': nrm(ks[17], (N_LRU_LAYERS, D_RNN), 0.01),
        'lru_w_i': nrm(ks[18], (N_LRU_LAYERS, N_RNN_HEADS, RNN_HEAD_DIM, RNN_HEAD_DIM), RNN_HEAD_DIM ** -0.5),
        'lru_b_i': nrm(ks[19], (N_LRU_LAYERS, D_RNN), 0.01),
        'lru_lambda': jnp.log(p0) - jnp.log1p(-p0),
        'lru_w_out': nrm(ks[21], (N_LRU_LAYERS, D_RNN, D), DEEPNORM_BETA * D_RNN ** -0.5),
        'conf_w_pw1': nrm(ks[22], (N_CONF_LAYERS, D, 2 * D_CONF), D ** -0.5),
        'conf_b_pw1': nrm(ks[23], (N_CONF_LAYERS, 2 * D_CONF), 0.01),
        'conf_dw_w': nrm(ks[24], (N_CONF_LAYERS, CONF_CONV_W, D_CONF), CONF_CONV_W ** -0.5),
        'conf_dw_b': nrm(ks[25], (N_CONF_LAYERS, D_CONF), 0.01),
        'conf_ln_g': 1.0 + nrm(ks[26], (N_CONF_LAYERS, D_CONF), 0.05),
        'conf_ln_b': nrm(ks[27], (N_CONF_LAYERS, D_CONF), 0.01),
        'conf_w_pw2': nrm(ks[28], (N_CONF_LAYERS, D_CONF, D), DEEPNORM_BETA * D_CONF ** -0.5),
        'conf_b_pw2': nrm(ks[29], (N_CONF_LAYERS, D), 0.01),
        'peer_w_q': nrm(ks[30], (DEPTH, D, N_PEER_HEADS * D_KEY), D ** -0.5),
        'peer_keys': nrm(ks[31], (DEPTH, N_PEER_HEADS, 2, N_KEYS, HALF_KEY), HALF_KEY ** -0.5),
        'peer_u': nrm(ks[32], (DEPTH, N_EXPERTS, D), D ** -0.5),
        'peer_v': nrm(ks[33], (DEPTH, N_EXPERTS, D), 0.5 * DEEPNORM_BETA),
    }


def reference(x_prompt, x_sample, state_lru_h, state_lru_conv, state_conf_conv, c_prompt, c_sample,
              w_ada, b_ada, ln_mix_g, ln_mix_b, ln_ffn_g, ln_ffn_b,
              lru_w_in, lru_conv_w, lru_conv_b, lru_w_a, lru_b_a, lru_w_i, lru_b_i, lru_lambda, lru_w_out,
              conf_w_pw1, conf_b_pw1, conf_dw_w, conf_dw_b, conf_ln_g, conf_ln_b, conf_w_pw2, conf_b_pw2,
              peer_w_q, peer_keys, peer_u, peer_v):
    params = {
        'w_ada': w_ada, 'b_ada': b_ada, 'ln_mix_g': ln_mix_g, 'ln_mix_b': ln_mix_b,
        'ln_ffn_g': ln_ffn_g, 'ln_ffn_b': ln_ffn_b,
        'lru_w_in': lru_w_in, 'lru_conv_w': lru_conv_w, 'lru_conv_b': lru_conv_b,
        'lru_w_a': lru_w_a, 'lru_b_a': lru_b_a, 'lru_w_i': lru_w_i, 'lru_b_i': lru_b_i,
        'lru_lambda': lru_lambda, 'lru_w_out': lru_w_out,
        'conf_w_pw1': conf_w_pw1, 'conf_b_pw1': conf_b_pw1, 'conf_dw_w': conf_dw_w, 'conf_dw_b': conf_dw_b,
        'conf_ln_g': conf_ln_g, 'conf_ln_b': conf_ln_b, 'conf_w_pw2': conf_w_pw2, 'conf_b_pw2': conf_b_pw2,
        'peer_w_q': peer_w_q, 'peer_keys': peer_keys, 'peer_u': peer_u, 'peer_v': peer_v,
    }
    bsz = x_prompt.shape[0]
    dt = x_prompt.dtype
    h0_p = jnp.zeros((N_LRU_LAYERS, bsz, D_RNN), dt)
    lconv0_p = jnp.zeros((N_LRU_LAYERS, bsz, LRU_CONV_W - 1, D_RNN), dt)
    cconv0_p = jnp.zeros((N_CONF_LAYERS, bsz, CONF_CONV_W - 1, D_CONF), dt)
    y_prompt, h_p, lconv_p, cconv_p = trunk(x_prompt, c_prompt, h0_p, lconv0_p, cconv0_p, params)
    y_sample, h_s, lconv_s, cconv_s = trunk(x_sample, c_sample, state_lru_h, state_lru_conv,
                                            state_conf_conv, params)
    return (y_prompt, y_sample, h_p, lconv_p, cconv_p, h_s, lconv_s, cconv_s)
```

```python
import numpy as np
from contextlib import ExitStack
import concourse.bass as bass
import concourse.mybir as mybir
from concourse.bass_utils import run_bass_kernel_spmd

F32 = mybir.dt.float32; BF16 = mybir.dt.bfloat16; I32 = mybir.dt.int32; U32 = mybir.dt.uint32
ALU = mybir.AluOpType; AF = mybir.ActivationFunctionType; AX = mybir.AxisListType

NCORES = 8
D = 1024; KD = 8; DR = 1280; KR = 10
NTILES = 17; NTOK = 2112
ALPHA = 4.0 ** 0.25
EPS = 1e-5
NB = 4
STAGE = 99
NCH = 99


class Res:
    __slots__ = ("name", "lw", "rd", "excl")

    def __init__(self, name, excl=False):
        self.name = name; self.lw = None; self.rd = {}; self.excl = excl


class Tl:
    def __init__(self, t, name):
        self.t = t; self.r = Res(name)

    def __getitem__(self, k):
        return self.t[k]


class PQ:
    def __init__(self, ap, res):
        self.ap = ap; self.r = res


class FW:
    def __init__(self, nc, es):
        self.nc = nc; self.es = es
        self.q = {"pe": nc.tensor, "act": nc.scalar, "dve": nc.vector, "pool": nc.gpsimd, "sp": nc.sync}
        self.sem = {}; self.cnt = {}; self.waited = {k: {} for k in self.q}
        self.nsem = 0; self.dsems = []
        for k in self.q:
            self._newsem(k)
        self.out_evs = []

    def _alloc(self, name):
        self.nsem += 1
        return self.es.enter_context(self.nc.semaphore(f"{name}_{self.nsem}"))

    def _newsem(self, k):
        self.sem[k] = self._alloc("e" + k); self.cnt[k] = 0

    def dsem(self, name):
        d = [self._alloc("d" + name), 0, None]
        self.dsems.append(d)
        return d

    def _wait(self, eng, ev):
        if ev is None:
            return
        s, v = ev
        w = self.waited[eng]
        if w.get(id(s), -1) >= v:
            return
        self.q[eng].wait_ge(s, v); w[id(s)] = v

    def emit(self, eng, fn, reads=(), writes=(), dsem=None, is_out=False, serial=True):
        skip = None
        if eng == "pe":
            skip = id(self.sem["pe"])
        if dsem is not None and not serial:
            skip = id(dsem[0])

        def w8(ev):
            if ev is not None and id(ev[0]) != skip:
                self._wait(eng, ev)
        for r in reads:
            w8(r.lw)
        for w_ in writes:
            w8(w_.lw)
            for ev in w_.rd.values():
                w8(ev)
        if dsem is not None and serial:
            self._wait(eng, dsem[2])
        ins = fn(self.q[eng])
        if dsem is not None:
            if dsem[1] >= 30000:
                dsem[0] = self._alloc("dx"); dsem[1] = 0
            dsem[1] += 16
            ins.then_inc(dsem[0], 16)
            ev = (dsem[0], dsem[1]); dsem[2] = ev
            if is_out:
                self.out_evs.append(ev)
        else:
            if self.cnt[eng] >= 30000:
                self._newsem(eng)
            self.cnt[eng] += 1
            ins.then_inc(self.sem[eng], 1)
            ev = (self.sem[eng], self.cnt[eng])
        for r in reads:
            r.rd[id(ev[0])] = ev
        for w_ in writes:
            w_.lw = ev; w_.rd = {}
        return ev

    def barrier(self):
        evs = [(self.sem[k], self.cnt[k]) for k in self.q if self.cnt[k] > 0]
        evs += [d[2] for d in self.dsems if d[2] is not None]
        for k in self.q:
            for ev in evs:
                self._wait(k, ev)

    def finish(self):
        self.barrier()


class TD:
    def __init__(self, idx):
        self.idx = idx
        if idx < 16:
            self.nt = 128; self.nseq = 1; self.L = 128; self.tok0 = idx * 128; self.s0 = 0
        else:
            self.nt = 64; self.nseq = 16; self.L = 4; self.tok0 = 2048; self.s0 = 1
        self.prompt = idx < 16


class Kern:
    def __init__(self, nc, fw, es, dr):
        self.nc = nc; self.fw = fw; self.es = es; self.dr = dr
        self.pqi = 0

    def sb(self, es, name, shape, dt):
        self.nsb = getattr(self, "nsb", 0) + 1
        name = f"{name}_{self.nsb}"
        return Tl(es.enter_context(self.nc.sbuf_tensor(name, shape, dt)), name)

    def E(self, eng, fn, rd=(), wr=(), **kw):
        rs = [x if isinstance(x, Res) else x.r for x in rd]
        ws = [x if isinstance(x, Res) else x.r for x in wr]
        ws = ws + [r for r in rs if r.excl]
        rs = [r for r in rs if not r.excl]
        return self.fw.emit(eng, fn, reads=rs, writes=ws, **kw)

    def pq(self):
        q = self.pqs[self.pqi % 32]; self.pqi += 1
        return q

    def alt(self):
        self.alti = getattr(self, "alti", 0) + 1
        return "act" if self.alti % 2 else "dve"

    def evac(self, eng, out_ap, in_ap, rd, wr):
        if eng == "act":
            self.E("act", lambda e: e.copy(out_ap, in_ap), rd, wr)
        else:
            self.E(eng, lambda e: e.tensor_copy(out_ap, in_ap), rd, wr)

    def transpose_to(self, in_ap, npart, nfree, rd):
        q = self.pq()
        o = q.ap[0:nfree, 0:npart]
        self.E("pe", lambda e: e.transpose(o, in_ap, self.ident[0:npart, 0:npart]), list(rd) + [self.ident], [q])
        return q, o

    def load_fm(self, dram2d, C, col):
        st = self.vstg[self.vstg_i % 2]; d = self.vstg_d[self.vstg_i % 2]; self.vstg_i += 1
        self.E("sp", lambda e: e.dma_start(out=st[0:C, :], in_=dram2d), [], [st], dsem=d)
        q, o = self.transpose_to(st[0:C, :], C, 128, [st])
        self.evac(self.alt(), self.cv[:, col:col + C], o, [q], [self.cv])

    def load_w(self, dst, dram, KC, N, dsem):
        src = dram.rearrange("(k p) n -> p k n", p=128)
        for k in range(KC):
            for n0 in range(0, N, 1024):
                n1 = min(N, n0 + 1024)
                self.E("pool", lambda e, k=k, n0=n0, n1=n1: e.dma_start(out=dst[:, k, n0:n1], in_=src[:, k, n0:n1]),
                       [], [dst], dsem=dsem, serial=False)

    def modap(self, li, part, k, td):
        return self.modT[:, li, part * 8 + k, td.s0:td.s0 + td.nseq]

    def modulate(self, out_ap, in_ap, td, sc, sh, rd, wr):
        if td.nseq == 1:
            self.E("act", lambda e: e.activation(out_ap, in_ap, AF.Identity,
                                                 bias=(sh if sh is not None else 0.0),
                                                 scale=(sc if sc is not None else 1.0)), rd, wr)
        else:
            shape = [128, td.nseq, td.L]
            o3 = out_ap.rearrange("p (s l) -> p s l", l=td.L)
            i3 = in_ap.rearrange("p (s l) -> p s l", l=td.L)
            if sc is not None and sh is not None:
                t3 = self.mtmp[:, 0:td.nt].rearrange("p (s l) -> p s l", l=td.L)
                self.E("dve", lambda e: e.tensor_tensor(t3, i3, sc.unsqueeze(2).to_broadcast(shape), ALU.mult), rd, [self.mtmp])
                self.E("dve", lambda e: e.tensor_tensor(o3, t3, sh.unsqueeze(2).to_broadcast(shape), ALU.add), [self.mtmp], wr)
            elif sc is not None:
                self.E("dve", lambda e: e.tensor_tensor(o3, i3, sc.unsqueeze(2).to_broadcast(shape), ALU.mult), rd, wr)
            else:
                self.E("dve", lambda e: e.tensor_tensor(o3, i3, sh.unsqueeze(2).to_broadcast(shape), ALU.add), rd, wr)

    def xr(self, td, k):
        return self.xres[:, k, td.tok0:td.tok0 + td.nt]

    def resid_ln(self, td, li, gpart, srcs, lng_col, lnb_col):
        nt = td.nt; Tb = self.Tb; Sq = self.Sq
        q1 = self.pq(); q2 = self.pq()
        for j in range(8):
            ap, res = srcs[j]
            self.modulate(Tb[:, j, 0:nt], ap, td, self.modap(li, gpart, j, td), None, [res, self.modT], [Tb])
            self.E("dve", lambda e, j=j: e.scalar_tensor_tensor(out=Tb[:, j, 0:nt], in0=self.xr(td, j), scalar=ALPHA,
                                                                in1=Tb[:, j, 0:nt], op0=ALU.mult, op1=ALU.add),
                   [self.xrr[td.idx], Tb], [Tb])
            self.E("act", lambda e, j=j: e.activation(Sq[:, j, 0:nt], Tb[:, j, 0:nt], AF.Square), [Tb], [Sq])
        for j in range(8):
            self.E("pe", lambda e, j=j: e.matmul(q1.ap[:, 0:nt], lhsT=self.onesf[:, :], rhs=Tb[:, j, 0:nt], start=(j == 0), stop=(j == 7)),
                   [Tb, self.onesf], [q1])
        for j in range(8):
            self.E("pe", lambda e, j=j: e.matmul(q2.ap[:, 0:nt], lhsT=self.onesf[:, :], rhs=Sq[:, j, 0:nt], start=(j == 0), stop=(j == 7)),
                   [Sq, self.onesf], [q2])
        self.ln_finish(td, q1, q2, Tb, lng_col, lnb_col, lambda j: self.xr(td, j), [self.xrr[td.idx]], AF.Identity)

    def ln_finish(self, td, q1, q2, Tb, lng_col, lnb_col, outfn, outres, func):
        nt = td.nt; st = self.lnst
        self.evac("act", st[:, 0, 0:nt], q1.ap[:, 0:nt], [q1], [st])
        self.E("dve", lambda e: e.tensor_tensor(st[:, 2, 0:nt], st[:, 0, 0:nt], st[:, 0, 0:nt], ALU.mult), [st], [st])
        self.E("dve", lambda e: e.tensor_tensor(st[:, 1, 0:nt], q2.ap[:, 0:nt], st[:, 2, 0:nt], ALU.subtract), [q2, st], [st])
        self.E("act", lambda e: e.activation(st[:, 1, 0:nt], st[:, 1, 0:nt], AF.Sqrt, bias=self.epsb[:, 0:1], scale=1.0), [st, self.epsb], [st])
        self.E("dve", lambda e: e.reciprocal(st[:, 1, 0:nt], st[:, 1, 0:nt]), [st], [st])
        for j in range(8):
            self.E("dve", lambda e, j=j: e.tensor_tensor(Tb[:, j, 0:nt], Tb[:, j, 0:nt], st[:, 0, 0:nt], ALU.subtract), [Tb, st], [Tb])
            self.E("dve", lambda e, j=j: e.tensor_tensor(Tb[:, j, 0:nt], Tb[:, j, 0:nt], st[:, 1, 0:nt], ALU.mult), [Tb, st], [Tb])
            self.E("act", lambda e, j=j: e.activation(outfn(j), Tb[:, j, 0:nt], func, bias=self.cv[:, lnb_col + j:lnb_col + j + 1],
                                                      scale=self.cv[:, lng_col + j:lng_col + j + 1]), [Tb, self.cv], outres)

    def state_out(self, stg, ncols, grp, dram_rows, rd):
        for g in range(ncols // grp):
            q, o = self.transpose_to(stg[:, g * grp:(g + 1) * grp], 128, grp, rd)
            ost = self.ost[g % 2]
            self.evac(self.alt(), ost[0:grp, :], o, [q], [ost])
            self.E("sp", lambda e, g=g, ost=ost: e.dma_start(out=dram_rows[g * grp:(g + 1) * grp, :], in_=ost[0:grp, :]),
                   [ost], [], dsem=self.dout[g % 2], is_out=True)

    def run(self):
        nc = self.nc; es = self.es; dr = self.dr; E = self.E
        self.pbanks = [es.enter_context(nc.psum_tensor(f"pb{b}", [128, 512], F32)) for b in range(8)]
        self.bres = [Res(f"bank{b}", excl=True) for b in range(8)]
        self.pqs = [PQ(self.pbanks[i % 8][:, (i // 8) * 128:(i // 8 + 1) * 128], self.bres[i % 8]) for i in range(32)]
        self.xres = self.sb(es, "xres", [128, 8, NTOK], F32)
        self.xrr = [Res(f"xr{t}") for t in range(NTILES)]
        self.ident = self.sb(es, "ident", [128, 128], F32)
        self.onesf = self.sb(es, "onesf", [128, 128], F32)
        self.iota16 = self.sb(es, "iota16", [128, 16], F32)
        self.epsb = self.sb(es, "epsb", [128, 1], F32)
        self.cv = self.sb(es, "cv", [128, 1024], F32)
        self.modT = self.sb(es, "modT", [128, 2, 48, 18], F32)
        self.mtmp = self.sb(es, "mtmp", [128, 128], F32)
        self.Tb = self.sb(es, "Tb", [128, 8, 128], F32)
        self.Sq = self.sb(es, "Sq", [128, 8, 128], F32)
        self.lnst = self.sb(es, "lnst", [128, 3, 128], F32)
        self.ost = [self.sb(es, f"ost{i}", [128, 128], F32) for i in range(2)]
        self.vstg = [self.sb(es, f"vstg{i}", [128, 128], F32) for i in range(2)]
        self.vstg_d = [self.fw.dsem(f"vs{i}") for i in range(2)]; self.vstg_i = 0
        self.dout = [self.fw.dsem(f"out{i}") for i in range(2)]
        self.tds = [TD(i) for i in range(NTILES)]
        wk = self.mtmp
        E("pool", lambda e: e.iota(wk[:], pattern=[[1, 128]], base=0, channel_multiplier=-1, allow_small_or_imprecise_dtypes=True), [], [wk])
        E("dve", lambda e: e.tensor_single_scalar(self.ident[:], wk[:], 0.0, ALU.is_equal), [wk], [self.ident])
        E("dve", lambda e: e.memset(self.onesf[:], 1.0 / 1024.0), [], [self.onesf])
        E("dve", lambda e: e.memset(self.epsb[:], EPS), [], [self.epsb])
        E("pool", lambda e: e.iota(self.iota16[:], pattern=[[1, 16]], base=0, channel_multiplier=0, allow_small_or_imprecise_dtypes=True), [], [self.iota16])

        if STAGE < -3: return
        col = {}
        cur = [0]

        def vec(name, dram2d, C):
            col[name] = cur[0]
            for r0 in range(0, C, 128):
                r1 = min(C, r0 + 128)
                self.load_fm(dram2d[r0:r1, :], r1 - r0, cur[0] + r0)
            cur[0] += C
        v2 = lambda a: a.rearrange("(c p) -> c p", p=128)
        for li in range(2):
            vec(f"bada{li}", v2(dr["b_ada"][li]), 48)
            vec(f"lnmg{li}", v2(dr["ln_mix_g"][li]), 8); vec(f"lnmb{li}", v2(dr["ln_mix_b"][li]), 8)
            vec(f"lnfg{li}", v2(dr["ln_ffn_g"][li]), 8); vec(f"lnfb{li}", v2(dr["ln_ffn_b"][li]), 8)
        vec("lcw", dr["lru_conv_w"][0].rearrange("k (c p) -> (k c) p", p=128), 40)
        vec("lcb", v2(dr["lru_conv_b"][0]), 10); vec("lba", v2(dr["lru_b_a"][0]), 10)
        vec("lbi", v2(dr["lru_b_i"][0]), 10); vec("lam", v2(dr["lru_lambda"][0]), 10)
        vec("cb1", v2(dr["conf_b_pw1"][0]), 16)
        vec("cdw", dr["conf_dw_w"][0].rearrange("k (c p) -> (k c) p", p=128), 248)
        vec("cdb", v2(dr["conf_dw_b"][0]), 8); vec("clg", v2(dr["conf_ln_g"][0]), 8)
        vec("clb", v2(dr["conf_ln_b"][0]), 8); vec("cb2", v2(dr["conf_b_pw2"][0]), 8)
        self.col = col
        cv = self.cv
        for li in range(2):
            for part in (1, 4):
                c0 = col[f"bada{li}"] + part * 8
                E("dve", lambda e, c0=c0: e.tensor_scalar_add(cv[:, c0:c0 + 8], cv[:, c0:c0 + 8], 1.0), [cv], [cv])
        cl = col["lam"]
        E("act", lambda e: e.activation(cv[:, cl:cl + 10], cv[:, cl:cl + 10], AF.Exp, scale=-1.0), [cv], [cv])
        E("act", lambda e: e.activation(cv[:, cl:cl + 10], cv[:, cl:cl + 10], AF.Ln, bias=1.0, scale=1.0), [cv], [cv])
        E("dve", lambda e: e.tensor_scalar_mul(cv[:, cl:cl + 10], cv[:, cl:cl + 10], -8.0), [cv], [cv])

        if STAGE < -2: return
        with ExitStack() as pes:
            csb = self.sb(pes, "csb", [18, 1024], F32)
            E("dve", lambda e: e.memset(csb[:, :], 0.0), [], [csb])
            cT = self.sb(pes, "cT", [128, 8, 18], BF16)
            wab = [self.sb(pes, f"wab{i}", [128, 8, 512], BF16) for i in range(2)]
            dwa = [self.fw.dsem(f"wa{i}") for i in range(2)]
            dc = self.fw.dsem("c")
            E("sp", lambda e: e.dma_start(out=csb[0:17, :], in_=dr["cc"]), [], [csb], dsem=dc)
            E("act", lambda e: e.activation(csb[:, :], csb[:, :], AF.Silu), [csb], [csb])
            for k in range(8):
                q, o = self.transpose_to(csb[0:18, k * 128:(k + 1) * 128], 18, 128, [csb])
                self.evac(self.alt(), cT[:, k, :], o[:, 0:18], [q], [cT])
            it = 0
            for li in range(2):
                if STAGE < -1.7: break
                for n0 in range(0, 6144, 512):
                    if it >= NCH: break
                    if STAGE < -1.3 and it >= 1: break
                    if STAGE < -1.15 and it >= 2: break
                    if STAGE < -1.05 and it >= 3: break
                    wb = wab[it % 2]; dd = dwa[it % 2]; it += 1
                    src = dr["w_ada"][li][:, n0:n0 + 512].rearrange("(k p) n -> p k n", p=128)
                    for k in range(8):
                        E("pool", lambda e, wb=wb, src=src, k=k: e.dma_start(out=wb[:, k, :], in_=src[:, k, :]), [], [wb], dsem=dd, serial=(k == 0))
                    for s in range(4):
                        if STAGE < -1.5: break
                        mc = n0 // 128 + s
                        q = self.pq()
                        for k in range(8):
                            E("pe", lambda e, k=k, s=s, wb=wb, q=q: e.matmul(q.ap[:, 0:18], lhsT=wb[:, k, s * 128:(s + 1) * 128], rhs=cT[:, k, :],
                                                                          start=(k == 0), stop=(k == 7)), [wb, cT], [q])
                        bc = col[f"bada{li}"] + mc
                        E("act", lambda e, q=q, li=li, mc=mc, bc=bc: e.activation(self.modT[:, li, mc, :], q.ap[:, 0:18], AF.Identity,
                                                                               bias=cv[:, bc:bc + 1], scale=1.0), [q, cv], [self.modT])
            self.fw.barrier()

        if STAGE < -1: return
        with ExitStack() as pes:
            xin = [self.sb(pes, f"xin{i}", [128, 1024], F32) for i in range(2)]
            dx = [self.fw.dsem(f"x{i}") for i in range(2)]
            for td in self.tds:
                xi = xin[td.idx % 2]
                src = dr["xp"][td.tok0:td.tok0 + 128, :] if td.prompt else dr["xs"]
                E("sp", lambda e, xi=xi, src=src, td=td: e.dma_start(out=xi[0:td.nt, :], in_=src), [], [xi], dsem=dx[td.idx % 2])
                for k in range(8):
                    q, o = self.transpose_to(xi[0:td.nt, k * 128:(k + 1) * 128], td.nt, 128, [xi])
                    self.evac(self.alt(), self.xr(td, k), o, [q], [self.xrr[td.idx]])
            self.fw.barrier()

        if STAGE < 1: return
        self.phase_m0()
        if STAGE < 2: return
        self.phase_peer(0)
        if STAGE < 3: return
        self.phase_m1()
        if STAGE < 4: return
        self.phase_peer(1)

    def phase_m0(self):
        E = self.E; dr = self.dr; col = self.col; cv = self.cv
        with ExitStack() as pes:
            w_in = self.sb(pes, "w_in", [128, 8, 2560], BF16)
            w_a = self.sb(pes, "w_a", [128, 10, 128], BF16)
            w_i = self.sb(pes, "w_i", [128, 10, 128], BF16)
            w_out = self.sb(pes, "w_out", [128, 10, 1024], BF16)
            dw = self.fw.dsem("wm0")
            self.load_w(w_in, dr["lru_w_in"][0], 8, 2560, dw)
            E("pool", lambda e: e.dma_start(out=w_a[:, :, :], in_=dr["lru_w_a"][0].rearrange("h i j -> i h j")), [], [w_a], dsem=dw, serial=False)
            E("pool", lambda e: e.dma_start(out=w_i[:, :, :], in_=dr["lru_w_i"][0].rearrange("h i j -> i h j")), [], [w_i], dsem=dw, serial=False)
            self.load_w(w_out, dr["lru_w_out"][0], 10, 1024, dw)
            hmT = self.sb(pes, "hmT", [128, 8, 128], BF16)
            xp_p = self.sb(pes, "xp_p", [128, 10, 1, 131], F32)
            xp_s = self.sb(pes, "xp_s", [128, 10, 16, 7], F32)
            hst_p = self.sb(pes, "hst_p", [128, 10], F32)
            hst_s = self.sb(pes, "hst_s", [128, 10, 16], F32)
            gg = self.sb(pes, "gg", [128, 10, 128], F32)
            yg = self.sb(pes, "yg", [128, 10, 128], BF16)
            nbuf = 2
            xc = [self.sb(pes, f"xc{i}", [128, 128], F32) for i in range(nbuf)]
            xcb = [self.sb(pes, f"xcb{i}", [128, 128], BF16) for i in range(nbuf)]
            rr = [self.sb(pes, f"rr{i}", [128, 128], F32) for i in range(nbuf)]
            ig = [self.sb(pes, f"ig{i}", [128, 128], F32) for i in range(nbuf)]
            aa = [self.sb(pes, f"aa{i}", [128, 128], F32) for i in range(nbuf)]
            a2 = [self.sb(pes, f"a2{i}", [128, 128], F32) for i in range(nbuf)]
            hh = [self.sb(pes, f"hh{i}", [128, 128], F32) for i in range(nbuf)]
            stg = self.sb(pes, "stg0", [128, 1280], F32)
            dst = self.fw.dsem("st0")
            E("dve", lambda e: e.memset(xp_p[:, :, :, :], 0.0), [], [xp_p])
            E("dve", lambda e: e.memset(hst_p[:, :], 0.0), [], [hst_p])
            E("sp", lambda e: e.dma_start(out=stg[0:16, :], in_=dr["slh"]), [], [stg], dsem=dst)
            for c in range(10):
                q, o = self.transpose_to(stg[0:16, c * 128:(c + 1) * 128], 16, 128, [stg])
                self.evac(self.alt(), hst_s[:, c, :], o, [q], [hst_s])
            E("sp", lambda e: e.dma_start(out=stg[0:48, :], in_=dr["slc"].rearrange("s k d -> (s k) d")), [], [stg], dsem=dst)
            for c in range(10):
                q, o = self.transpose_to(stg[0:48, c * 128:(c + 1) * 128], 48, 128, [stg])
                self.evac(self.alt(), xp_s[:, c, :, 0:3], o.rearrange("p (s k) -> p s k", k=3), [q], [xp_s])

            for td in self.tds:
                nt = td.nt; L = td.L; ns = td.nseq
                xp = xp_p if td.prompt else xp_s
                for k in range(8):
                    self.modulate(hmT[:, k, 0:nt], self.xr(td, k), td, self.modap(0, 1, k, td), self.modap(0, 0, k, td),
                                  [self.xrr[td.idx], self.modT], [hmT])
                for c in range(20):
                    q = self.pq()
                    for k in range(8):
                        E("pe", lambda e, k=k, c=c, q=q: e.matmul(q.ap[:, 0:nt], lhsT=w_in[:, k, c * 128:(c + 1) * 128], rhs=hmT[:, k, 0:nt],
                                                                 start=(k == 0), stop=(k == 7)), [w_in, hmT], [q])
                    if c < 10:
                        o3 = xp[:, c, :, 3:3 + L]
                        self.evac("dve", o3, q.ap[:, 0:nt].rearrange("p (s l) -> p s l", l=L), [q], [xp])
                    else:
                        E("act", lambda e, c=c, q=q: e.activation(gg[:, c - 10, 0:nt], q.ap[:, 0:nt], AF.Gelu), [q], [gg])
                for c in range(10):
                    b = c % nbuf
                    xc3 = xc[b][:, 0:nt].rearrange("p (s l) -> p s l", l=L)
                    lw = col["lcw"]
                    E("dve", lambda e, c=c, xc3=xc3: e.tensor_scalar(xc3, xp[:, c, :, 0:L], cv[:, lw + c:lw + c + 1],
                                                                     cv[:, col["lcb"] + c:col["lcb"] + c + 1], op0=ALU.mult, op1=ALU.add),
                      [xp, cv], [xc[b]])
                    for kk in range(1, 4):
                        E("dve", lambda e, c=c, kk=kk, xc3=xc3: e.scalar_tensor_tensor(out=xc3, in0=xp[:, c, :, kk:kk + L],
                                                                                        scalar=cv[:, lw + kk * 10 + c:lw + kk * 10 + c + 1],
                                                                                        in1=xc3, op0=ALU.mult, op1=ALU.add), [xp, cv, xc[b]], [xc[b]])
                    if td.prompt:
                        E("pool", lambda e, c=c: e.tensor_copy(xp[:, c, :, 0:3], xp[:, c, :, L:L + 3]), [xp], [xp])
                    E("act", lambda e, b=b: e.copy(xcb[b][:, 0:nt], xc[b][:, 0:nt]), [xc[b]], [xcb[b]])
                    qa = self.pq(); qi = self.pq()
                    E("pe", lambda e, c=c, b=b, qa=qa: e.matmul(qa.ap[:, 0:nt], lhsT=w_a[:, c, :], rhs=xcb[b][:, 0:nt], start=True, stop=True), [w_a, xcb[b]], [qa])
                    E("pe", lambda e, c=c, b=b, qi=qi: e.matmul(qi.ap[:, 0:nt], lhsT=w_i[:, c, :], rhs=xcb[b][:, 0:nt], start=True, stop=True), [w_i, xcb[b]], [qi])
                    E("act", lambda e, c=c, b=b, qa=qa: e.activation(rr[b][:, 0:nt], qa.ap[:, 0:nt], AF.Sigmoid,
                                                                     bias=cv[:, col["lba"] + c:col["lba"] + c + 1], scale=1.0), [qa, cv], [rr[b]])
                    E("act", lambda e, c=c, b=b, qi=qi: e.activation(ig[b][:, 0:nt], qi.ap[:, 0:nt], AF.Sigmoid,
                                                                     bias=cv[:, col["lbi"] + c:col["lbi"] + c + 1], scale=1.0), [qi, cv], [ig[b]])
                    E("act", lambda e, c=c, b=b: e.activation(aa[b][:, 0:nt], rr[b][:, 0:nt], AF.Exp,
                                                              scale=cv[:, col["lam"] + c:col["lam"] + c + 1]), [rr[b], cv], [aa[b]])
                    E("dve", lambda e, b=b: e.tensor_tensor(a2[b][:, 0:nt], aa[b][:, 0:nt], aa[b][:, 0:nt], ALU.mult), [aa[b]], [a2[b]])
                    E("act", lambda e, b=b: e.activation(a2[b][:, 0:nt], a2[b][:, 0:nt], AF.Sqrt, bias=self.onesb[:, 0:1], scale=-1.0), [a2[b], self.onesb], [a2[b]])
                    E("dve", lambda e, b=b: e.tensor_tensor(ig[b][:, 0:nt], ig[b][:, 0:nt], xc[b][:, 0:nt], ALU.mult), [ig[b], xc[b]], [ig[b]])
                    E("dve", lambda e, b=b: e.tensor_tensor(ig[b][:, 0:nt], ig[b][:, 0:nt], a2[b][:, 0:nt], ALU.mult), [ig[b], a2[b]], [ig[b]])
                    if td.prompt:
                        E("dve", lambda e, c=c, b=b: e.tensor_tensor_scan(hh[b][:, 0:nt], aa[b][:, 0:nt], ig[b][:, 0:nt], hst_p[:, c:c + 1],
                                                                           op0=ALU.mult, op1=ALU.add), [aa[b], ig[b], hst_p], [hh[b]])
                        E("dve", lambda e, c=c, b=b: e.tensor_copy(hst_p[:, c:c + 1], hh[b][:, nt - 1:nt]), [hh[b]], [hst_p])
                    else:
                        h3 = hh[b][:, 0:nt].rearrange("p (s l) -> p s l", l=L)
                        a3 = aa[b][:, 0:nt].rearrange("p (s l) -> p s l", l=L)
                        b3 = ig[b][:, 0:nt].rearrange("p (s l) -> p s l", l=L)
                        for t in range(L):
                            prev = hst_s[:, c, :] if t == 0 else h3[:, :, t - 1]
                            E("dve", lambda e, t=t, prev=prev, h3=h3, a3=a3: e.tensor_tensor(h3[:, :, t], a3[:, :, t], prev, ALU.mult),
                              [aa[b], hst_s, hh[b]], [hh[b]])
                            E("dve", lambda e, t=t, h3=h3, b3=b3: e.tensor_tensor(h3[:, :, t], h3[:, :, t], b3[:, :, t], ALU.add),
                              [ig[b], hh[b]], [hh[b]])
                        E("dve", lambda e, c=c, h3=h3: e.tensor_copy(hst_s[:, c, :], h3[:, :, L - 1]), [hh[b]], [hst_s])
                    E("dve", lambda e, c=c, b=b: e.tensor_tensor(yg[:, c, 0:nt], hh[b][:, 0:nt], gg[:, c, 0:nt], ALU.mult), [hh[b], gg], [yg])
                srcs = []
                for j in range(8):
                    q = self.pq()
                    for k in range(10):
                        E("pe", lambda e, k=k, j=j, q=q: e.matmul(q.ap[:, 0:nt], lhsT=w_out[:, k, j * 128:(j + 1) * 128], rhs=yg[:, k, 0:nt],
                                                                 start=(k == 0), stop=(k == 9)), [w_out, yg], [q])
                    srcs.append((q.ap[:, 0:nt], q.r))
                self.resid_ln(td, 0, 2, srcs, col["lnmg0"], col["lnmb0"])
                if td.idx == 15:
                    self.state_out(hst_p, 10, 10, self.dr["o_hp"], [hst_p])
                    E("dve", lambda e: e.tensor_copy(stg[:, 0:30].rearrange("p (k c) -> p k c", c=10),
                                                     xp_p[:, :, 0, 0:3].rearrange("p c k -> p k c")), [xp_p], [stg])
                    self.state_out(stg, 30, 30, self.dr["o_lcp"], [stg])
                if td.idx == 16:
                    E("dve", lambda e: e.tensor_copy(stg[:, 0:160].rearrange("p (s c) -> p s c", c=10),
                                                     hst_s[:, :, :].rearrange("p c s -> p s c")), [hst_s], [stg])
                    self.state_out(stg, 160, 80, self.dr["o_hs"], [stg])
                    for s in range(16):
                        E("dve", lambda e, s=s: e.tensor_copy(stg[:, 160 + s * 30:160 + (s + 1) * 30].rearrange("p (k c) -> p k c", c=10),
                                                              xp_s[:, :, s, 4:7].rearrange("p c k -> p k c")), [xp_s], [stg])
                    self.state_out(stg[:, 160:640], 480, 120, self.dr["o_lcs"], [stg])
            self.fw.barrier()

    def phase_m1(self):
        E = self.E; dr = self.dr; col = self.col; cv = self.cv
        with ExitStack() as pes:
            w1 = self.sb(pes, "w_pw1", [128, 8, 2048], BF16)
            w2 = self.sb(pes, "w_pw2", [128, 8, 1024], BF16)
            dw = self.fw.dsem("wm1")
            self.load_w(w1, dr["conf_w_pw1"][0], 8, 2048, dw)
            self.load_w(w2, dr["conf_w_pw2"][0], 8, 1024, dw)
            hmT = self.sb(pes, "hmT1", [128, 8, 128], BF16)
            sg = self.sb(pes, "sg", [128, 8, 128], F32)
            up_p = self.sb(pes, "up_p", [128, 8, 1, 158], F32)
            up_s = self.sb(pes, "up_s", [128, 8, 16, 34], F32)
            dd = self.sb(pes, "dd", [128, 8, 128], F32)
            dsT = self.sb(pes, "dsT", [128, 8, 128], BF16)
            osb = self.sb(pes, "osb", [128, 8, 128], F32)
            stg = self.sb(pes, "stg1", [128, 1024], F32)
            dst = self.fw.dsem("st1")
            E("dve", lambda e: e.memset(up_p[:, :, :, :], 0.0), [], [up_p])
            src = dr["scc"].rearrange("s k d -> (s k) d")
            for g in range(4):
                E("sp", lambda e, g=g: e.dma_start(out=stg[0:120, :], in_=src[g * 120:(g + 1) * 120, :]), [], [stg], dsem=dst)
                for c in range(8):
                    q, o = self.transpose_to(stg[0:120, c * 128:(c + 1) * 128], 120, 128, [stg])
                    self.evac(self.alt(), up_s[:, c, g * 4:(g + 1) * 4, 0:30], o.rearrange("p (s k) -> p s k", k=30), [q], [up_s])
            for td in self.tds:
                nt = td.nt; L = td.L
                up = up_p if td.prompt else up_s
                for k in range(8):
                    self.modulate(hmT[:, k, 0:nt], self.xr(td, k), td, self.modap(1, 1, k, td), self.modap(1, 0, k, td),
                                  [self.xrr[td.idx], self.modT], [hmT])
                for c in list(range(8, 16)) + list(range(8)):
                    q = self.pq()
                    for k in range(8):
                        E("pe", lambda e, k=k, c=c, q=q: e.matmul(q.ap[:, 0:nt], lhsT=w1[:, k, c * 128:(c + 1) * 128], rhs=hmT[:, k, 0:nt],
                                                                 start=(k == 0), stop=(k == 7)), [w1, hmT], [q])
                    bcol = col["cb1"] + c
                    if c >= 8:
                        E("act", lambda e, c=c, q=q, bcol=bcol: e.activation(sg[:, c - 8, 0:nt], q.ap[:, 0:nt], AF.Sigmoid,
                                                                            bias=cv[:, bcol:bcol + 1], scale=1.0), [q, cv], [sg])
                    else:
                        E("dve", lambda e, c=c, q=q, bcol=bcol: e.scalar_tensor_tensor(
                            out=up[:, c, :, 30:30 + L], in0=q.ap[:, 0:nt].rearrange("p (s l) -> p s l", l=L), scalar=cv[:, bcol:bcol + 1],
                            in1=sg[:, c, 0:nt].rearrange("p (s l) -> p s l", l=L), op0=ALU.add, op1=ALU.mult), [q, cv, sg], [up])
                for c in range(8):
                    d3 = dd[:, c, 0:nt].rearrange("p (s l) -> p s l", l=L)
                    cw = col["cdw"]
                    E("dve", lambda e, c=c, d3=d3: e.tensor_scalar(d3, up[:, c, :, 0:L], cv[:, cw + c:cw + c + 1],
                                                                   cv[:, col["cdb"] + c:col["cdb"] + c + 1], op0=ALU.mult, op1=ALU.add), [up, cv], [dd])
                    for kk in range(1, 31):
                        E("dve", lambda e, c=c, kk=kk, d3=d3: e.scalar_tensor_tensor(out=d3, in0=up[:, c, :, kk:kk + L],
                                                                                      scalar=cv[:, cw + kk * 8 + c:cw + kk * 8 + c + 1],
                                                                                      in1=d3, op0=ALU.mult, op1=ALU.add), [up, cv, dd], [dd])
                    if td.prompt:
                        E("pool", lambda e, c=c: e.tensor_copy(up[:, c, :, 0:30], up[:, c, :, L:L + 30]), [up], [up])
                Sq = self.Sq
                q1 = self.pq(); q2 = self.pq()
                for j in range(8):
                    E("act", lambda e, j=j: e.activation(Sq[:, j, 0:nt], dd[:, j, 0:nt], AF.Square), [dd], [Sq])
                for j in range(8):
                    E("pe", lambda e, j=j: e.matmul(q1.ap[:, 0:nt], lhsT=self.onesf[:, :], rhs=dd[:, j, 0:nt], start=(j == 0), stop=(j == 7)), [dd, self.onesf], [q1])
                for j in range(8):
                    E("pe", lambda e, j=j: e.matmul(q2.ap[:, 0:nt], lhsT=self.onesf[:, :], rhs=Sq[:, j, 0:nt], start=(j == 0), stop=(j == 7)), [Sq, self.onesf], [q2])
                self.ln_finish(td, q1, q2, dd, col["clg"], col["clb"], lambda j: dsT[:, j, 0:nt], [dsT], AF.Silu)
                srcs = []
                for j in range(8):
                    q = self.pq()
                    for k in range(8):
                        E("pe", lambda e, k=k, j=j, q=q: e.matmul(q.ap[:, 0:nt], lhsT=w2[:, k, j * 128:(j + 1) * 128], rhs=dsT[:, k, 0:nt],
                                                                 start=(k == 0), stop=(k == 7)), [w2, dsT], [q])
                    bcol = col["cb2"] + j
                    E("act", lambda e, j=j, q=q, bcol=bcol: e.activation(osb[:, j, 0:nt], q.ap[:, 0:nt], AF.Identity, bias=cv[:, bcol:bcol + 1], scale=1.0),
                      [q, cv], [osb])
                    srcs.append((osb[:, j, 0:nt], osb.r))
                self.resid_ln(td, 1, 2, srcs, col["lnmg1"], col["lnmb1"])
                if td.idx == 15:
                    E("dve", lambda e: e.tensor_copy(stg[:, 0:240].rearrange("p (k c) -> p k c", c=8),
                                                     up_p[:, :, 0, 0:30].rearrange("p c k -> p k c")), [up_p], [stg])
                    self.state_out(stg, 240, 120, self.dr["o_ccp"], [stg])
                if td.idx == 16:
                    for s in range(16):
                        E("dve", lambda e, s=s: e.tensor_copy(stg[:, 0:240].rearrange("p (k c) -> p k c", c=8),
                                                              up_s[:, :, s, 4:34].rearrange("p c k -> p k c")), [up_s], [stg])
                        self.state_out(stg, 240, 120, self.dr["o_ccs"][s * 240:(s + 1) * 240, :], [stg])
            self.fw.barrier()

    def phase_peer(self, li):
        E = self.E; dr = self.dr; col = self.col; cv = self.cv
        last = (li == 1)
        with ExitStack() as pes:
            wq = self.sb(pes, "wq", [128, 8, 2048], BF16)
            keysT = self.sb(pes, "keysT", [128, 16, 128], BF16)
            dw = self.fw.dsem(f"wq{li}")
            self.load_w(wq, dr["peer_w_q"][li], 8, 2048, dw)
            hfT32 = self.sb(pes, "hfT32", [128, 8, 128], F32)
            hfT = self.sb(pes, "hfT", [128, 8, 128], BF16)
            hf = self.sb(pes, "hf", [128, 1024], F32)
            qT = self.sb(pes, "qT", [128, 16, 128], BF16)
            S = self.sb(pes, "S", [128, 8, 2, 128], F32)
            S2 = self.sb(pes, "S2", [128, 128], F32)
            vv = self.sb(pes, "vv", [128, 8, 2, 16], F32)
            iu = self.sb(pes, "iu", [128, 8, 2, 16], U32)
            iff = self.sb(pes, "iff", [128, 8, 2, 16], F32)
            cand = self.sb(pes, "cand", [128, 4, 16, 16], F32)
            cand2 = self.sb(pes, "cand2", [128, 4, 16, 16], F32)
            ts = self.sb(pes, "ts", [128, 8, 16], F32)
            pu = self.sb(pes, "pu", [128, 8, 16], U32)
            p1u = self.sb(pes, "p1u", [128, 8, 16], U32)
            p1f = self.sb(pes, "p1f", [128, 8, 16], F32)
            p2f = self.sb(pes, "p2f", [128, 8, 16], F32)
            gte = self.sb(pes, "gte", [128, 8, 16], F32)
            ssum = self.sb(pes, "ssum", [128, 8], F32)
            e1 = self.sb(pes, "e1", [128, 128], F32)
            e2 = self.sb(pes, "e2", [128, 128], F32)
            idx = self.sb(pes, "idx", [128, 128], I32)
            dots = self.sb(pes, "dots", [128, 128], F32)
            wgt = self.sb(pes, "wgt", [128, 128], F32)
            gb = [self.sb(pes, f"gb{i}", [128, 1024], F32) for i in range(NB)]
            dg = [self.fw.dsem(f"g{li}_{i}") for i in range(NB)]
            junk = self.sb(pes, "junk", [128, 1024], F32)
            acc = self.sb(pes, "acc", [128, 1024], F32)
            yst = self.sb(pes, "yst", [128, 1024], F32)
            dy = self.fw.dsem(f"y{li}")
            kst = self.sb(pes, "kst", [128, 128], F32)
            dk = self.fw.dsem(f"k{li}")
            E("pool", lambda e: e.memset(idx[:, :], 0), [], [idx])
            E("dve", lambda e: e.memset(hf[:, :], 0.0), [], [hf])
            ksrc = dr["peer_keys"][li].rearrange("h p n d -> (h p n) d")
            for c in range(16):
                E("sp", lambda e, c=c: e.dma_start(out=kst[:, :], in_=ksrc[c * 128:(c + 1) * 128, :]), [], [kst], dsem=dk)
                q, o = self.transpose_to(kst[:, :], 128, 128, [kst])
                self.evac(self.alt(), keysT[:, c, :], o, [q], [keysT])
            utab = dr["peer_u"].rearrange("l e d -> (l e) d"); vtab = dr["peer_v"].rearrange("l e d -> (l e) d")
            for td in self.tds:
                nt = td.nt
                for k in range(8):
                    self.modulate(hfT32[:, k, 0:nt], self.xr(td, k), td, self.modap(li, 4, k, td), self.modap(li, 3, k, td),
                                  [self.xrr[td.idx], self.modT], [hfT32])
                    E("act", lambda e, k=k: e.copy(hfT[:, k, 0:nt], hfT32[:, k, 0:nt]), [hfT32], [hfT])
                    q, o = self.transpose_to(hfT32[:, k, 0:nt], 128, nt, [hfT32])
                    self.evac("dve", hf[0:nt, k * 128:(k + 1) * 128], o, [q], [hf])
                for c in range(16):
                    q = self.pq()
                    for k in range(8):
                        E("pe", lambda e, k=k, c=c, q=q: e.matmul(q.ap[:, 0:nt], lhsT=wq[:, k, c * 128:(c + 1) * 128], rhs=hfT[:, k, 0:nt],
                                                                 start=(k == 0), stop=(k == 7)), [wq, hfT], [q])
                    self.evac("act", qT[:, c, 0:nt], q.ap[:, 0:nt], [q], [qT])
                for c in range(16):
                    q = self.pq()
                    E("pe", lambda e, c=c, q=q: e.matmul(q.ap[0:nt, :], lhsT=qT[:, c, 0:nt], rhs=keysT[:, c, :], start=True, stop=True), [qT, keysT], [q])
                    self.evac("act", S[0:nt, c // 2, c % 2, :], q.ap[0:nt, :], [q], [S])
                for c in range(16):
                    h, p = c // 2, c % 2
                    sc_ = S[0:nt, h, p, :]
                    E("dve", lambda e, h=h, p=p, sc_=sc_: e.max(out=vv[0:nt, h, p, 0:8], in_=sc_), [S], [vv])
                    E("dve", lambda e, h=h, p=p, sc_=sc_: e.max_index(iu[0:nt, h, p, 0:8], vv[0:nt, h, p, 0:8], sc_), [S, vv], [iu])
                    E("dve", lambda e, h=h, p=p, sc_=sc_: e.match_replace(out=S2[0:nt, :], in_to_replace=vv[0:nt, h, p, 0:8], in_values=sc_, imm_value=-1e30), [S, vv], [S2])
                    E("dve", lambda e, h=h, p=p: e.max(out=vv[0:nt, h, p, 8:16], in_=S2[0:nt, :]), [S2], [vv])
                    E("dve", lambda e, h=h, p=p: e.max_index(iu[0:nt, h, p, 8:16], vv[0:nt, h, p, 8:16], S2[0:nt, :]), [S2, vv], [iu])
                E("dve", lambda e: e.tensor_copy(iff[0:nt], iu[0:nt]), [iu], [iff])
                for hh_ in range(2):
                    h0 = hh_ * 4
                    E("dve", lambda e, h0=h0: e.tensor_tensor(cand[0:nt], vv[0:nt, h0:h0 + 4, 0, :].unsqueeze(3).to_broadcast([nt, 4, 16, 16]),
                                                              vv[0:nt, h0:h0 + 4, 1, :].unsqueeze(2).to_broadcast([nt, 4, 16, 16]), ALU.add), [vv], [cand])
                    for hl in range(4):
                        h = h0 + hl
                        cf = cand[0:nt, hl].rearrange("p a b -> p (a b)")
                        c2 = cand2[0:nt, hl].rearrange("p a b -> p (a b)")
                        E("dve", lambda e, h=h, cf=cf: e.max(out=ts[0:nt, h, 0:8], in_=cf), [cand], [ts])
                        E("dve", lambda e, h=h, cf=cf: e.max_index(pu[0:nt, h, 0:8], ts[0:nt, h, 0:8], cf), [cand, ts], [pu])
                        E("dve", lambda e, h=h, cf=cf, c2=c2: e.match_replace(out=c2, in_to_replace=ts[0:nt, h, 0:8], in_values=cf, imm_value=-1e30), [cand, ts], [cand2])
                        E("dve", lambda e, h=h, c2=c2: e.max(out=ts[0:nt, h, 8:16], in_=c2), [cand2], [ts])
                        E("dve", lambda e, h=h, c2=c2: e.max_index(pu[0:nt, h, 8:16], ts[0:nt, h, 8:16], c2), [cand2, ts], [pu])
                E("dve", lambda e: e.tensor_tensor(gte[0:nt], ts[0:nt], ts[0:nt, :, 0:1].to_broadcast([nt, 8, 16]), ALU.subtract), [ts], [gte])
                E("act", lambda e: e.activation(gte[0:nt], gte[0:nt], AF.Exp), [gte], [gte])
                E("dve", lambda e: e.tensor_reduce(ssum[0:nt], gte[0:nt], axis=AX.X, op=ALU.add), [gte], [ssum])
                E("dve", lambda e: e.reciprocal(ssum[0:nt], ssum[0:nt]), [ssum], [ssum])
                E("dve", lambda e: e.tensor_tensor(gte[0:nt], gte[0:nt], ssum[0:nt].unsqueeze(2).to_broadcast([nt, 8, 16]), ALU.mult), [gte, ssum], [gte])
                E("dve", lambda e: e.tensor_single_scalar(p1u[0:nt], pu[0:nt], 4, ALU.logical_shift_right), [pu], [p1u])
                E("dve", lambda e: e.tensor_copy(p1f[0:nt], p1u[0:nt]), [p1u], [p1f])
                E("dve", lambda e: e.tensor_single_scalar(p1u[0:nt], pu[0:nt], 15, ALU.bitwise_and), [pu, p1f], [p1u])
                E("dve", lambda e: e.tensor_copy(p2f[0:nt], p1u[0:nt]), [p1u], [p2f])
                for hh_ in range(2):
                    h0 = hh_ * 4
                    for (pf, half, eo) in ((p1f, 0, e1), (p2f, 1, e2)):
                        oh = cand[0:nt].rearrange("p a b c -> p (a b) c")
                        E("dve", lambda e, pf=pf, h0=h0, oh=oh: e.tensor_tensor(
                            oh, pf[0:nt, h0:h0 + 4, :].rearrange("p a b -> p (a b)").unsqueeze(2).to_broadcast([nt, 64, 16]),
                            self.iota16[0:nt, :].unsqueeze(1).to_broadcast([nt, 64, 16]), ALU.is_equal), [pf, self.iota16], [cand])
                        E("dve", lambda e, half=half, h0=h0: e.tensor_tensor(
                            cand[0:nt], cand[0:nt], iff[0:nt, h0:h0 + 4, half, :].unsqueeze(2).to_broadcast([nt, 4, 16, 16]), ALU.mult), [cand, iff], [cand])
                        E("dve", lambda e, eo=eo, h0=h0, oh=oh: e.tensor_reduce(eo[0:nt, h0 * 16:(h0 + 4) * 16], oh, axis=AX.X, op=ALU.add), [cand], [eo])
                E("dve", lambda e: e.scalar_tensor_tensor(out=e1[0:nt, :], in0=e1[0:nt, :], scalar=128.0, in1=e2[0:nt, :], op0=ALU.mult, op1=ALU.add), [e1, e2], [e1])
                if li > 0:
                    E("dve", lambda e: e.tensor_scalar_add(e1[0:nt, :], e1[0:nt, :], float(li * 16384)), [e1], [e1])
                E("dve", lambda e: e.tensor_copy(idx[0:nt, :], e1[0:nt, :]), [e1], [idx])
                for j in range(128):
                    s = j % NB
                    E("pool", lambda e, j=j, s=s: e.indirect_dma_start(out=gb[s][:, :], out_offset=None, in_=utab,
                                                                        in_offset=bass.IndirectOffsetOnAxis(ap=idx[:, j:j + 1], axis=0)),
                      [idx], [gb[s]], dsem=dg[s])
                    E("dve", lambda e, j=j, s=s: e.scalar_tensor_tensor(out=junk[:, :], in0=hf[:, :], scalar=1.0, in1=gb[s][:, :],
                                                                         op0=ALU.mult, op1=ALU.mult, accum_out=dots[:, j:j + 1]),
                      [hf, gb[s]], [junk, dots])
                E("act", lambda e: e.activation(wgt[:, :], dots[:, :], AF.Gelu), [dots], [wgt])
                E("dve", lambda e: e.tensor_tensor(wgt[0:nt, :], wgt[0:nt, :], gte[0:nt].rearrange("p a b -> p (a b)"), ALU.mult), [wgt, gte], [wgt])
                for j in range(128):
                    s = j % NB
                    E("pool", lambda e, j=j, s=s: e.indirect_dma_start(out=gb[s][:, :], out_offset=None, in_=vtab,
                                                                        in_offset=bass.IndirectOffsetOnAxis(ap=idx[:, j:j + 1], axis=0)),
                      [idx], [gb[s]], dsem=dg[s])
                    if j == 0:
                        E("dve", lambda e, s=s: e.tensor_scalar(acc[0:nt, :], gb[s][0:nt, :], wgt[0:nt, 0:1], None, op0=ALU.mult), [gb[s], wgt], [acc])
                    else:
                        E("dve", lambda e, j=j, s=s: e.scalar_tensor_tensor(out=acc[0:nt, :], in0=gb[s][0:nt, :], scalar=wgt[0:nt, j:j + 1], in1=acc[0:nt, :],
                                                                             op0=ALU.mult, op1=ALU.add), [gb[s], wgt, acc], [acc])
                srcs = []
                for k in range(8):
                    q, o = self.transpose_to(acc[0:nt, k * 128:(k + 1) * 128], nt, 128, [acc])
                    srcs.append((o, q.r))
                self.resid_ln(td, li, 5, srcs, col[f"lnfg{li}"], col[f"lnfb{li}"])
                if last:
                    for k in range(8):
                        q, o = self.transpose_to(self.xr(td, k), 128, nt, [self.xrr[td.idx]])
                        self.evac(self.alt(), yst[0:nt, k * 128:(k + 1) * 128], o, [q], [yst])
                    dst_ = dr["o_yp"][td.tok0:td.tok0 + 128, :] if td.prompt else dr["o_ys"]
                    E("sp", lambda e, dst_=dst_: e.dma_start(out=dst_, in_=yst[0:nt, :]), [yst], [], dsem=dy, is_out=True)
            self.fw.barrier()


def build_nc():
    nc = bass.Bass("TRN2", target_bir_lowering=False)
    dr = {}

    def din(name, shape, dt=F32):
        dr[name] = nc.dram_tensor(name, list(shape), dt, kind="ExternalInput").ap()

    def dout(name, shape):
        dr[name] = nc.dram_tensor(name, list(shape), F32, kind="ExternalOutput").ap()
    din("xp", [2048, 1024]); din("xs", [64, 1024]); din("slh", [16, 1280]); din("slc", [16, 3, 1280])
    din("scc", [16, 30, 1024]); din("cc", [17, 1024])
    din("w_ada", [2, 1024, 6144]); din("b_ada", [2, 6144])
    for n in ("ln_mix_g", "ln_mix_b", "ln_ffn_g", "ln_ffn_b"):
        din(n, [2, 1024])
    din("lru_w_in", [1, 1024, 2560]); din("lru_conv_w", [1, 4, 1280]); din("lru_conv_b", [1, 1280])
    din("lru_w_a", [1, 10, 128, 128]); din("lru_b_a", [1, 1280]); din("lru_w_i", [1, 10, 128, 128]); din("lru_b_i", [1, 1280])
    din("lru_lambda", [1, 1280]); din("lru_w_out", [1, 1280, 1024])
    din("conf_w_pw1", [1, 1024, 2048]); din("conf_b_pw1", [1, 2048]); din("conf_dw_w", [1, 31, 1024]); din("conf_dw_b", [1, 1024])
    din("conf_ln_g", [1, 1024]); din("conf_ln_b", [1, 1024]); din("conf_w_pw2", [1, 1024, 1024]); din("conf_b_pw2", [1, 1024])
    din("peer_w_q", [2, 1024, 2048]); din("peer_keys", [2, 8, 2, 128, 128]); din("peer_u", [2, 16384, 1024]); din("peer_v", [2, 16384, 1024])
    dout("o_yp", [2048, 1024]); dout("o_ys", [64, 1024]); dout("o_hp", [10, 128]); dout("o_lcp", [30, 128]); dout("o_ccp", [240, 128])
    dout("o_hs", [160, 128]); dout("o_lcs", [480, 128]); dout("o_ccs", [3840, 128])
    with ExitStack() as es:
        fw = FW(nc, es)
        k = Kern(nc, fw, es, dr)
        k.onesb = k.sb(es, "onesb", [128, 1], F32)
        k.E("dve", lambda e: e.memset(k.onesb[:, :], 1.0), [], [k.onesb])
        k.run()
        fw.finish()
    return nc


_WNAMES = ["w_ada", "b_ada", "ln_mix_g", "ln_mix_b", "ln_ffn_g", "ln_ffn_b", "lru_w_in", "lru_conv_w", "lru_conv_b", "lru_w_a", "lru_b_a",
           "lru_w_i", "lru_b_i", "lru_lambda", "lru_w_out", "conf_w_pw1", "conf_b_pw1", "conf_dw_w", "conf_dw_b", "conf_ln_g", "conf_ln_b",
           "conf_w_pw2", "conf_b_pw2", "peer_w_q", "peer_keys", "peer_u", "peer_v"]


def kernel(**inp):
    f = lambda a: np.ascontiguousarray(np.asarray(a, dtype=np.float32))
    W = {n: f(inp[n]) for n in _WNAMES}
    xp = f(inp["x_prompt"]); xs = f(inp["x_sample"])
    slh = f(inp["state_lru_h"]); slc = f(inp["state_lru_conv"]); scc = f(inp["state_conf_conv"])
    cp = f(inp["c_prompt"]); cs = f(inp["c_sample"])
    in_maps = []
    for i in range(NCORES):
        m = dict(W)
        sl = slice(16 * i, 16 * i + 16)
        m["xp"] = xp[i]; m["xs"] = np.ascontiguousarray(xs[sl].reshape(64, 1024))
        m["slh"] = np.ascontiguousarray(slh[0, sl]); m["slc"] = np.ascontiguousarray(slc[0, sl]); m["scc"] = np.ascontiguousarray(scc[0, sl])
        m["cc"] = np.ascontiguousarray(np.concatenate([cp[i:i + 1], cs[sl]], 0))
        in_maps.append(m)
    nc = build_nc()
    res = run_bass_kernel_spmd(nc, in_maps, core_ids=list(range(NCORES)))
    R = res.results
    y_p = np.stack([R[i]["o_yp"] for i in range(NCORES)], 0).astype(np.float32)
    y_s = np.concatenate([R[i]["o_ys"].reshape(16, 4, 1024) for i in range(NCORES)], 0).astype(np.float32)
    h_p = np.stack([R[i]["o_hp"].reshape(1280) for i in range(NCORES)], 0)[None].astype(np.float32)
    lc_p = np.stack([R[i]["o_lcp"].reshape(3, 1280) for i in range(NCORES)], 0)[None].astype(np.float32)
    cc_p = np.stack([R[i]["o_ccp"].reshape(30, 1024) for i in range(NCORES)], 0)[None].astype(np.float32)
    h_s = np.concatenate([R[i]["o_hs"].reshape(16, 1280) for i in range(NCORES)], 0)[None].astype(np.float32)
    lc_s = np.concatenate([R[i]["o_lcs"].reshape(16, 3, 1280) for i in range(NCORES)], 0)[None].astype(np.float32)
    cc_s = np.concatenate([R[i]["o_ccs"].reshape(16, 30, 1024) for i in range(NCORES)], 0)[None].astype(np.float32)
    return (y_p, y_s, h_p, lc_p, cc_p, h_s, lc_s, cc_s)
```

```python
import numpy as np
from contextlib import ExitStack
import concourse.bass as bass
import concourse.mybir as mybir
from concourse.bass_utils import run_bass_kernel_spmd

F32 = mybir.dt.float32; BF16 = mybir.dt.bfloat16; I32 = mybir.dt.int32; U32 = mybir.dt.uint32
ALU = mybir.AluOpType; AF = mybir.ActivationFunctionType; AX = mybir.AxisListType

NCORES = 8
D = 1024; KD = 8; DR = 1280; KR = 10
NTILES = 17; NTOK = 2112
ALPHA = 4.0 ** 0.25
EPS = 1e-5
NB = 4
STAGE = 99
NCH = 99


class Res:
    __slots__ = ("name", "lw", "rd", "excl")

    def __init__(self, name, excl=False):
        self.name = name; self.lw = None; self.rd = {}; self.excl = excl


class Tl:
    def __init__(self, t, name):
        self.t = t; self.r = Res(name)

    def __getitem__(self, k):
        return self.t[k]


class PQ:
    def __init__(self, ap, res):
        self.ap = ap; self.r = res


class FW:
    def __init__(self, nc, es):
        self.nc = nc; self.es = es
        self.q = {"pe": nc.tensor, "act": nc.scalar, "dve": nc.vector, "pool": nc.gpsimd, "sp": nc.sync}
        self.sem = {}; self.cnt = {}; self.waited = {k: {} for k in self.q}
        self.nsem = 0; self.dsems = []
        for k in self.q:
            self._newsem(k)
        self.out_evs = []

    def _alloc(self, name):
        self.nsem += 1
        return self.es.enter_context(self.nc.semaphore(f"{name}_{self.nsem}"))

    def _newsem(self, k):
        self.sem[k] = self._alloc("e" + k); self.cnt[k] = 0

    def dsem(self, name):
        d = [self._alloc("d" + name), 0, None]
        self.dsems.append(d)
        return d

    def _wait(self, eng, ev):
        if ev is None:
            return
        s, v = ev
        w = self.waited[eng]
        if w.get(id(s), -1) >= v:
            return
        self.q[eng].wait_ge(s, v); w[id(s)] = v

    def emit(self, eng, fn, reads=(), writes=(), dsem=None, is_out=False, serial=True):
        skip = None
        if eng == "pe":
            skip = id(self.sem["pe"])
        if dsem is not None and not serial:
            skip = id(dsem[0])

        def w8(ev):
            if ev is not None and id(ev[0]) != skip:
                self._wait(eng, ev)
        for r in reads:
            w8(r.lw)
        for w_ in writes:
            w8(w_.lw)
            for ev in w_.rd.values():
                w8(ev)
        if dsem is not None and serial:
            self._wait(eng, dsem[2])
        ins = fn(self.q[eng])
        if dsem is not None:
            if dsem[1] >= 30000:
                dsem[0] = self._alloc("dx"); dsem[1] = 0
            dsem[1] += 16
            ins.then_inc(dsem[0], 16)
            ev = (dsem[0], dsem[1]); dsem[2] = ev
            if is_out:
                self.out_evs.append(ev)
        else:
            if self.cnt[eng] >= 30000:
                self._newsem(eng)
            self.cnt[eng] += 1
            ins.then_inc(self.sem[eng], 1)
            ev = (self.sem[eng], self.cnt[eng])
        for r in reads:
            r.rd[id(ev[0])] = ev
        for w_ in writes:
            w_.lw = ev; w_.rd = {}
        return ev

    def barrier(self):
        evs = [(self.sem[k], self.cnt[k]) for k in self.q if self.cnt[k] > 0]
        evs += [d[2] for d in self.dsems if d[2] is not None]
        for k in self.q:
            for ev in evs:
                self._wait(k, ev)

    def finish(self):
        self.barrier()


class TD:
    def __init__(self, idx):
        self.idx = idx
        if idx < 16:
            self.nt = 128; self.nseq = 1; self.L = 128; self.tok0 = idx * 128; self.s0 = 0
        else:
            self.nt = 64; self.nseq = 16; self.L = 4; self.tok0 = 2048; self.s0 = 1
        self.prompt = idx < 16


class Kern:
    def __init__(self, nc, fw, es, dr):
        self.nc = nc; self.fw = fw; self.es = es; self.dr = dr
        self.pqi = 0

    def sb(self, es, name, shape, dt):
        self.nsb = getattr(self, "nsb", 0) + 1
        name = f"{name}_{self.nsb}"
        return Tl(es.enter_context(self.nc.sbuf_tensor(name, shape, dt)), name)

    def E(self, eng, fn, rd=(), wr=(), **kw):
        rs = [x if isinstance(x, Res) else x.r for x in rd]
        ws = [x if isinstance(x, Res) else x.r for x in wr]
        ws = ws + [r for r in rs if r.excl]
        rs = [r for r in rs if not r.excl]
        return self.fw.emit(eng, fn, reads=rs, writes=ws, **kw)

    def pq(self):
        q = self.pqs[self.pqi % 24]; self.pqi += 1
        return q

    def alt(self):
        self.alti = getattr(self, "alti", 0) + 1
        return "act" if self.alti % 2 else "dve"

    def evac(self, eng, out_ap, in_ap, rd, wr):
        if eng == "act":
            self.E("act", lambda e: e.copy(out_ap, in_ap), rd, wr)
        else:
            self.E(eng, lambda e: e.tensor_copy(out_ap, in_ap), rd, wr)

    def transpose_to(self, in_ap, npart, nfree, rd):
        q = self.pq()
        o = q.ap[0:nfree, 0:npart]
        self.E("pe", lambda e: e.transpose(o, in_ap, self.ident[0:npart, 0:npart]), list(rd) + [self.ident], [q])
        return q, o

    def load_fm(self, dram2d, C, col):
        st = self.vstg[self.vstg_i % 2]; d = self.vstg_d[self.vstg_i % 2]; self.vstg_i += 1
        self.E("sp", lambda e: e.dma_start(out=st[0:C, :], in_=dram2d), [], [st], dsem=d)
        q, o = self.transpose_to(st[0:C, :], C, 128, [st])
        self.evac(self.alt(), self.cv[:, col:col + C], o, [q], [self.cv])

    def load_w(self, dst, dram, KC, N, dsem):
        src = dram.rearrange("(k p) n -> p k n", p=128)
        for k in range(KC):
            for n0 in range(0, N, 1024):
                n1 = min(N, n0 + 1024)
                self.E("pool", lambda e, k=k, n0=n0, n1=n1: e.dma_start(out=dst[:, k, n0:n1], in_=src[:, k, n0:n1]),
                       [], [dst], dsem=dsem, serial=False)

    def modap(self, li, part, k, td):
        return self.modT[:, li, part * 8 + k, td.s0:td.s0 + td.nseq]

    def modulate(self, out_ap, in_ap, td, sc, sh, rd, wr):
        if td.nseq == 1:
            self.E("act", lambda e: e.activation(out_ap, in_ap, AF.Identity,
                                                 bias=(sh if sh is not None else 0.0),
                                                 scale=(sc if sc is not None else 1.0)), rd, wr)
        else:
            shape = [128, td.nseq, td.L]
            o3 = out_ap.rearrange("p (s l) -> p s l", l=td.L)
            i3 = in_ap.rearrange("p (s l) -> p s l", l=td.L)
            if sc is not None and sh is not None:
                t3 = self.mtmp[:, 0:td.nt].rearrange("p (s l) -> p s l", l=td.L)
                self.E("dve", lambda e: e.tensor_tensor(t3, i3, sc.unsqueeze(2).to_broadcast(shape), ALU.mult), rd, [self.mtmp])
                self.E("dve", lambda e: e.tensor_tensor(o3, t3, sh.unsqueeze(2).to_broadcast(shape), ALU.add), [self.mtmp], wr)
            elif sc is not None:
                self.E("dve", lambda e: e.tensor_tensor(o3, i3, sc.unsqueeze(2).to_broadcast(shape), ALU.mult), rd, wr)
            else:
                self.E("dve", lambda e: e.tensor_tensor(o3, i3, sh.unsqueeze(2).to_broadcast(shape), ALU.add), rd, wr)

    def xr(self, td, k):
        return self.xres[:, k, td.tok0:td.tok0 + td.nt]

    def resid_ln(self, td, li, gpart, srcs, lng_col, lnb_col):
        nt = td.nt; Tb = self.Tb; Sq = self.Sq
        q1 = self.pq(); q2 = self.pq()
        for j in range(8):
            ap, res = srcs[j]
            self.modulate(Tb[:, j, 0:nt], ap, td, self.modap(li, gpart, j, td), None, [res, self.modT], [Tb])
            self.E("dve", lambda e, j=j: e.scalar_tensor_tensor(out=Tb[:, j, 0:nt], in0=self.xr(td, j), scalar=ALPHA,
                                                                in1=Tb[:, j, 0:nt], op0=ALU.mult, op1=ALU.add),
                   [self.xrr[td.idx], Tb], [Tb])
            self.E("act", lambda e, j=j: e.activation(Sq[:, j, 0:nt], Tb[:, j, 0:nt], AF.Square), [Tb], [Sq])
        for j in range(8):
            self.E("pe", lambda e, j=j: e.matmul(q1.ap[:, 0:nt], lhsT=self.onesf[:, :], rhs=Tb[:, j, 0:nt], start=(j == 0), stop=(j == 7)),
                   [Tb, self.onesf], [q1])
        for j in range(8):
            self.E("pe", lambda e, j=j: e.matmul(q2.ap[:, 0:nt], lhsT=self.onesf[:, :], rhs=Sq[:, j, 0:nt], start=(j == 0), stop=(j == 7)),
                   [Sq, self.onesf], [q2])
        self.ln_finish(td, q1, q2, Tb, lng_col, lnb_col, lambda j: self.xr(td, j), [self.xrr[td.idx]], AF.Identity)

    def ln_finish(self, td, q1, q2, Tb, lng_col, lnb_col, outfn, outres, func):
        nt = td.nt; st = self.lnst
        self.evac("act", st[:, 0, 0:nt], q1.ap[:, 0:nt], [q1], [st])
        self.E("dve", lambda e: e.tensor_tensor(st[:, 2, 0:nt], st[:, 0, 0:nt], st[:, 0, 0:nt], ALU.mult), [st], [st])
        self.E("dve", lambda e: e.tensor_tensor(st[:, 1, 0:nt], q2.ap[:, 0:nt], st[:, 2, 0:nt], ALU.subtract), [q2, st], [st])
        self.E("act", lambda e: e.activation(st[:, 1, 0:nt], st[:, 1, 0:nt], AF.Sqrt, bias=self.epsb[:, 0:1], scale=1.0), [st, self.epsb], [st])
        self.E("dve", lambda e: e.reciprocal(st[:, 1, 0:nt], st[:, 1, 0:nt]), [st], [st])
        for j in range(8):
            self.E("dve", lambda e, j=j: e.tensor_tensor(Tb[:, j, 0:nt], Tb[:, j, 0:nt], st[:, 0, 0:nt], ALU.subtract), [Tb, st], [Tb])
            self.E("dve", lambda e, j=j: e.tensor_tensor(Tb[:, j, 0:nt], Tb[:, j, 0:nt], st[:, 1, 0:nt], ALU.mult), [Tb, st], [Tb])
            self.E("act", lambda e, j=j: e.activation(outfn(j), Tb[:, j, 0:nt], func, bias=self.cv[:, lnb_col + j:lnb_col + j + 1],
                                                      scale=self.cv[:, lng_col + j:lng_col + j + 1]), [Tb, self.cv], outres)

    def state_out(self, stg, ncols, grp, dram_rows, rd):
        for g in range(ncols // grp):
            q, o = self.transpose_to(stg[:, g * grp:(g + 1) * grp], 128, grp, rd)
            ost = self.ost[g % 2]
            self.evac(self.alt(), ost[0:grp, :], o, [q], [ost])
            self.E("sp", lambda e, g=g, ost=ost: e.dma_start(out=dram_rows[g * grp:(g + 1) * grp, :], in_=ost[0:grp, :]),
                   [ost], [], dsem=self.dout[g % 2], is_out=True)

    def run(self):
        nc = self.nc; es = self.es; dr = self.dr; E = self.E
        self.pbanks = [es.enter_context(nc.psum_tensor(f"pb{b}", [128, 512], F32)) for b in range(8)]
        self.bres = [Res(f"bank{b}", excl=True) for b in range(8)]
        self.pqs = [PQ(self.pbanks[i % 6][:, (i // 6) * 128:(i // 6 + 1) * 128], self.bres[i % 6]) for i in range(24)]
        self.accq = [PQ(self.pbanks[6 + i][:, :], self.bres[6 + i]) for i in range(2)]
        self.xres = self.sb(es, "xres", [128, 8, NTOK], F32)
        self.xrr = [Res(f"xr{t}") for t in range(NTILES)]
        self.ident = self.sb(es, "ident", [128, 128], F32)
        self.onesf = self.sb(es, "onesf", [128, 128], F32)
        self.iota16 = self.sb(es, "iota16", [128, 16], F32)
        self.epsb = self.sb(es, "epsb", [128, 1], F32)
        self.cv = self.sb(es, "cv", [128, 544], F32)
        self.modT = self.sb(es, "modT", [128, 2, 48, 18], F32)
        self.mtmp = self.sb(es, "mtmp", [128, 128], F32)
        self.Tb = self.sb(es, "Tb", [128, 8, 128], F32)
        self.Sq = self.sb(es, "Sq", [128, 8, 128], F32)
        self.lnst = self.sb(es, "lnst", [128, 3, 128], F32)
        self.ost = [self.sb(es, f"ost{i}", [128, 128], F32) for i in range(2)]
        self.vstg = [self.sb(es, f"vstg{i}", [128, 128], F32) for i in range(2)]
        self.vstg_d = [self.fw.dsem(f"vs{i}") for i in range(2)]; self.vstg_i = 0
        self.dout = [self.fw.dsem(f"out{i}") for i in range(2)]
        self.tds = [TD(i) for i in range(NTILES)]
        wk = self.mtmp
        E("pool", lambda e: e.iota(wk[:], pattern=[[1, 128]], base=0, channel_multiplier=-1, allow_small_or_imprecise_dtypes=True), [], [wk])
        E("dve", lambda e: e.tensor_single_scalar(self.ident[:], wk[:], 0.0, ALU.is_equal), [wk], [self.ident])
        E("dve", lambda e: e.memset(self.onesf[:], 1.0 / 1024.0), [], [self.onesf])
        E("dve", lambda e: e.memset(self.epsb[:], EPS), [], [self.epsb])
        E("pool", lambda e: e.iota(self.iota16[:], pattern=[[1, 16]], base=0, channel_multiplier=0, allow_small_or_imprecise_dtypes=True), [], [self.iota16])

        if STAGE < -3: return
        col = {}
        cur = [0]

        def vec(name, dram2d, C):
            col[name] = cur[0]
            for r0 in range(0, C, 128):
                r1 = min(C, r0 + 128)
                self.load_fm(dram2d[r0:r1, :], r1 - r0, cur[0] + r0)
            cur[0] += C
        v2 = lambda a: a.rearrange("(c p) -> c p", p=128)
        for li in range(2):
            vec(f"bada{li}", v2(dr["b_ada"][li]), 48)
            vec(f"lnmg{li}", v2(dr["ln_mix_g"][li]), 8); vec(f"lnmb{li}", v2(dr["ln_mix_b"][li]), 8)
            vec(f"lnfg{li}", v2(dr["ln_ffn_g"][li]), 8); vec(f"lnfb{li}", v2(dr["ln_ffn_b"][li]), 8)
        vec("lcw", dr["lru_conv_w"][0].rearrange("k (c p) -> (k c) p", p=128), 40)
        vec("lcb", v2(dr["lru_conv_b"][0]), 10); vec("lba", v2(dr["lru_b_a"][0]), 10)
        vec("lbi", v2(dr["lru_b_i"][0]), 10); vec("lam", v2(dr["lru_lambda"][0]), 10)
        vec("cb1", v2(dr["conf_b_pw1"][0]), 16)
        vec("cdw", dr["conf_dw_w"][0].rearrange("k (c p) -> (k c) p", p=128), 248)
        vec("cdb", v2(dr["conf_dw_b"][0]), 8); vec("clg", v2(dr["conf_ln_g"][0]), 8)
        vec("clb", v2(dr["conf_ln_b"][0]), 8); vec("cb2", v2(dr["conf_b_pw2"][0]), 8)
        self.col = col
        cv = self.cv
        for li in range(2):
            for part in (1, 4):
                c0 = col[f"bada{li}"] + part * 8
                E("dve", lambda e, c0=c0: e.tensor_scalar_add(cv[:, c0:c0 + 8], cv[:, c0:c0 + 8], 1.0), [cv], [cv])
        cl = col["lam"]
        E("act", lambda e: e.activation(cv[:, cl:cl + 10], cv[:, cl:cl + 10], AF.Exp, scale=-1.0), [cv], [cv])
        E("act", lambda e: e.activation(cv[:, cl:cl + 10], cv[:, cl:cl + 10], AF.Ln, bias=1.0, scale=1.0), [cv], [cv])
        E("dve", lambda e: e.tensor_scalar_mul(cv[:, cl:cl + 10], cv[:, cl:cl + 10], -8.0), [cv], [cv])

        if STAGE < -2: return
        with ExitStack() as pes:
            csb = self.sb(pes, "csb", [18, 1024], F32)
            E("dve", lambda e: e.memset(csb[:, :], 0.0), [], [csb])
            cT = self.sb(pes, "cT", [128, 8, 18], BF16)
            wab = [self.sb(pes, f"wab{i}", [128, 8, 512], BF16) for i in range(2)]
            dwa = [self.fw.dsem(f"wa{i}") for i in range(2)]
            dc = self.fw.dsem("c")
            E("sp", lambda e: e.dma_start(out=csb[0:17, :], in_=dr["cc"]), [], [csb], dsem=dc)
            E("act", lambda e: e.activation(csb[:, :], csb[:, :], AF.Silu), [csb], [csb])
            for k in range(8):
                q, o = self.transpose_to(csb[0:18, k * 128:(k + 1) * 128], 18, 128, [csb])
                self.evac(self.alt(), cT[:, k, :], o[:, 0:18], [q], [cT])
            it = 0
            for li in range(2):
                if STAGE < -1.7: break
                for n0 in range(0, 6144, 512):
                    if it >= NCH: break
                    if STAGE < -1.3 and it >= 1: break
                    if STAGE < -1.15 and it >= 2: break
                    if STAGE < -1.05 and it >= 3: break
                    wb = wab[it % 2]; dd = dwa[it % 2]; it += 1
                    src = dr["w_ada"][li][:, n0:n0 + 512].rearrange("(k p) n -> p k n", p=128)
                    for k in range(8):
                        E("pool", lambda e, wb=wb, src=src, k=k: e.dma_start(out=wb[:, k, :], in_=src[:, k, :]), [], [wb], dsem=dd, serial=(k == 0))
                    for s in range(4):
                        if STAGE < -1.5: break
                        mc = n0 // 128 + s
                        q = self.pq()
                        for k in range(8):
                            E("pe", lambda e, k=k, s=s, wb=wb, q=q: e.matmul(q.ap[:, 0:18], lhsT=wb[:, k, s * 128:(s + 1) * 128], rhs=cT[:, k, :],
                                                                          start=(k == 0), stop=(k == 7)), [wb, cT], [q])
                        bc = col[f"bada{li}"] + mc
                        E("act", lambda e, q=q, li=li, mc=mc, bc=bc: e.activation(self.modT[:, li, mc, :], q.ap[:, 0:18], AF.Identity,
                                                                               bias=cv[:, bc:bc + 1], scale=1.0), [q, cv], [self.modT])
            self.fw.barrier()

        if STAGE < -1: return
        with ExitStack() as pes:
            xin = [self.sb(pes, f"xin{i}", [128, 1024], F32) for i in range(2)]
            dx = [self.fw.dsem(f"x{i}") for i in range(2)]
            for td in self.tds:
                xi = xin[td.idx % 2]
                src = dr["xp"][td.tok0:td.tok0 + 128, :] if td.prompt else dr["xs"]
                E("sp", lambda e, xi=xi, src=src, td=td: e.dma_start(out=xi[0:td.nt, :], in_=src), [], [xi], dsem=dx[td.idx % 2])
                for k in range(8):
                    q, o = self.transpose_to(xi[0:td.nt, k * 128:(k + 1) * 128], td.nt, 128, [xi])
                    self.evac(self.alt(), self.xr(td, k), o, [q], [self.xrr[td.idx]])
            self.fw.barrier()

        if STAGE < 1: return
        self.phase_m0()
        if STAGE < 2: return
        self.phase_peer(0)
        if STAGE < 3: return
        self.phase_m1()
        if STAGE < 4: return
        self.phase_peer(1)

    def phase_m0(self):
        E = self.E; dr = self.dr; col = self.col; cv = self.cv
        with ExitStack() as pes:
            w_in = self.sb(pes, "w_in", [128, 8, 2560], BF16)
            w_a = self.sb(pes, "w_a", [128, 10, 128], BF16)
            w_i = self.sb(pes, "w_i", [128, 10, 128], BF16)
            w_out = self.sb(pes, "w_out", [128, 10, 1024], BF16)
            dw = self.fw.dsem("wm0")
            self.load_w(w_in, dr["lru_w_in"][0], 8, 2560, dw)
            E("pool", lambda e: e.dma_start(out=w_a[:, :, :], in_=dr["lru_w_a"][0].rearrange("h i j -> i h j")), [], [w_a], dsem=dw, serial=False)
            E("pool", lambda e: e.dma_start(out=w_i[:, :, :], in_=dr["lru_w_i"][0].rearrange("h i j -> i h j")), [], [w_i], dsem=dw, serial=False)
            self.load_w(w_out, dr["lru_w_out"][0], 10, 1024, dw)
            hmT = self.sb(pes, "hmT", [128, 8, 128], BF16)
            xp_p = self.sb(pes, "xp_p", [128, 10, 1, 131], F32)
            xp_s = self.sb(pes, "xp_s", [128, 10, 16, 7], F32)
            hst_p = self.sb(pes, "hst_p", [128, 10], F32)
            hst_s = self.sb(pes, "hst_s", [128, 10, 16], F32)
            gg = self.sb(pes, "gg", [128, 10, 128], F32)
            yg = self.sb(pes, "yg", [128, 10, 128], BF16)
            nbuf = 2
            xc = [self.sb(pes, f"xc{i}", [128, 128], F32) for i in range(nbuf)]
            xcb = [self.sb(pes, f"xcb{i}", [128, 128], BF16) for i in range(nbuf)]
            rr = [self.sb(pes, f"rr{i}", [128, 128], F32) for i in range(nbuf)]
            ig = [self.sb(pes, f"ig{i}", [128, 128], F32) for i in range(nbuf)]
            aa = [self.sb(pes, f"aa{i}", [128, 128], F32) for i in range(nbuf)]
            a2 = [self.sb(pes, f"a2{i}", [128, 128], F32) for i in range(nbuf)]
            hh = [self.sb(pes, f"hh{i}", [128, 128], F32) for i in range(nbuf)]
            stg = self.sb(pes, "stg0", [128, 1280], F32)
            dst = self.fw.dsem("st0")
            E("dve", lambda e: e.memset(xp_p[:, :, :, :], 0.0), [], [xp_p])
            E("dve", lambda e: e.memset(hst_p[:, :], 0.0), [], [hst_p])
            E("sp", lambda e: e.dma_start(out=stg[0:16, :], in_=dr["slh"]), [], [stg], dsem=dst)
            for c in range(10):
                q, o = self.transpose_to(stg[0:16, c * 128:(c + 1) * 128], 16, 128, [stg])
                self.evac(self.alt(), hst_s[:, c, :], o, [q], [hst_s])
            E("sp", lambda e: e.dma_start(out=stg[0:48, :], in_=dr["slc"].rearrange("s k d -> (s k) d")), [], [stg], dsem=dst)
            for c in range(10):
                q, o = self.transpose_to(stg[0:48, c * 128:(c + 1) * 128], 48, 128, [stg])
                self.evac(self.alt(), xp_s[:, c, :, 0:3], o.rearrange("p (s k) -> p s k", k=3), [q], [xp_s])

            for td in self.tds:
                nt = td.nt; L = td.L; ns = td.nseq
                xp = xp_p if td.prompt else xp_s
                for k in range(8):
                    self.modulate(hmT[:, k, 0:nt], self.xr(td, k), td, self.modap(0, 1, k, td), self.modap(0, 0, k, td),
                                  [self.xrr[td.idx], self.modT], [hmT])
                for c in range(20):
                    q = self.pq()
                    for k in range(8):
                        E("pe", lambda e, k=k, c=c, q=q: e.matmul(q.ap[:, 0:nt], lhsT=w_in[:, k, c * 128:(c + 1) * 128], rhs=hmT[:, k, 0:nt],
                                                                 start=(k == 0), stop=(k == 7)), [w_in, hmT], [q])
                    if c < 10:
                        o3 = xp[:, c, :, 3:3 + L]
                        self.evac("dve", o3, q.ap[:, 0:nt].rearrange("p (s l) -> p s l", l=L), [q], [xp])
                    else:
                        E("act", lambda e, c=c, q=q: e.activation(gg[:, c - 10, 0:nt], q.ap[:, 0:nt], AF.Gelu), [q], [gg])
                for c in range(10):
                    b = c % nbuf
                    xc3 = xc[b][:, 0:nt].rearrange("p (s l) -> p s l", l=L)
                    lw = col["lcw"]
                    E("dve", lambda e, c=c, xc3=xc3: e.tensor_scalar(xc3, xp[:, c, :, 0:L], cv[:, lw + c:lw + c + 1],
                                                                     cv[:, col["lcb"] + c:col["lcb"] + c + 1], op0=ALU.mult, op1=ALU.add),
                      [xp, cv], [xc[b]])
                    for kk in range(1, 4):
                        E("dve", lambda e, c=c, kk=kk, xc3=xc3: e.scalar_tensor_tensor(out=xc3, in0=xp[:, c, :, kk:kk + L],
                                                                                        scalar=cv[:, lw + kk * 10 + c:lw + kk * 10 + c + 1],
                                                                                        in1=xc3, op0=ALU.mult, op1=ALU.add), [xp, cv, xc[b]], [xc[b]])
                    if td.prompt:
                        E("pool", lambda e, c=c: e.tensor_copy(xp[:, c, :, 0:3], xp[:, c, :, L:L + 3]), [xp], [xp])
                    E("act", lambda e, b=b: e.copy(xcb[b][:, 0:nt], xc[b][:, 0:nt]), [xc[b]], [xcb[b]])
                    qa = self.pq(); qi = self.pq()
                    E("pe", lambda e, c=c, b=b, qa=qa: e.matmul(qa.ap[:, 0:nt], lhsT=w_a[:, c, :], rhs=xcb[b][:, 0:nt], start=True, stop=True), [w_a, xcb[b]], [qa])
                    E("pe", lambda e, c=c, b=b, qi=qi: e.matmul(qi.ap[:, 0:nt], lhsT=w_i[:, c, :], rhs=xcb[b][:, 0:nt], start=True, stop=True), [w_i, xcb[b]], [qi])
                    E("act", lambda e, c=c, b=b, qa=qa: e.activation(rr[b][:, 0:nt], qa.ap[:, 0:nt], AF.Sigmoid,
                                                                     bias=cv[:, col["lba"] + c:col["lba"] + c + 1], scale=1.0), [qa, cv], [rr[b]])
                    E("act", lambda e, c=c, b=b, qi=qi: e.activation(ig[b][:, 0:nt], qi.ap[:, 0:nt], AF.Sigmoid,
                                                                     bias=cv[:, col["lbi"] + c:col["lbi"] + c + 1], scale=1.0), [qi, cv], [ig[b]])
                    E("act", lambda e, c=c, b=b: e.activation(aa[b][:, 0:nt], rr[b][:, 0:nt], AF.Exp,
                                                              scale=cv[:, col["lam"] + c:col["lam"] + c + 1]), [rr[b], cv], [aa[b]])
                    E("dve", lambda e, b=b: e.tensor_tensor(a2[b][:, 0:nt], aa[b][:, 0:nt], aa[b][:, 0:nt], ALU.mult), [aa[b]], [a2[b]])
                    E("act", lambda e, b=b: e.activation(a2[b][:, 0:nt], a2[b][:, 0:nt], AF.Sqrt, bias=self.onesb[:, 0:1], scale=-1.0), [a2[b], self.onesb], [a2[b]])
                    E("dve", lambda e, b=b: e.tensor_tensor(ig[b][:, 0:nt], ig[b][:, 0:nt], xc[b][:, 0:nt], ALU.mult), [ig[b], xc[b]], [ig[b]])
                    E("dve", lambda e, b=b: e.tensor_tensor(ig[b][:, 0:nt], ig[b][:, 0:nt], a2[b][:, 0:nt], ALU.mult), [ig[b], a2[b]], [ig[b]])
                    if td.prompt:
                        E("dve", lambda e, c=c, b=b: e.tensor_tensor_scan(hh[b][:, 0:nt], aa[b][:, 0:nt], ig[b][:, 0:nt], hst_p[:, c:c + 1],
                                                                           op0=ALU.mult, op1=ALU.add), [aa[b], ig[b], hst_p], [hh[b]])
                        E("dve", lambda e, c=c, b=b: e.tensor_copy(hst_p[:, c:c + 1], hh[b][:, nt - 1:nt]), [hh[b]], [hst_p])
                    else:
                        h3 = hh[b][:, 0:nt].rearrange("p (s l) -> p s l", l=L)
                        a3 = aa[b][:, 0:nt].rearrange("p (s l) -> p s l", l=L)
                        b3 = ig[b][:, 0:nt].rearrange("p (s l) -> p s l", l=L)
                        for t in range(L):
                            prev = hst_s[:, c, :] if t == 0 else h3[:, :, t - 1]
                            E("dve", lambda e, t=t, prev=prev, h3=h3, a3=a3: e.tensor_tensor(h3[:, :, t], a3[:, :, t], prev, ALU.mult),
                              [aa[b], hst_s, hh[b]], [hh[b]])
                            E("dve", lambda e, t=t, h3=h3, b3=b3: e.tensor_tensor(h3[:, :, t], h3[:, :, t], b3[:, :, t], ALU.add),
                              [ig[b], hh[b]], [hh[b]])
                        E("dve", lambda e, c=c, h3=h3: e.tensor_copy(hst_s[:, c, :], h3[:, :, L - 1]), [hh[b]], [hst_s])
                    E("dve", lambda e, c=c, b=b: e.tensor_tensor(yg[:, c, 0:nt], hh[b][:, 0:nt], gg[:, c, 0:nt], ALU.mult), [hh[b], gg], [yg])
                srcs = []
                for j in range(8):
                    q = self.pq()
                    for k in range(10):
                        E("pe", lambda e, k=k, j=j, q=q: e.matmul(q.ap[:, 0:nt], lhsT=w_out[:, k, j * 128:(j + 1) * 128], rhs=yg[:, k, 0:nt],
                                                                 start=(k == 0), stop=(k == 9)), [w_out, yg], [q])
                    srcs.append((q.ap[:, 0:nt], q.r))
                self.resid_ln(td, 0, 2, srcs, col["lnmg0"], col["lnmb0"])
                if td.idx == 15:
                    self.state_out(hst_p, 10, 10, self.dr["o_hp"], [hst_p])
                    E("dve", lambda e: e.tensor_copy(stg[:, 0:30].rearrange("p (k c) -> p k c", c=10),
                                                     xp_p[:, :, 0, 0:3].rearrange("p c k -> p k c")), [xp_p], [stg])
                    self.state_out(stg, 30, 30, self.dr["o_lcp"], [stg])
                if td.idx == 16:
                    E("dve", lambda e: e.tensor_copy(stg[:, 0:160].rearrange("p (s c) -> p s c", c=10),
                                                     hst_s[:, :, :].rearrange("p c s -> p s c")), [hst_s], [stg])
                    self.state_out(stg, 160, 80, self.dr["o_hs"], [stg])
                    for s in range(16):
                        E("dve", lambda e, s=s: e.tensor_copy(stg[:, 160 + s * 30:160 + (s + 1) * 30].rearrange("p (k c) -> p k c", c=10),
                                                              xp_s[:, :, s, 4:7].rearrange("p c k -> p k c")), [xp_s], [stg])
                    self.state_out(stg[:, 160:640], 480, 120, self.dr["o_lcs"], [stg])
            self.fw.barrier()

    def phase_m1(self):
        E = self.E; dr = self.dr; col = self.col; cv = self.cv
        with ExitStack() as pes:
            w1 = self.sb(pes, "w_pw1", [128, 8, 2048], BF16)
            w2 = self.sb(pes, "w_pw2", [128, 8, 1024], BF16)
            dw = self.fw.dsem("wm1")
            self.load_w(w1, dr["conf_w_pw1"][0], 8, 2048, dw)
            self.load_w(w2, dr["conf_w_pw2"][0], 8, 1024, dw)
            hmT = self.sb(pes, "hmT1", [128, 8, 128], BF16)
            sg = self.sb(pes, "sg", [128, 8, 128], F32)
            up_p = self.sb(pes, "up_p", [128, 8, 1, 158], F32)
            up_s = self.sb(pes, "up_s", [128, 8, 16, 34], F32)
            dd = self.sb(pes, "dd", [128, 8, 128], F32)
            dsT = self.sb(pes, "dsT", [128, 8, 128], BF16)
            osb = self.sb(pes, "osb", [128, 8, 128], F32)
            stg = self.sb(pes, "stg1", [128, 1024], F32)
            dst = self.fw.dsem("st1")
            E("dve", lambda e: e.memset(up_p[:, :, :, :], 0.0), [], [up_p])
            src = dr["scc"].rearrange("s k d -> (s k) d")
            for g in range(4):
                E("sp", lambda e, g=g: e.dma_start(out=stg[0:120, :], in_=src[g * 120:(g + 1) * 120, :]), [], [stg], dsem=dst)
                for c in range(8):
                    q, o = self.transpose_to(stg[0:120, c * 128:(c + 1) * 128], 120, 128, [stg])
                    self.evac(self.alt(), up_s[:, c, g * 4:(g + 1) * 4, 0:30], o.rearrange("p (s k) -> p s k", k=30), [q], [up_s])
            for td in self.tds:
                nt = td.nt; L = td.L
                up = up_p if td.prompt else up_s
                for k in range(8):
                    self.modulate(hmT[:, k, 0:nt], self.xr(td, k), td, self.modap(1, 1, k, td), self.modap(1, 0, k, td),
                                  [self.xrr[td.idx], self.modT], [hmT])
                for c in list(range(8, 16)) + list(range(8)):
                    q = self.pq()
                    for k in range(8):
                        E("pe", lambda e, k=k, c=c, q=q: e.matmul(q.ap[:, 0:nt], lhsT=w1[:, k, c * 128:(c + 1) * 128], rhs=hmT[:, k, 0:nt],
                                                                 start=(k == 0), stop=(k == 7)), [w1, hmT], [q])
                    bcol = col["cb1"] + c
                    if c >= 8:
                        E("act", lambda e, c=c, q=q, bcol=bcol: e.activation(sg[:, c - 8, 0:nt], q.ap[:, 0:nt], AF.Sigmoid,
                                                                            bias=cv[:, bcol:bcol + 1], scale=1.0), [q, cv], [sg])
                    else:
                        E("dve", lambda e, c=c, q=q, bcol=bcol: e.scalar_tensor_tensor(
                            out=up[:, c, :, 30:30 + L], in0=q.ap[:, 0:nt].rearrange("p (s l) -> p s l", l=L), scalar=cv[:, bcol:bcol + 1],
                            in1=sg[:, c, 0:nt].rearrange("p (s l) -> p s l", l=L), op0=ALU.add, op1=ALU.mult), [q, cv, sg], [up])
                for c in range(8):
                    d3 = dd[:, c, 0:nt].rearrange("p (s l) -> p s l", l=L)
                    cw = col["cdw"]
                    E("dve", lambda e, c=c, d3=d3: e.tensor_scalar(d3, up[:, c, :, 0:L], cv[:, cw + c:cw + c + 1],
                                                                   cv[:, col["cdb"] + c:col["cdb"] + c + 1], op0=ALU.mult, op1=ALU.add), [up, cv], [dd])
                    for kk in range(1, 31):
                        E("dve", lambda e, c=c, kk=kk, d3=d3: e.scalar_tensor_tensor(out=d3, in0=up[:, c, :, kk:kk + L],
                                                                                      scalar=cv[:, cw + kk * 8 + c:cw + kk * 8 + c + 1],
                                                                                      in1=d3, op0=ALU.mult, op1=ALU.add), [up, cv, dd], [dd])
                    if td.prompt:
                        E("pool", lambda e, c=c: e.tensor_copy(up[:, c, :, 0:30], up[:, c, :, L:L + 30]), [up], [up])
                Sq = self.Sq
                q1 = self.pq(); q2 = self.pq()
                for j in range(8):
                    E("act", lambda e, j=j: e.activation(Sq[:, j, 0:nt], dd[:, j, 0:nt], AF.Square), [dd], [Sq])
                for j in range(8):
                    E("pe", lambda e, j=j: e.matmul(q1.ap[:, 0:nt], lhsT=self.onesf[:, :], rhs=dd[:, j, 0:nt], start=(j == 0), stop=(j == 7)), [dd, self.onesf], [q1])
                for j in range(8):
                    E("pe", lambda e, j=j: e.matmul(q2.ap[:, 0:nt], lhsT=self.onesf[:, :], rhs=Sq[:, j, 0:nt], start=(j == 0), stop=(j == 7)), [Sq, self.onesf], [q2])
                self.ln_finish(td, q1, q2, dd, col["clg"], col["clb"], lambda j: dsT[:, j, 0:nt], [dsT], AF.Silu)
                srcs = []
                for j in range(8):
                    q = self.pq()
                    for k in range(8):
                        E("pe", lambda e, k=k, j=j, q=q: e.matmul(q.ap[:, 0:nt], lhsT=w2[:, k, j * 128:(j + 1) * 128], rhs=dsT[:, k, 0:nt],
                                                                 start=(k == 0), stop=(k == 7)), [w2, dsT], [q])
                    bcol = col["cb2"] + j
                    E("act", lambda e, j=j, q=q, bcol=bcol: e.activation(osb[:, j, 0:nt], q.ap[:, 0:nt], AF.Identity, bias=cv[:, bcol:bcol + 1], scale=1.0),
                      [q, cv], [osb])
                    srcs.append((osb[:, j, 0:nt], osb.r))
                self.resid_ln(td, 1, 2, srcs, col["lnmg1"], col["lnmb1"])
                if td.idx == 15:
                    E("dve", lambda e: e.tensor_copy(stg[:, 0:240].rearrange("p (k c) -> p k c", c=8),
                                                     up_p[:, :, 0, 0:30].rearrange("p c k -> p k c")), [up_p], [stg])
                    self.state_out(stg, 240, 120, self.dr["o_ccp"], [stg])
                if td.idx == 16:
                    for s in range(16):
                        E("dve", lambda e, s=s: e.tensor_copy(stg[:, 0:240].rearrange("p (k c) -> p k c", c=8),
                                                              up_s[:, :, s, 4:34].rearrange("p c k -> p k c")), [up_s], [stg])
                        self.state_out(stg, 240, 120, self.dr["o_ccs"][s * 240:(s + 1) * 240, :], [stg])
            self.fw.barrier()

    def phase_peer(self, li):
        E = self.E; dr = self.dr; col = self.col; cv = self.cv
        last = (li == 1)
        with ExitStack() as pes:
            wq = self.sb(pes, "wq", [128, 8, 2048], BF16)
            keysT = self.sb(pes, "keysT", [128, 16, 128], BF16)
            dw = self.fw.dsem(f"wq{li}")
            self.load_w(wq, dr["peer_w_q"][li], 8, 2048, dw)
            hfT32 = self.sb(pes, "hfT32", [128, 8, 128], F32)
            hfT = self.sb(pes, "hfT", [128, 8, 128], BF16)
            hf = self.sb(pes, "hf", [128, 1024], F32)
            qT = self.sb(pes, "qT", [128, 16, 128], BF16)
            S = self.sb(pes, "S", [128, 8, 2, 128], F32)
            S2 = self.sb(pes, "S2", [128, 128], F32)
            vv = self.sb(pes, "vv", [128, 8, 2, 16], F32)
            iu = self.sb(pes, "iu", [128, 8, 2, 16], U32)
            iff = self.sb(pes, "iff", [128, 8, 2, 16], F32)
            cand = self.sb(pes, "cand", [128, 4, 16, 16], F32)
            cand2 = Tl(self.Sq.t[:, :, :].rearrange("p a b -> p (a b)").rearrange("p (h i j) -> p h i j", h=4, i=16), "x"); cand2.r = self.Sq.r
            ts = self.sb(pes, "ts", [128, 8, 16], F32)
            pu = self.sb(pes, "pu", [128, 8, 16], U32)
            p1u = self.sb(pes, "p1u", [128, 8, 16], U32)
            p1f = self.sb(pes, "p1f", [128, 8, 16], F32)
            p2f = self.sb(pes, "p2f", [128, 8, 16], F32)
            gte = self.sb(pes, "gte", [128, 8, 16], F32)
            ssum = self.sb(pes, "ssum", [128, 8], F32)
            e1 = self.sb(pes, "e1", [128, 128], F32)
            e2 = self.sb(pes, "e2", [128, 128], F32)
            idx = self.sb(pes, "idx", [128, 128], I32)
            dots = self.sb(pes, "dots", [128, 128], F32)
            wgt = self.sb(pes, "wgt", [128, 128], F32)
            gb = [self.sb(pes, f"gb{i}", [128, 2048], F32) for i in range(NB)]
            dg = [self.fw.dsem(f"g{li}_{i}") for i in range(NB)]
            junk = Tl(self.Sq.t[:, :, :].rearrange("p a b -> p (a b)"), "x"); junk.r = self.Sq.r
            acc = Tl(self.Tb.t[:, :, :].rearrange("p a b -> p (a b)"), "x"); acc.r = self.Tb.r
            yst = Tl(hfT32.t[:, :, :].rearrange("p a b -> p (a b)"), "x"); yst.r = hfT32.r
            gel = self.sb(pes, "gel", [128, 128], F32)
            dgt = [self.sb(pes, f"dgt{i}", [128, 128], F32) for i in range(3)]
            dy = self.fw.dsem(f"y{li}")
            kst = self.sb(pes, "kst", [128, 128], F32)
            dk = self.fw.dsem(f"k{li}")
            E("pool", lambda e: e.memset(idx[:, :], 0), [], [idx])
            E("dve", lambda e: e.memset(hf[:, :], 0.0), [], [hf])
            ksrc = dr["peer_keys"][li].rearrange("h p n d -> (h p n) d")
            for c in range(16):
                E("sp", lambda e, c=c: e.dma_start(out=kst[:, :], in_=ksrc[c * 128:(c + 1) * 128, :]), [], [kst], dsem=dk)
                q, o = self.transpose_to(kst[:, :], 128, 128, [kst])
                self.evac(self.alt(), keysT[:, c, :], o, [q], [keysT])
            uvtab = dr["peer_uv"].rearrange("l e d -> (l e) d")
            for td in self.tds:
                nt = td.nt
                for k in range(8):
                    self.modulate(hfT32[:, k, 0:nt], self.xr(td, k), td, self.modap(li, 4, k, td), self.modap(li, 3, k, td),
                                  [self.xrr[td.idx], self.modT], [hfT32])
                    E("act", lambda e, k=k: e.copy(hfT[:, k, 0:nt], hfT32[:, k, 0:nt]), [hfT32], [hfT])
                    q, o = self.transpose_to(hfT32[:, k, 0:nt], 128, nt, [hfT32])
                    self.evac("dve", hf[0:nt, k * 128:(k + 1) * 128], o, [q], [hf])
                for c in range(16):
                    q = self.pq()
                    for k in range(8):
                        E("pe", lambda e, k=k, c=c, q=q: e.matmul(q.ap[:, 0:nt], lhsT=wq[:, k, c * 128:(c + 1) * 128], rhs=hfT[:, k, 0:nt],
                                                                 start=(k == 0), stop=(k == 7)), [wq, hfT], [q])
                    self.evac("act", qT[:, c, 0:nt], q.ap[:, 0:nt], [q], [qT])
                for c in range(16):
                    q = self.pq()
                    E("pe", lambda e, c=c, q=q: e.matmul(q.ap[0:nt, :], lhsT=qT[:, c, 0:nt], rhs=keysT[:, c, :], start=True, stop=True), [qT, keysT], [q])
                    self.evac("act", S[0:nt, c // 2, c % 2, :], q.ap[0:nt, :], [q], [S])
                for c in range(16):
                    h, p = c // 2, c % 2
                    sc_ = S[0:nt, h, p, :]
                    E("dve", lambda e, h=h, p=p, sc_=sc_: e.max(out=vv[0:nt, h, p, 0:8], in_=sc_), [S], [vv])
                    E("dve", lambda e, h=h, p=p, sc_=sc_: e.max_index(iu[0:nt, h, p, 0:8], vv[0:nt, h, p, 0:8], sc_), [S, vv], [iu])
                    E("dve", lambda e, h=h, p=p, sc_=sc_: e.match_replace(out=S2[0:nt, :], in_to_replace=vv[0:nt, h, p, 0:8], in_values=sc_, imm_value=-1e30), [S, vv], [S2])
                    E("dve", lambda e, h=h, p=p: e.max(out=vv[0:nt, h, p, 8:16], in_=S2[0:nt, :]), [S2], [vv])
                    E("dve", lambda e, h=h, p=p: e.max_index(iu[0:nt, h, p, 8:16], vv[0:nt, h, p, 8:16], S2[0:nt, :]), [S2, vv], [iu])
                E("dve", lambda e: e.tensor_copy(iff[0:nt], iu[0:nt]), [iu], [iff])
                for hh_ in range(2):
                    h0 = hh_ * 4
                    E("dve", lambda e, h0=h0: e.tensor_tensor(cand[0:nt], vv[0:nt, h0:h0 + 4, 0, :].unsqueeze(3).to_broadcast([nt, 4, 16, 16]),
                                                              vv[0:nt, h0:h0 + 4, 1, :].unsqueeze(2).to_broadcast([nt, 4, 16, 16]), ALU.add), [vv], [cand])
                    for hl in range(4):
                        h = h0 + hl
                        cf = cand[0:nt, hl].rearrange("p a b -> p (a b)")
                        c2 = cand2[0:nt, hl].rearrange("p a b -> p (a b)")
                        E("dve", lambda e, h=h, cf=cf: e.max(out=ts[0:nt, h, 0:8], in_=cf), [cand], [ts])
                        E("dve", lambda e, h=h, cf=cf: e.max_index(pu[0:nt, h, 0:8], ts[0:nt, h, 0:8], cf), [cand, ts], [pu])
                        E("dve", lambda e, h=h, cf=cf, c2=c2: e.match_replace(out=c2, in_to_replace=ts[0:nt, h, 0:8], in_values=cf, imm_value=-1e30), [cand, ts], [cand2])
                        E("dve", lambda e, h=h, c2=c2: e.max(out=ts[0:nt, h, 8:16], in_=c2), [cand2], [ts])
                        E("dve", lambda e, h=h, c2=c2: e.max_index(pu[0:nt, h, 8:16], ts[0:nt, h, 8:16], c2), [cand2, ts], [pu])
                E("dve", lambda e: e.tensor_tensor(gte[0:nt], ts[0:nt], ts[0:nt, :, 0:1].to_broadcast([nt, 8, 16]), ALU.subtract), [ts], [gte])
                E("act", lambda e: e.activation(gte[0:nt], gte[0:nt], AF.Exp), [gte], [gte])
                E("dve", lambda e: e.tensor_reduce(ssum[0:nt], gte[0:nt], axis=AX.X, op=ALU.add), [gte], [ssum])
                E("dve", lambda e: e.reciprocal(ssum[0:nt], ssum[0:nt]), [ssum], [ssum])
                E("dve", lambda e: e.tensor_tensor(gte[0:nt], gte[0:nt], ssum[0:nt].unsqueeze(2).to_broadcast([nt, 8, 16]), ALU.mult), [gte, ssum], [gte])
                E("dve", lambda e: e.tensor_single_scalar(p1u[0:nt], pu[0:nt], 4, ALU.logical_shift_right), [pu], [p1u])
                E("dve", lambda e: e.tensor_copy(p1f[0:nt], p1u[0:nt]), [p1u], [p1f])
                E("dve", lambda e: e.tensor_single_scalar(p1u[0:nt], pu[0:nt], 15, ALU.bitwise_and), [pu, p1f], [p1u])
                E("dve", lambda e: e.tensor_copy(p2f[0:nt], p1u[0:nt]), [p1u], [p2f])
                for hh_ in range(2):
                    h0 = hh_ * 4
                    for (pf, half, eo) in ((p1f, 0, e1), (p2f, 1, e2)):
                        oh = cand[0:nt].rearrange("p a b c -> p (a b) c")
                        E("dve", lambda e, pf=pf, h0=h0, oh=oh: e.tensor_tensor(
                            oh, pf[0:nt, h0:h0 + 4, :].rearrange("p a b -> p (a b)").unsqueeze(2).to_broadcast([nt, 64, 16]),
                            self.iota16[0:nt, :].unsqueeze(1).to_broadcast([nt, 64, 16]), ALU.is_equal), [pf, self.iota16], [cand])
                        E("dve", lambda e, half=half, h0=h0: e.tensor_tensor(
                            cand[0:nt], cand[0:nt], iff[0:nt, h0:h0 + 4, half, :].unsqueeze(2).to_broadcast([nt, 4, 16, 16]), ALU.mult), [cand, iff], [cand])
                        E("dve", lambda e, eo=eo, h0=h0, oh=oh: e.tensor_reduce(eo[0:nt, h0 * 16:(h0 + 4) * 16], oh, axis=AX.X, op=ALU.add), [cand], [eo])
                E("dve", lambda e: e.scalar_tensor_tensor(out=e1[0:nt, :], in0=e1[0:nt, :], scalar=128.0, in1=e2[0:nt, :], op0=ALU.mult, op1=ALU.add), [e1, e2], [e1])
                if li > 0:
                    E("dve", lambda e: e.tensor_scalar_add(e1[0:nt, :], e1[0:nt, :], float(li * 16384)), [e1], [e1])
                E("dve", lambda e: e.tensor_copy(idx[0:nt, :], e1[0:nt, :]), [e1], [idx])
                gflat = gte[:, :, :].rearrange("p a b -> p (a b)")
                for j in range(128):
                    s = j % NB
                    E("pool", lambda e, j=j, s=s: e.indirect_dma_start(out=gb[s][:, :], out_offset=None, in_=uvtab,
                                                                        in_offset=bass.IndirectOffsetOnAxis(ap=idx[:, j:j + 1], axis=0)),
                      [idx], [gb[s]], dsem=dg[s])
                    E("dve", lambda e, j=j, s=s: e.scalar_tensor_tensor(out=junk[:, :], in0=hf[:, :], scalar=1.0, in1=gb[s][:, 0:1024],
                                                                         op0=ALU.mult, op1=ALU.mult, accum_out=dots[:, j:j + 1]),
                      [hf, gb[s]], [junk, dots])
                    E("act", lambda e, j=j: e.activation(gel[0:nt, j:j + 1], dots[0:nt, j:j + 1], AF.Gelu), [dots], [gel])
                    E("act", lambda e, j=j: e.activation(wgt[0:nt, j:j + 1], gel[0:nt, j:j + 1], AF.Identity, scale=gflat[0:nt, j:j + 1]), [gel, gte], [wgt])
                    dt_ = dgt[j % 3]
                    E("act", lambda e, j=j, dt_=dt_: e.activation(dt_[0:nt, 0:nt], self.ident[0:nt, 0:nt], AF.Identity, scale=wgt[0:nt, j:j + 1]),
                      [wgt, self.ident], [dt_])
                    for hv in range(2):
                        aq = self.accq[hv]
                        E("pe", lambda e, j=j, s=s, hv=hv, aq=aq, dt_=dt_: e.matmul(aq.ap[0:nt, :], lhsT=dt_[0:nt, 0:nt],
                                                                                   rhs=gb[s][0:nt, 1024 + hv * 512:1536 + hv * 512],
                                                                                   start=(j == 0), stop=(j == 127)), [dt_, gb[s]], [aq])
                for hv in range(2):
                    self.evac("act" if hv else "dve", acc[0:nt, hv * 512:(hv + 1) * 512], self.accq[hv].ap[0:nt, :], [self.accq[hv]], [acc])
                srcs = []
                for k in range(8):
                    q, o = self.transpose_to(acc[0:nt, k * 128:(k + 1) * 128], nt, 128, [acc])
                    srcs.append((o, q.r))
                self.resid_ln(td, li, 5, srcs, col[f"lnfg{li}"], col[f"lnfb{li}"])
                if last:
                    for k in range(8):
                        q, o = self.transpose_to(self.xr(td, k), 128, nt, [self.xrr[td.idx]])
                        self.evac(self.alt(), yst[0:nt, k * 128:(k + 1) * 128], o, [q], [yst])
                    dst_ = dr["o_yp"][td.tok0:td.tok0 + 128, :] if td.prompt else dr["o_ys"]
                    E("sp", lambda e, dst_=dst_: e.dma_start(out=dst_, in_=yst[0:nt, :]), [yst], [], dsem=dy, is_out=True)
            self.fw.barrier()


def build_nc():
    nc = bass.Bass("TRN2", target_bir_lowering=False)
    dr = {}

    def din(name, shape, dt=F32):
        dr[name] = nc.dram_tensor(name, list(shape), dt, kind="ExternalInput").ap()

    def dout(name, shape):
        dr[name] = nc.dram_tensor(name, list(shape), F32, kind="ExternalOutput").ap()
    din("xp", [2048, 1024]); din("xs", [64, 1024]); din("slh", [16, 1280]); din("slc", [16, 3, 1280])
    din("scc", [16, 30, 1024]); din("cc", [17, 1024])
    din("w_ada", [2, 1024, 6144]); din("b_ada", [2, 6144])
    for n in ("ln_mix_g", "ln_mix_b", "ln_ffn_g", "ln_ffn_b"):
        din(n, [2, 1024])
    din("lru_w_in", [1, 1024, 2560]); din("lru_conv_w", [1, 4, 1280]); din("lru_conv_b", [1, 1280])
    din("lru_w_a", [1, 10, 128, 128]); din("lru_b_a", [1, 1280]); din("lru_w_i", [1, 10, 128, 128]); din("lru_b_i", [1, 1280])
    din("lru_lambda", [1, 1280]); din("lru_w_out", [1, 1280, 1024])
    din("conf_w_pw1", [1, 1024, 2048]); din("conf_b_pw1", [1, 2048]); din("conf_dw_w", [1, 31, 1024]); din("conf_dw_b", [1, 1024])
    din("conf_ln_g", [1, 1024]); din("conf_ln_b", [1, 1024]); din("conf_w_pw2", [1, 1024, 1024]); din("conf_b_pw2", [1, 1024])
    din("peer_w_q", [2, 1024, 2048]); din("peer_keys", [2, 8, 2, 128, 128]); din("peer_uv", [2, 16384, 2048])
    dout("o_yp", [2048, 1024]); dout("o_ys", [64, 1024]); dout("o_hp", [10, 128]); dout("o_lcp", [30, 128]); dout("o_ccp", [240, 128])
    dout("o_hs", [160, 128]); dout("o_lcs", [480, 128]); dout("o_ccs", [3840, 128])
    with ExitStack() as es:
        fw = FW(nc, es)
        k = Kern(nc, fw, es, dr)
        k.onesb = k.sb(es, "onesb", [128, 1], F32)
        k.E("dve", lambda e: e.memset(k.onesb[:, :], 1.0), [], [k.onesb])
        k.run()
        fw.finish()
    return nc


_WNAMES = ["w_ada", "b_ada", "ln_mix_g", "ln_mix_b", "ln_ffn_g", "ln_ffn_b", "lru_w_in", "lru_conv_w", "lru_conv_b", "lru_w_a", "lru_b_a",
           "lru_w_i", "lru_b_i", "lru_lambda", "lru_w_out", "conf_w_pw1", "conf_b_pw1", "conf_dw_w", "conf_dw_b", "conf_ln_g", "conf_ln_b",
           "conf_w_pw2", "conf_b_pw2", "peer_w_q", "peer_keys"]


def kernel(**inp):
    f = lambda a: np.ascontiguousarray(np.asarray(a, dtype=np.float32))
    W = {n: f(inp[n]) for n in _WNAMES}
    W["peer_uv"] = np.ascontiguousarray(np.concatenate([f(inp["peer_u"]), f(inp["peer_v"])], axis=2))
    xp = f(inp["x_prompt"]); xs = f(inp["x_sample"])
    slh = f(inp["state_lru_h"]); slc = f(inp["state_lru_conv"]); scc = f(inp["state_conf_conv"])
    cp = f(inp["c_prompt"]); cs = f(inp["c_sample"])
    in_maps = []
    for i in range(NCORES):
        m = dict(W)
        sl = slice(16 * i, 16 * i + 16)
        m["xp"] = xp[i]; m["xs"] = np.ascontiguousarray(xs[sl].reshape(64, 1024))
        m["slh"] = np.ascontiguousarray(slh[0, sl]); m["slc"] = np.ascontiguousarray(slc[0, sl]); m["scc"] = np.ascontiguousarray(scc[0, sl])
        m["cc"] = np.ascontiguousarray(np.concatenate([cp[i:i + 1], cs[sl]], 0))
        in_maps.append(m)
    nc = build_nc()
    res = run_bass_kernel_spmd(nc, in_maps, core_ids=list(range(NCORES)))
    R = res.results
    y_p = np.stack([R[i]["o_yp"] for i in range(NCORES)], 0).astype(np.float32)
    y_s = np.concatenate([R[i]["o_ys"].reshape(16, 4, 1024) for i in range(NCORES)], 0).astype(np.float32)
    h_p = np.stack([R[i]["o_hp"].reshape(1280) for i in range(NCORES)], 0)[None].astype(np.float32)
    lc_p = np.stack([R[i]["o_lcp"].reshape(3, 1280) for i in range(NCORES)], 0)[None].astype(np.float32)
    cc_p = np.stack([R[i]["o_ccp"].reshape(30, 1024) for i in range(NCORES)], 0)[None].astype(np.float32)
    h_s = np.concatenate([R[i]["o_hs"].reshape(16, 1280) for i in range(NCORES)], 0)[None].astype(np.float32)
    lc_s = np.concatenate([R[i]["o_lcs"].reshape(16, 3, 1280) for i in range(NCORES)], 0)[None].astype(np.float32)
    cc_s = np.concatenate([R[i]["o_ccs"].reshape(16, 30, 1024) for i in range(NCORES)], 0)[None].astype(np.float32)
    return (y_p, y_s, h_p, lc_p, cc_p, h_s, lc_s, cc_s)
```

```python
import numpy as np
from contextlib import ExitStack
import concourse.bass as bass
import concourse.mybir as mybir
from concourse.bass_utils import run_bass_kernel_spmd

F32 = mybir.dt.float32; BF16 = mybir.dt.bfloat16; I32 = mybir.dt.int32; U32 = mybir.dt.uint32
ALU = mybir.AluOpType; AF = mybir.ActivationFunctionType; AX = mybir.AxisListType

NCORES = 8
D = 1024; KD = 8; DR = 1280; KR = 10
NTILES = 17; NTOK = 2112
ALPHA = 4.0 ** 0.25
EPS = 1e-5
NB = 6
STAGE = 99
NCH = 99


class Res:
    __slots__ = ("name", "lw", "rd", "excl")

    def __init__(self, name, excl=False):
        self.name = name; self.lw = None; self.rd = {}; self.excl = excl


class Tl:
    def __init__(self, t, name):
        self.t = t; self.r = Res(name)

    def __getitem__(self, k):
        return self.t[k]


class PQ:
    def __init__(self, ap, res):
        self.ap = ap; self.r = res


class FW:
    def __init__(self, nc, es):
        self.nc = nc; self.es = es
        self.q = {"pe": nc.tensor, "act": nc.scalar, "dve": nc.vector, "pool": nc.gpsimd, "sp": nc.sync}
        self.sem = {}; self.cnt = {}; self.waited = {k: {} for k in self.q}
        self.nsem = 0; self.dsems = []
        for k in self.q:
            self._newsem(k)
        self.out_evs = []

    def _alloc(self, name):
        self.nsem += 1
        return self.es.enter_context(self.nc.semaphore(f"{name}_{self.nsem}"))

    def _newsem(self, k):
        self.sem[k] = self._alloc("e" + k); self.cnt[k] = 0

    def dsem(self, name):
        d = [self._alloc("d" + name), 0, None]
        self.dsems.append(d)
        return d

    def _wait(self, eng, ev):
        if ev is None:
            return
        s, v = ev
        w = self.waited[eng]
        if w.get(id(s), -1) >= v:
            return
        self.q[eng].wait_ge(s, v); w[id(s)] = v

    def emit(self, eng, fn, reads=(), writes=(), dsem=None, is_out=False, serial=True):
        skip = None
        if eng == "pe":
            skip = id(self.sem["pe"])
        if dsem is not None and not serial:
            skip = id(dsem[0])

        def w8(ev):
            if ev is not None and id(ev[0]) != skip:
                self._wait(eng, ev)
        for r in reads:
            w8(r.lw)
        for w_ in writes:
            w8(w_.lw)
            for ev in w_.rd.values():
                w8(ev)
        if dsem is not None and serial:
            self._wait(eng, dsem[2])
        ins = fn(self.q[eng])
        if dsem is not None:
            if dsem[1] >= 30000:
                dsem[0] = self._alloc("dx"); dsem[1] = 0
            dsem[1] += 16
            ins.then_inc(dsem[0], 16)
            ev = (dsem[0], dsem[1]); dsem[2] = ev
            if is_out:
                self.out_evs.append(ev)
        else:
            if self.cnt[eng] >= 30000:
                self._newsem(eng)
            self.cnt[eng] += 1
            ins.then_inc(self.sem[eng], 1)
            ev = (self.sem[eng], self.cnt[eng])
        for r in reads:
            r.rd[id(ev[0])] = ev
        for w_ in writes:
            w_.lw = ev; w_.rd = {}
        return ev

    def barrier(self):
        evs = [(self.sem[k], self.cnt[k]) for k in self.q if self.cnt[k] > 0]
        evs += [d[2] for d in self.dsems if d[2] is not None]
        for k in self.q:
            for ev in evs:
                self._wait(k, ev)

    def finish(self):
        self.barrier()


class TD:
    def __init__(self, idx):
        self.idx = idx
        if idx < 16:
            self.nt = 128; self.nseq = 1; self.L = 128; self.tok0 = idx * 128; self.s0 = 0
        else:
            self.nt = 64; self.nseq = 16; self.L = 4; self.tok0 = 2048; self.s0 = 1
        self.prompt = idx < 16


class Kern:
    def __init__(self, nc, fw, es, dr):
        self.nc = nc; self.fw = fw; self.es = es; self.dr = dr
        self.pqi = 0

    def sb(self, es, name, shape, dt):
        self.nsb = getattr(self, "nsb", 0) + 1
        name = f"{name}_{self.nsb}"
        return Tl(es.enter_context(self.nc.sbuf_tensor(name, shape, dt)), name)

    def E(self, eng, fn, rd=(), wr=(), **kw):
        rs = [x if isinstance(x, Res) else x.r for x in rd]
        ws = [x if isinstance(x, Res) else x.r for x in wr]
        ws = ws + [r for r in rs if r.excl]
        rs = [r for r in rs if not r.excl]
        return self.fw.emit(eng, fn, reads=rs, writes=ws, **kw)

    def pq(self):
        q = self.pqs[self.pqi % 24]; self.pqi += 1
        return q

    def alt(self):
        self.alti = getattr(self, "alti", 0) + 1
        return "act" if self.alti % 2 else "dve"

    def evac(self, eng, out_ap, in_ap, rd, wr):
        if eng == "act":
            self.E("act", lambda e: e.copy(out_ap, in_ap), rd, wr)
        else:
            self.E(eng, lambda e: e.tensor_copy(out_ap, in_ap), rd, wr)

    def transpose_to(self, in_ap, npart, nfree, rd):
        q = self.pq()
        o = q.ap[0:nfree, 0:npart]
        self.E("pe", lambda e: e.transpose(o, in_ap, self.ident[0:npart, 0:npart]), list(rd) + [self.ident], [q])
        return q, o

    def load_fm(self, dram2d, C, col):
        st = self.vstg[self.vstg_i % 2]; d = self.vstg_d[self.vstg_i % 2]; self.vstg_i += 1
        self.E("sp", lambda e: e.dma_start(out=st[0:C, :], in_=dram2d), [], [st], dsem=d)
        q, o = self.transpose_to(st[0:C, :], C, 128, [st])
        self.evac(self.alt(), self.cv[:, col:col + C], o, [q], [self.cv])

    def load_w(self, dst, dram, KC, N, dsem):
        src = dram.rearrange("(k p) n -> p k n", p=128)
        for k in range(KC):
            for n0 in range(0, N, 1024):
                n1 = min(N, n0 + 1024)
                self.E("pool", lambda e, k=k, n0=n0, n1=n1: e.dma_start(out=dst[:, k, n0:n1], in_=src[:, k, n0:n1]),
                       [], [dst], dsem=dsem, serial=False)

    def modap(self, li, part, k, td):
        return self.modT[:, li, part * 8 + k, td.s0:td.s0 + td.nseq]

    def modulate(self, out_ap, in_ap, td, sc, sh, rd, wr):
        if td.nseq == 1:
            self.E("act", lambda e: e.activation(out_ap, in_ap, AF.Identity,
                                                 bias=(sh if sh is not None else 0.0),
                                                 scale=(sc if sc is not None else 1.0)), rd, wr)
        else:
            shape = [128, td.nseq, td.L]
            o3 = out_ap.rearrange("p (s l) -> p s l", l=td.L)
            i3 = in_ap.rearrange("p (s l) -> p s l", l=td.L)
            if sc is not None and sh is not None:
                t3 = self.mtmp[:, 0:td.nt].rearrange("p (s l) -> p s l", l=td.L)
                self.E("dve", lambda e: e.tensor_tensor(t3, i3, sc.unsqueeze(2).to_broadcast(shape), ALU.mult), rd, [self.mtmp])
                self.E("dve", lambda e: e.tensor_tensor(o3, t3, sh.unsqueeze(2).to_broadcast(shape), ALU.add), [self.mtmp], wr)
            elif sc is not None:
                self.E("dve", lambda e: e.tensor_tensor(o3, i3, sc.unsqueeze(2).to_broadcast(shape), ALU.mult), rd, wr)
            else:
                self.E("dve", lambda e: e.tensor_tensor(o3, i3, sh.unsqueeze(2).to_broadcast(shape), ALU.add), rd, wr)

    def xr(self, td, k):
        return self.xres[:, k, td.tok0:td.tok0 + td.nt]

    def resid_ln(self, td, li, gpart, srcs, lng_col, lnb_col):
        nt = td.nt; Tb = self.Tb; Sq = self.Sq
        q1 = self.pq(); q2 = self.pq()
        for j in range(8):
            ap, res = srcs[j]
            self.modulate(Tb[:, j, 0:nt], ap, td, self.modap(li, gpart, j, td), None, [res, self.modT], [Tb])
            self.E("dve", lambda e, j=j: e.scalar_tensor_tensor(out=Tb[:, j, 0:nt], in0=self.xr(td, j), scalar=ALPHA,
                                                                in1=Tb[:, j, 0:nt], op0=ALU.mult, op1=ALU.add),
                   [self.xrr[td.idx], Tb], [Tb])
            self.E("act", lambda e, j=j: e.activation(Sq[:, j, 0:nt], Tb[:, j, 0:nt], AF.Square), [Tb], [Sq])
        for j in range(8):
            self.E("pe", lambda e, j=j: e.matmul(q1.ap[:, 0:nt], lhsT=self.onesf[:, :], rhs=Tb[:, j, 0:nt], start=(j == 0), stop=(j == 7)),
                   [Tb, self.onesf], [q1])
        for j in range(8):
            self.E("pe", lambda e, j=j: e.matmul(q2.ap[:, 0:nt], lhsT=self.onesf[:, :], rhs=Sq[:, j, 0:nt], start=(j == 0), stop=(j == 7)),
                   [Sq, self.onesf], [q2])
        self.ln_finish(td, q1, q2, Tb, lng_col, lnb_col, lambda j: self.xr(td, j), [self.xrr[td.idx]], AF.Identity)

    def ln_finish(self, td, q1, q2, Tb, lng_col, lnb_col, outfn, outres, func):
        nt = td.nt; st = self.lnst
        self.evac("act", st[:, 0, 0:nt], q1.ap[:, 0:nt], [q1], [st])
        self.E("dve", lambda e: e.tensor_tensor(st[:, 2, 0:nt], st[:, 0, 0:nt], st[:, 0, 0:nt], ALU.mult), [st], [st])
        self.E("dve", lambda e: e.tensor_tensor(st[:, 1, 0:nt], q2.ap[:, 0:nt], st[:, 2, 0:nt], ALU.subtract), [q2, st], [st])
        self.E("act", lambda e: e.activation(st[:, 1, 0:nt], st[:, 1, 0:nt], AF.Sqrt, bias=self.epsb[:, 0:1], scale=1.0), [st, self.epsb], [st])
        self.E("dve", lambda e: e.reciprocal(st[:, 1, 0:nt], st[:, 1, 0:nt]), [st], [st])
        for j in range(8):
            self.E("dve", lambda e, j=j: e.tensor_tensor(Tb[:, j, 0:nt], Tb[:, j, 0:nt], st[:, 0, 0:nt], ALU.subtract), [Tb, st], [Tb])
            self.E("dve", lambda e, j=j: e.tensor_tensor(Tb[:, j, 0:nt], Tb[:, j, 0:nt], st[:, 1, 0:nt], ALU.mult), [Tb, st], [Tb])
            self.E("act", lambda e, j=j: e.activation(outfn(j), Tb[:, j, 0:nt], func, bias=self.cv[:, lnb_col + j:lnb_col + j + 1],
                                                      scale=self.cv[:, lng_col + j:lng_col + j + 1]), [Tb, self.cv], outres)

    def state_out(self, stg, ncols, grp, dram_rows, rd):
        for g in range(ncols // grp):
            q, o = self.transpose_to(stg[:, g * grp:(g + 1) * grp], 128, grp, rd)
            ost = self.ost[g % 2]
            self.evac(self.alt(), ost[0:grp, :], o, [q], [ost])
            self.E("sp", lambda e, g=g, ost=ost: e.dma_start(out=dram_rows[g * grp:(g + 1) * grp, :], in_=ost[0:grp, :]),
                   [ost], [], dsem=self.dout[g % 2], is_out=True)

    def run(self):
        nc = self.nc; es = self.es; dr = self.dr; E = self.E
        self.pbanks = [es.enter_context(nc.psum_tensor(f"pb{b}", [128, 512], F32)) for b in range(8)]
        self.bres = [Res(f"bank{b}", excl=True) for b in range(8)]
        self.pqs = [PQ(self.pbanks[i % 6][:, (i // 6) * 128:(i // 6 + 1) * 128], self.bres[i % 6]) for i in range(24)]
        self.accq = [PQ(self.pbanks[6 + i][:, :], self.bres[6 + i]) for i in range(2)]
        self.xres = self.sb(es, "xres", [128, 8, NTOK], F32)
        self.xrr = [Res(f"xr{t}") for t in range(NTILES)]
        self.ident = self.sb(es, "ident", [128, 128], F32)
        self.onesf = self.sb(es, "onesf", [128, 128], F32)
        self.iota16 = self.sb(es, "iota16", [128, 16], F32)
        self.epsb = self.sb(es, "epsb", [128, 1], F32)
        self.cv = self.sb(es, "cv", [128, 544], F32)
        self.modT = self.sb(es, "modT", [128, 2, 48, 18], F32)
        self.mtmp = self.sb(es, "mtmp", [128, 128], F32)
        self.Tb = self.sb(es, "Tb", [128, 8, 128], F32)
        self.Sq = self.sb(es, "Sq", [128, 8, 128], F32)
        self.lnst = self.sb(es, "lnst", [128, 3, 128], F32)
        self.ost = [self.sb(es, f"ost{i}", [128, 128], F32) for i in range(2)]
        self.vstg = [self.sb(es, f"vstg{i}", [128, 128], F32) for i in range(2)]
        self.vstg_d = [self.fw.dsem(f"vs{i}") for i in range(2)]; self.vstg_i = 0
        self.dout = [self.fw.dsem(f"out{i}") for i in range(2)]
        self.tds = [TD(i) for i in range(NTILES)]
        wk = self.mtmp
        E("pool", lambda e: e.iota(wk[:], pattern=[[1, 128]], base=0, channel_multiplier=-1, allow_small_or_imprecise_dtypes=True), [], [wk])
        E("dve", lambda e: e.tensor_single_scalar(self.ident[:], wk[:], 0.0, ALU.is_equal), [wk], [self.ident])
        E("dve", lambda e: e.memset(self.onesf[:], 1.0 / 1024.0), [], [self.onesf])
        E("dve", lambda e: e.memset(self.epsb[:], EPS), [], [self.epsb])
        E("pool", lambda e: e.iota(self.iota16[:], pattern=[[1, 16]], base=0, channel_multiplier=0, allow_small_or_imprecise_dtypes=True), [], [self.iota16])

        if STAGE < -3: return
        col = {}
        cur = [0]

        def vec(name, dram2d, C):
            col[name] = cur[0]
            for r0 in range(0, C, 128):
                r1 = min(C, r0 + 128)
                self.load_fm(dram2d[r0:r1, :], r1 - r0, cur[0] + r0)
            cur[0] += C
        v2 = lambda a: a.rearrange("(c p) -> c p", p=128)
        for li in range(2):
            vec(f"bada{li}", v2(dr["b_ada"][li]), 48)
            vec(f"lnmg{li}", v2(dr["ln_mix_g"][li]), 8); vec(f"lnmb{li}", v2(dr["ln_mix_b"][li]), 8)
            vec(f"lnfg{li}", v2(dr["ln_ffn_g"][li]), 8); vec(f"lnfb{li}", v2(dr["ln_ffn_b"][li]), 8)
        vec("lcw", dr["lru_conv_w"][0].rearrange("k (c p) -> (k c) p", p=128), 40)
        vec("lcb", v2(dr["lru_conv_b"][0]), 10); vec("lba", v2(dr["lru_b_a"][0]), 10)
        vec("lbi", v2(dr["lru_b_i"][0]), 10); vec("lam", v2(dr["lru_lambda"][0]), 10)
        vec("cb1", v2(dr["conf_b_pw1"][0]), 16)
        vec("cdw", dr["conf_dw_w"][0].rearrange("k (c p) -> (k c) p", p=128), 248)
        vec("cdb", v2(dr["conf_dw_b"][0]), 8); vec("clg", v2(dr["conf_ln_g"][0]), 8)
        vec("clb", v2(dr["conf_ln_b"][0]), 8); vec("cb2", v2(dr["conf_b_pw2"][0]), 8)
        self.col = col
        cv = self.cv
        for li in range(2):
            for part in (1, 4):
                c0 = col[f"bada{li}"] + part * 8
                E("dve", lambda e, c0=c0: e.tensor_scalar_add(cv[:, c0:c0 + 8], cv[:, c0:c0 + 8], 1.0), [cv], [cv])
        cl = col["lam"]
        E("act", lambda e: e.activation(cv[:, cl:cl + 10], cv[:, cl:cl + 10], AF.Exp, scale=-1.0), [cv], [cv])
        E("act", lambda e: e.activation(cv[:, cl:cl + 10], cv[:, cl:cl + 10], AF.Ln, bias=1.0, scale=1.0), [cv], [cv])
        E("dve", lambda e: e.tensor_scalar_mul(cv[:, cl:cl + 10], cv[:, cl:cl + 10], -8.0), [cv], [cv])

        if STAGE < -2: return
        with ExitStack() as pes:
            csb = self.sb(pes, "csb", [18, 1024], F32)
            E("dve", lambda e: e.memset(csb[:, :], 0.0), [], [csb])
            cT = self.sb(pes, "cT", [128, 8, 18], BF16)
            wab = [self.sb(pes, f"wab{i}", [128, 8, 512], BF16) for i in range(2)]
            dwa = [self.fw.dsem(f"wa{i}") for i in range(2)]
            dc = self.fw.dsem("c")
            E("sp", lambda e: e.dma_start(out=csb[0:17, :], in_=dr["cc"]), [], [csb], dsem=dc)
            E("act", lambda e: e.activation(csb[:, :], csb[:, :], AF.Silu), [csb], [csb])
            for k in range(8):
                q, o = self.transpose_to(csb[0:18, k * 128:(k + 1) * 128], 18, 128, [csb])
                self.evac(self.alt(), cT[:, k, :], o[:, 0:18], [q], [cT])
            it = 0
            for li in range(2):
                if STAGE < -1.7: break
                for n0 in range(0, 6144, 512):
                    if it >= NCH: break
                    if STAGE < -1.3 and it >= 1: break
                    if STAGE < -1.15 and it >= 2: break
                    if STAGE < -1.05 and it >= 3: break
                    wb = wab[it % 2]; dd = dwa[it % 2]; it += 1
                    src = dr["w_ada"][li][:, n0:n0 + 512].rearrange("(k p) n -> p k n", p=128)
                    for k in range(8):
                        E("pool", lambda e, wb=wb, src=src, k=k: e.dma_start(out=wb[:, k, :], in_=src[:, k, :]), [], [wb], dsem=dd, serial=(k == 0))
                    for s in range(4):
                        if STAGE < -1.5: break
                        mc = n0 // 128 + s
                        q = self.pq()
                        for k in range(8):
                            E("pe", lambda e, k=k, s=s, wb=wb, q=q: e.matmul(q.ap[:, 0:18], lhsT=wb[:, k, s * 128:(s + 1) * 128], rhs=cT[:, k, :],
                                                                          start=(k == 0), stop=(k == 7)), [wb, cT], [q])
                        bc = col[f"bada{li}"] + mc
                        E("act", lambda e, q=q, li=li, mc=mc, bc=bc: e.activation(self.modT[:, li, mc, :], q.ap[:, 0:18], AF.Identity,
                                                                               bias=cv[:, bc:bc + 1], scale=1.0), [q, cv], [self.modT])
            self.fw.barrier()

        if STAGE < -1: return
        with ExitStack() as pes:
            xin = [self.sb(pes, f"xin{i}", [128, 1024], F32) for i in range(2)]
            dx = [self.fw.dsem(f"x{i}") for i in range(2)]
            for td in self.tds:
                xi = xin[td.idx % 2]
                src = dr["xp"][td.tok0:td.tok0 + 128, :] if td.prompt else dr["xs"]
                E("sp", lambda e, xi=xi, src=src, td=td: e.dma_start(out=xi[0:td.nt, :], in_=src), [], [xi], dsem=dx[td.idx % 2])
                for k in range(8):
                    q, o = self.transpose_to(xi[0:td.nt, k * 128:(k + 1) * 128], td.nt, 128, [xi])
                    self.evac(self.alt(), self.xr(td, k), o, [q], [self.xrr[td.idx]])
            self.fw.barrier()

        if STAGE < 1: return
        self.phase_m0()
        if STAGE < 2: return
        self.phase_peer(0)
        if STAGE < 3: return
        self.phase_m1()
        if STAGE < 4: return
        self.phase_peer(1)

    def phase_m0(self):
        E = self.E; dr = self.dr; col = self.col; cv = self.cv
        with ExitStack() as pes:
            w_in = self.sb(pes, "w_in", [128, 8, 2560], BF16)
            w_a = self.sb(pes, "w_a", [128, 10, 128], BF16)
            w_i = self.sb(pes, "w_i", [128, 10, 128], BF16)
            w_out = self.sb(pes, "w_out", [128, 10, 1024], BF16)
            dw = self.fw.dsem("wm0")
            self.load_w(w_in, dr["lru_w_in"][0], 8, 2560, dw)
            E("pool", lambda e: e.dma_start(out=w_a[:, :, :], in_=dr["lru_w_a"][0].rearrange("h i j -> i h j")), [], [w_a], dsem=dw, serial=False)
            E("pool", lambda e: e.dma_start(out=w_i[:, :, :], in_=dr["lru_w_i"][0].rearrange("h i j -> i h j")), [], [w_i], dsem=dw, serial=False)
            self.load_w(w_out, dr["lru_w_out"][0], 10, 1024, dw)
            hmT = self.sb(pes, "hmT", [128, 8, 128], BF16)
            xp_p = self.sb(pes, "xp_p", [128, 10, 1, 131], F32)
            xp_s = self.sb(pes, "xp_s", [128, 10, 16, 7], F32)
            hst_p = self.sb(pes, "hst_p", [128, 10], F32)
            hst_s = self.sb(pes, "hst_s", [128, 10, 16], F32)
            gg = self.sb(pes, "gg", [128, 10, 128], F32)
            yg = self.sb(pes, "yg", [128, 10, 128], BF16)
            nbuf = 2
            xc = [self.sb(pes, f"xc{i}", [128, 128], F32) for i in range(nbuf)]
            xcb = [self.sb(pes, f"xcb{i}", [128, 128], BF16) for i in range(nbuf)]
            rr = [self.sb(pes, f"rr{i}", [128, 128], F32) for i in range(nbuf)]
            ig = [self.sb(pes, f"ig{i}", [128, 128], F32) for i in range(nbuf)]
            aa = [self.sb(pes, f"aa{i}", [128, 128], F32) for i in range(nbuf)]
            a2 = [self.sb(pes, f"a2{i}", [128, 128], F32) for i in range(nbuf)]
            hh = [self.sb(pes, f"hh{i}", [128, 128], F32) for i in range(nbuf)]
            stg = self.sb(pes, "stg0", [128, 1280], F32)
            dst = self.fw.dsem("st0")
            E("dve", lambda e: e.memset(xp_p[:, :, :, :], 0.0), [], [xp_p])
            E("dve", lambda e: e.memset(hst_p[:, :], 0.0), [], [hst_p])
            E("sp", lambda e: e.dma_start(out=stg[0:16, :], in_=dr["slh"]), [], [stg], dsem=dst)
            for c in range(10):
                q, o = self.transpose_to(stg[0:16, c * 128:(c + 1) * 128], 16, 128, [stg])
                self.evac(self.alt(), hst_s[:, c, :], o, [q], [hst_s])
            E("sp", lambda e: e.dma_start(out=stg[0:48, :], in_=dr["slc"].rearrange("s k d -> (s k) d")), [], [stg], dsem=dst)
            for c in range(10):
                q, o = self.transpose_to(stg[0:48, c * 128:(c + 1) * 128], 48, 128, [stg])
                self.evac(self.alt(), xp_s[:, c, :, 0:3], o.rearrange("p (s k) -> p s k", k=3), [q], [xp_s])

            for td in self.tds:
                nt = td.nt; L = td.L; ns = td.nseq
                xp = xp_p if td.prompt else xp_s
                for k in range(8):
                    self.modulate(hmT[:, k, 0:nt], self.xr(td, k), td, self.modap(0, 1, k, td), self.modap(0, 0, k, td),
                                  [self.xrr[td.idx], self.modT], [hmT])
                for c in range(20):
                    q = self.pq()
                    for k in range(8):
                        E("pe", lambda e, k=k, c=c, q=q: e.matmul(q.ap[:, 0:nt], lhsT=w_in[:, k, c * 128:(c + 1) * 128], rhs=hmT[:, k, 0:nt],
                                                                 start=(k == 0), stop=(k == 7)), [w_in, hmT], [q])
                    if c < 10:
                        o3 = xp[:, c, :, 3:3 + L]
                        self.evac("dve", o3, q.ap[:, 0:nt].rearrange("p (s l) -> p s l", l=L), [q], [xp])
                    else:
                        E("act", lambda e, c=c, q=q: e.activation(gg[:, c - 10, 0:nt], q.ap[:, 0:nt], AF.Gelu), [q], [gg])
                for c in range(10):
                    b = c % nbuf
                    xc3 = xc[b][:, 0:nt].rearrange("p (s l) -> p s l", l=L)
                    lw = col["lcw"]
                    E("dve", lambda e, c=c, xc3=xc3: e.tensor_scalar(xc3, xp[:, c, :, 0:L], cv[:, lw + c:lw + c + 1],
                                                                     cv[:, col["lcb"] + c:col["lcb"] + c + 1], op0=ALU.mult, op1=ALU.add),
                      [xp, cv], [xc[b]])
                    for kk in range(1, 4):
                        E("dve", lambda e, c=c, kk=kk, xc3=xc3: e.scalar_tensor_tensor(out=xc3, in0=xp[:, c, :, kk:kk + L],
                                                                                        scalar=cv[:, lw + kk * 10 + c:lw + kk * 10 + c + 1],
                                                                                        in1=xc3, op0=ALU.mult, op1=ALU.add), [xp, cv, xc[b]], [xc[b]])
                    if td.prompt:
                        E("pool", lambda e, c=c: e.tensor_copy(xp[:, c, :, 0:3], xp[:, c, :, L:L + 3]), [xp], [xp])
                    E("act", lambda e, b=b: e.copy(xcb[b][:, 0:nt], xc[b][:, 0:nt]), [xc[b]], [xcb[b]])
                    qa = self.pq(); qi = self.pq()
                    E("pe", lambda e, c=c, b=b, qa=qa: e.matmul(qa.ap[:, 0:nt], lhsT=w_a[:, c, :], rhs=xcb[b][:, 0:nt], start=True, stop=True), [w_a, xcb[b]], [qa])
                    E("pe", lambda e, c=c, b=b, qi=qi: e.matmul(qi.ap[:, 0:nt], lhsT=w_i[:, c, :], rhs=xcb[b][:, 0:nt], start=True, stop=True), [w_i, xcb[b]], [qi])
                    E("act", lambda e, c=c, b=b, qa=qa: e.activation(rr[b][:, 0:nt], qa.ap[:, 0:nt], AF.Sigmoid,
                                                                     bias=cv[:, col["lba"] + c:col["lba"] + c + 1], scale=1.0), [qa, cv], [rr[b]])
                    E("act", lambda e, c=c, b=b, qi=qi: e.activation(ig[b][:, 0:nt], qi.ap[:, 0:nt], AF.Sigmoid,
                                                                     bias=cv[:, col["lbi"] + c:col["lbi"] + c + 1], scale=1.0), [qi, cv], [ig[b]])
                    E("act", lambda e, c=c, b=b: e.activation(aa[b][:, 0:nt], rr[b][:, 0:nt], AF.Exp,
                                                              scale=cv[:, col["lam"] + c:col["lam"] + c + 1]), [rr[b], cv], [aa[b]])
                    E("dve", lambda e, b=b: e.tensor_tensor(a2[b][:, 0:nt], aa[b][:, 0:nt], aa[b][:, 0:nt], ALU.mult), [aa[b]], [a2[b]])
                    E("act", lambda e, b=b: e.activation(a2[b][:, 0:nt], a2[b][:, 0:nt], AF.Sqrt, bias=self.onesb[:, 0:1], scale=-1.0), [a2[b], self.onesb], [a2[b]])
                    E("dve", lambda e, b=b: e.tensor_tensor(ig[b][:, 0:nt], ig[b][:, 0:nt], xc[b][:, 0:nt], ALU.mult), [ig[b], xc[b]], [ig[b]])
                    E("dve", lambda e, b=b: e.tensor_tensor(ig[b][:, 0:nt], ig[b][:, 0:nt], a2[b][:, 0:nt], ALU.mult), [ig[b], a2[b]], [ig[b]])
                    if td.prompt:
                        E("dve", lambda e, c=c, b=b: e.tensor_tensor_scan(hh[b][:, 0:nt], aa[b][:, 0:nt], ig[b][:, 0:nt], hst_p[:, c:c + 1],
                                                                           op0=ALU.mult, op1=ALU.add), [aa[b], ig[b], hst_p], [hh[b]])
                        E("dve", lambda e, c=c, b=b: e.tensor_copy(hst_p[:, c:c + 1], hh[b][:, nt - 1:nt]), [hh[b]], [hst_p])
                    else:
                        h3 = hh[b][:, 0:nt].rearrange("p (s l) -> p s l", l=L)
                        a3 = aa[b][:, 0:nt].rearrange("p (s l) -> p s l", l=L)
                        b3 = ig[b][:, 0:nt].rearrange("p (s l) -> p s l", l=L)
                        for t in range(L):
                            prev = hst_s[:, c, :] if t == 0 else h3[:, :, t - 1]
                            E("dve", lambda e, t=t, prev=prev, h3=h3, a3=a3: e.tensor_tensor(h3[:, :, t], a3[:, :, t], prev, ALU.mult),
                              [aa[b], hst_s, hh[b]], [hh[b]])
                            E("dve", lambda e, t=t, h3=h3, b3=b3: e.tensor_tensor(h3[:, :, t], h3[:, :, t], b3[:, :, t], ALU.add),
                              [ig[b], hh[b]], [hh[b]])
                        E("dve", lambda e, c=c, h3=h3: e.tensor_copy(hst_s[:, c, :], h3[:, :, L - 1]), [hh[b]], [hst_s])
                    E("dve", lambda e, c=c, b=b: e.tensor_tensor(yg[:, c, 0:nt], hh[b][:, 0:nt], gg[:, c, 0:nt], ALU.mult), [hh[b], gg], [yg])
                srcs = []
                for j in range(8):
                    q = self.pq()
                    for k in range(10):
                        E("pe", lambda e, k=k, j=j, q=q: e.matmul(q.ap[:, 0:nt], lhsT=w_out[:, k, j * 128:(j + 1) * 128], rhs=yg[:, k, 0:nt],
                                                                 start=(k == 0), stop=(k == 9)), [w_out, yg], [q])
                    srcs.append((q.ap[:, 0:nt], q.r))
                self.resid_ln(td, 0, 2, srcs, col["lnmg0"], col["lnmb0"])
                if td.idx == 15:
                    self.state_out(hst_p, 10, 10, self.dr["o_hp"], [hst_p])
                    E("dve", lambda e: e.tensor_copy(stg[:, 0:30].rearrange("p (k c) -> p k c", c=10),
                                                     xp_p[:, :, 0, 0:3].rearrange("p c k -> p k c")), [xp_p], [stg])
                    self.state_out(stg, 30, 30, self.dr["o_lcp"], [stg])
                if td.idx == 16:
                    E("dve", lambda e: e.tensor_copy(stg[:, 0:160].rearrange("p (s c) -> p s c", c=10),
                                                     hst_s[:, :, :].rearrange("p c s -> p s c")), [hst_s], [stg])
                    self.state_out(stg, 160, 80, self.dr["o_hs"], [stg])
                    for s in range(16):
                        E("dve", lambda e, s=s: e.tensor_copy(stg[:, 160 + s * 30:160 + (s + 1) * 30].rearrange("p (k c) -> p k c", c=10),
                                                              xp_s[:, :, s, 4:7].rearrange("p c k -> p k c")), [xp_s], [stg])
                    self.state_out(stg[:, 160:640], 480, 120, self.dr["o_lcs"], [stg])
            self.fw.barrier()

    def phase_m1(self):
        E = self.E; dr = self.dr; col = self.col; cv = self.cv
        with ExitStack() as pes:
            w1 = self.sb(pes, "w_pw1", [128, 8, 2048], BF16)
            w2 = self.sb(pes, "w_pw2", [128, 8, 1024], BF16)
            dw = self.fw.dsem("wm1")
            self.load_w(w1, dr["conf_w_pw1"][0], 8, 2048, dw)
            self.load_w(w2, dr["conf_w_pw2"][0], 8, 1024, dw)
            hmT = self.sb(pes, "hmT1", [128, 8, 128], BF16)
            sg = self.sb(pes, "sg", [128, 8, 128], F32)
            up_p = self.sb(pes, "up_p", [128, 8, 1, 158], F32)
            up_s = self.sb(pes, "up_s", [128, 8, 16, 34], F32)
            dd = self.sb(pes, "dd", [128, 8, 128], F32)
            dsT = self.sb(pes, "dsT", [128, 8, 128], BF16)
            osb = self.sb(pes, "osb", [128, 8, 128], F32)
            stg = self.sb(pes, "stg1", [128, 1024], F32)
            dst = self.fw.dsem("st1")
            E("dve", lambda e: e.memset(up_p[:, :, :, :], 0.0), [], [up_p])
            src = dr["scc"].rearrange("s k d -> (s k) d")
            for g in range(4):
                E("sp", lambda e, g=g: e.dma_start(out=stg[0:120, :], in_=src[g * 120:(g + 1) * 120, :]), [], [stg], dsem=dst)
                for c in range(8):
                    q, o = self.transpose_to(stg[0:120, c * 128:(c + 1) * 128], 120, 128, [stg])
                    self.evac(self.alt(), up_s[:, c, g * 4:(g + 1) * 4, 0:30], o.rearrange("p (s k) -> p s k", k=30), [q], [up_s])
            for td in self.tds:
                nt = td.nt; L = td.L
                up = up_p if td.prompt else up_s
                for k in range(8):
                    self.modulate(hmT[:, k, 0:nt], self.xr(td, k), td, self.modap(1, 1, k, td), self.modap(1, 0, k, td),
                                  [self.xrr[td.idx], self.modT], [hmT])
                for c in list(range(8, 16)) + list(range(8)):
                    q = self.pq()
                    for k in range(8):
                        E("pe", lambda e, k=k, c=c, q=q: e.matmul(q.ap[:, 0:nt], lhsT=w1[:, k, c * 128:(c + 1) * 128], rhs=hmT[:, k, 0:nt],
                                                                 start=(k == 0), stop=(k == 7)), [w1, hmT], [q])
                    bcol = col["cb1"] + c
                    if c >= 8:
                        E("act", lambda e, c=c, q=q, bcol=bcol: e.activation(sg[:, c - 8, 0:nt], q.ap[:, 0:nt], AF.Sigmoid,
                                                                            bias=cv[:, bcol:bcol + 1], scale=1.0), [q, cv], [sg])
                    else:
                        E("dve", lambda e, c=c, q=q, bcol=bcol: e.scalar_tensor_tensor(
                            out=up[:, c, :, 30:30 + L], in0=q.ap[:, 0:nt].rearrange("p (s l) -> p s l", l=L), scalar=cv[:, bcol:bcol + 1],
                            in1=sg[:, c, 0:nt].rearrange("p (s l) -> p s l", l=L), op0=ALU.add, op1=ALU.mult), [q, cv, sg], [up])
                for c in range(8):
                    d3 = dd[:, c, 0:nt].rearrange("p (s l) -> p s l", l=L)
                    cw = col["cdw"]
                    E("dve", lambda e, c=c, d3=d3: e.tensor_scalar(d3, up[:, c, :, 0:L], cv[:, cw + c:cw + c + 1],
                                                                   cv[:, col["cdb"] + c:col["cdb"] + c + 1], op0=ALU.mult, op1=ALU.add), [up, cv], [dd])
                    for kk in range(1, 31):
                        E("dve", lambda e, c=c, kk=kk, d3=d3: e.scalar_tensor_tensor(out=d3, in0=up[:, c, :, kk:kk + L],
                                                                                      scalar=cv[:, cw + kk * 8 + c:cw + kk * 8 + c + 1],
                                                                                      in1=d3, op0=ALU.mult, op1=ALU.add), [up, cv, dd], [dd])
                    if td.prompt:
                        E("pool", lambda e, c=c: e.tensor_copy(up[:, c, :, 0:30], up[:, c, :, L:L + 30]), [up], [up])
                Sq = self.Sq
                q1 = self.pq(); q2 = self.pq()
                for j in range(8):
                    E("act", lambda e, j=j: e.activation(Sq[:, j, 0:nt], dd[:, j, 0:nt], AF.Square), [dd], [Sq])
                for j in range(8):
                    E("pe", lambda e, j=j: e.matmul(q1.ap[:, 0:nt], lhsT=self.onesf[:, :], rhs=dd[:, j, 0:nt], start=(j == 0), stop=(j == 7)), [dd, self.onesf], [q1])
                for j in range(8):
                    E("pe", lambda e, j=j: e.matmul(q2.ap[:, 0:nt], lhsT=self.onesf[:, :], rhs=Sq[:, j, 0:nt], start=(j == 0), stop=(j == 7)), [Sq, self.onesf], [q2])
                self.ln_finish(td, q1, q2, dd, col["clg"], col["clb"], lambda j: dsT[:, j, 0:nt], [dsT], AF.Silu)
                srcs = []
                for j in range(8):
                    q = self.pq()
                    for k in range(8):
                        E("pe", lambda e, k=k, j=j, q=q: e.matmul(q.ap[:, 0:nt], lhsT=w2[:, k, j * 128:(j + 1) * 128], rhs=dsT[:, k, 0:nt],
                                                                 start=(k == 0), stop=(k == 7)), [w2, dsT], [q])
                    bcol = col["cb2"] + j
                    E("act", lambda e, j=j, q=q, bcol=bcol: e.activation(osb[:, j, 0:nt], q.ap[:, 0:nt], AF.Identity, bias=cv[:, bcol:bcol + 1], scale=1.0),
                      [q, cv], [osb])
                    srcs.append((osb[:, j, 0:nt], osb.r))
                self.resid_ln(td, 1, 2, srcs, col["lnmg1"], col["lnmb1"])
                if td.idx == 15:
                    E("dve", lambda e: e.tensor_copy(stg[:, 0:240].rearrange("p (k c) -> p k c", c=8),
                                                     up_p[:, :, 0, 0:30].rearrange("p c k -> p k c")), [up_p], [stg])
                    self.state_out(stg, 240, 120, self.dr["o_ccp"], [stg])
                if td.idx == 16:
                    for s in range(16):
                        E("dve", lambda e, s=s: e.tensor_copy(stg[:, 0:240].rearrange("p (k c) -> p k c", c=8),
                                                              up_s[:, :, s, 4:34].rearrange("p c k -> p k c")), [up_s], [stg])
                        self.state_out(stg, 240, 120, self.dr["o_ccs"][s * 240:(s + 1) * 240, :], [stg])
            self.fw.barrier()

    def phase_peer(self, li):
        E = self.E; dr = self.dr; col = self.col; cv = self.cv
        last = (li == 1)
        with ExitStack() as pes:
            wq = self.sb(pes, "wq", [128, 8, 2048], BF16)
            keysT = self.sb(pes, "keysT", [128, 16, 128], BF16)
            dw = self.fw.dsem(f"wq{li}")
            self.load_w(wq, dr["peer_w_q"][li], 8, 2048, dw)
            hrot = [self.sb(pes, f"hrot{i}", [128, 128], F32) for i in range(2)]
            hfT = self.sb(pes, "hfT", [128, 8, 128], BF16)
            hf = [self.sb(pes, f"hf{i}", [128, 1024], F32) for i in range(2)]
            qTc = [self.sb(pes, f"qTc{i}", [128, 128], BF16) for i in range(2)]
            S = self.sb(pes, "S", [128, 8, 2, 128], F32)
            Sflat = S.t[:, :, :, :].rearrange("p a b c -> p (a b c)")
            S2 = self.sb(pes, "S2", [128, 128], F32)
            vv = self.sb(pes, "vv", [128, 8, 2, 16], F32)
            iu = self.sb(pes, "iu", [128, 8, 2, 16], U32)
            iff = self.sb(pes, "iff", [128, 8, 2, 16], F32)
            cand = Tl(Sflat[:, 0:1024].rearrange("p (h i j) -> p h i j", h=4, i=16), "x"); cand.r = S.r
            cand2 = Tl(self.Sq.t[:, :, :].rearrange("p a b -> p (a b)").rearrange("p (h i j) -> p h i j", h=4, i=16), "x"); cand2.r = self.Sq.r
            yst = Tl(Sflat[:, 1024:2048], "x"); yst.r = S.r
            ts = self.sb(pes, "ts", [128, 8, 16], F32)
            pu = self.sb(pes, "pu", [128, 8, 16], U32)
            p1u = self.sb(pes, "p1u", [128, 8, 16], U32)
            p1f = self.sb(pes, "p1f", [128, 8, 16], F32)
            p2f = self.sb(pes, "p2f", [128, 8, 16], F32)
            gte = [self.sb(pes, f"gte{i}", [128, 8, 16], F32) for i in range(2)]
            ssum = self.sb(pes, "ssum", [128, 8], F32)
            e1 = self.sb(pes, "e1", [128, 128], F32)
            e2 = self.sb(pes, "e2", [128, 128], F32)
            idx = [self.sb(pes, f"idx{i}", [128, 128], I32) for i in range(2)]
            dots = self.sb(pes, "dots", [128, 128], F32)
            wgt = self.sb(pes, "wgt", [128, 128], F32)
            gel = self.sb(pes, "gel", [128, 128], F32)
            gb = [self.sb(pes, f"gb{i}", [128, 2048], F32) for i in range(NB)]
            dg = [self.fw.dsem(f"g{li}_{i}") for i in range(NB)]
            junk = self.sb(pes, "junk", [128, 1024], BF16)
            acc = Tl(self.Tb.t[:, :, :].rearrange("p a b -> p (a b)"), "x"); acc.r = self.Tb.r
            dgt = [self.sb(pes, f"dgt{i}", [128, 128], F32) for i in range(3)]
            dy = self.fw.dsem(f"y{li}")
            dk = self.fw.dsem(f"k{li}")
            kst = S2
            for b in range(2):
                E("pool", lambda e, b=b: e.memset(idx[b][:, :], 0), [], [idx[b]])
                E("dve", lambda e, b=b: e.memset(hf[b][:, :], 0.0), [], [hf[b]])
                E("dve", lambda e, b=b: e.memset(gte[b][:, :, :], 0.0), [], [gte[b]])
            ksrc = dr["peer_keys"][li].rearrange("h p n d -> (h p n) d")
            for c in range(16):
                E("sp", lambda e, c=c: e.dma_start(out=kst[:, :], in_=ksrc[c * 128:(c + 1) * 128, :]), [], [kst], dsem=dk)
                q, o = self.transpose_to(kst[:, :], 128, 128, [kst])
                self.evac(self.alt(), keysT[:, c, :], o, [q], [keysT])
            uvtab = dr["peer_uv"].rearrange("l e d -> (l e) d")

            def R(td, b):
                nt = td.nt; hfb = hf[b]; gt = gte[b]
                for k in range(8):
                    h32 = hrot[k % 2]
                    self.modulate(h32[:, 0:nt], self.xr(td, k), td, self.modap(li, 4, k, td), self.modap(li, 3, k, td),
                                  [self.xrr[td.idx], self.modT], [h32])
                    E("act", lambda e, k=k, h32=h32: e.copy(hfT[:, k, 0:nt], h32[:, 0:nt]), [h32], [hfT])
                    q, o = self.transpose_to(h32[:, 0:nt], 128, nt, [h32])
                    self.evac("dve", hfb[0:nt, k * 128:(k + 1) * 128], o, [q], [hfb])
                    yield
                for c in range(16):
                    q = self.pq(); qc = qTc[c % 2]
                    for k in range(8):
                        E("pe", lambda e, k=k, c=c, q=q: e.matmul(q.ap[:, 0:nt], lhsT=wq[:, k, c * 128:(c + 1) * 128], rhs=hfT[:, k, 0:nt],
                                                                 start=(k == 0), stop=(k == 7)), [wq, hfT], [q])
                    self.evac("act", qc[:, 0:nt], q.ap[:, 0:nt], [q], [qc])
                    q2 = self.pq()
                    E("pe", lambda e, c=c, q2=q2, qc=qc: e.matmul(q2.ap[0:nt, :], lhsT=qc[:, 0:nt], rhs=keysT[:, c, :], start=True, stop=True), [qc, keysT], [q2])
                    self.evac("act", S[0:nt, c // 2, c % 2, :], q2.ap[0:nt, :], [q2], [S])
                    yield
                for c in range(16):
                    h, p = c // 2, c % 2
                    sc_ = S[0:nt, h, p, :]
                    E("dve", lambda e, h=h, p=p, sc_=sc_: e.max(out=vv[0:nt, h, p, 0:8], in_=sc_), [S], [vv])
                    E("dve", lambda e, h=h, p=p, sc_=sc_: e.max_index(iu[0:nt, h, p, 0:8], vv[0:nt, h, p, 0:8], sc_), [S, vv], [iu])
                    E("dve", lambda e, h=h, p=p, sc_=sc_: e.match_replace(out=S2[0:nt, :], in_to_replace=vv[0:nt, h, p, 0:8], in_values=sc_, imm_value=-1e30), [S, vv], [S2])
                    yield
                    E("dve", lambda e, h=h, p=p: e.max(out=vv[0:nt, h, p, 8:16], in_=S2[0:nt, :]), [S2], [vv])
                    E("dve", lambda e, h=h, p=p: e.max_index(iu[0:nt, h, p, 8:16], vv[0:nt, h, p, 8:16], S2[0:nt, :]), [S2, vv], [iu])
                    yield
                E("dve", lambda e: e.tensor_copy(iff[0:nt], iu[0:nt]), [iu], [iff])
                for hh_ in range(2):
                    h0 = hh_ * 4
                    E("dve", lambda e, h0=h0: e.tensor_tensor(cand[0:nt], vv[0:nt, h0:h0 + 4, 0, :].unsqueeze(3).to_broadcast([nt, 4, 16, 16]),
                                                              vv[0:nt, h0:h0 + 4, 1, :].unsqueeze(2).to_broadcast([nt, 4, 16, 16]), ALU.add), [vv], [cand])
                    yield
                    for hl in range(4):
                        h = h0 + hl
                        cf = cand[0:nt, hl].rearrange("p a b -> p (a b)")
                        c2 = cand2[0:nt, hl].rearrange("p a b -> p (a b)")
                        E("dve", lambda e, h=h, cf=cf: e.max(out=ts[0:nt, h, 0:8], in_=cf), [cand], [ts])
                        E("dve", lambda e, h=h, cf=cf: e.max_index(pu[0:nt, h, 0:8], ts[0:nt, h, 0:8], cf), [cand, ts], [pu])
                        E("dve", lambda e, h=h, cf=cf, c2=c2: e.match_replace(out=c2, in_to_replace=ts[0:nt, h, 0:8], in_values=cf, imm_value=-1e30), [cand, ts], [cand2])
                        yield
                        E("dve", lambda e, h=h, c2=c2: e.max(out=ts[0:nt, h, 8:16], in_=c2), [cand2], [ts])
                        E("dve", lambda e, h=h, c2=c2: e.max_index(pu[0:nt, h, 8:16], ts[0:nt, h, 8:16], c2), [cand2, ts], [pu])
                        yield
                E("dve", lambda e: e.tensor_tensor(gt[0:nt], ts[0:nt], ts[0:nt, :, 0:1].to_broadcast([nt, 8, 16]), ALU.subtract), [ts], [gt])
                E("act", lambda e: e.activation(gt[0:nt], gt[0:nt], AF.Exp), [gt], [gt])
                E("dve", lambda e: e.tensor_reduce(ssum[0:nt], gt[0:nt], axis=AX.X, op=ALU.add), [gt], [ssum])
                E("dve", lambda e: e.reciprocal(ssum[0:nt], ssum[0:nt]), [ssum], [ssum])
                E("dve", lambda e: e.tensor_tensor(gt[0:nt], gt[0:nt], ssum[0:nt].unsqueeze(2).to_broadcast([nt, 8, 16]), ALU.mult), [gt, ssum], [gt])
                yield
                E("dve", lambda e: e.tensor_single_scalar(p1u[0:nt], pu[0:nt], 4, ALU.logical_shift_right), [pu], [p1u])
                E("dve", lambda e: e.tensor_copy(p1f[0:nt], p1u[0:nt]), [p1u], [p1f])
                E("dve", lambda e: e.tensor_single_scalar(p1u[0:nt], pu[0:nt], 15, ALU.bitwise_and), [pu, p1f], [p1u])
                E("dve", lambda e: e.tensor_copy(p2f[0:nt], p1u[0:nt]), [p1u], [p2f])
                yield
                for hh_ in range(2):
                    h0 = hh_ * 4
                    for (pf, half, eo) in ((p1f, 0, e1), (p2f, 1, e2)):
                        oh = cand[0:nt].rearrange("p a b c -> p (a b) c")
                        E("dve", lambda e, pf=pf, h0=h0, oh=oh: e.tensor_tensor(
                            oh, pf[0:nt, h0:h0 + 4, :].rearrange("p a b -> p (a b)").unsqueeze(2).to_broadcast([nt, 64, 16]),
                            self.iota16[0:nt, :].unsqueeze(1).to_broadcast([nt, 64, 16]), ALU.is_equal), [pf, self.iota16], [cand])
                        E("dve", lambda e, half=half, h0=h0: e.tensor_tensor(
                            cand[0:nt], cand[0:nt], iff[0:nt, h0:h0 + 4, half, :].unsqueeze(2).to_broadcast([nt, 4, 16, 16]), ALU.mult), [cand, iff], [cand])
                        E("dve", lambda e, eo=eo, h0=h0, oh=oh: e.tensor_reduce(eo[0:nt, h0 * 16:(h0 + 4) * 16], oh, axis=AX.X, op=ALU.add), [cand], [eo])
                        yield
                E("dve", lambda e: e.scalar_tensor_tensor(out=e1[0:nt, :], in0=e1[0:nt, :], scalar=128.0, in1=e2[0:nt, :], op0=ALU.mult, op1=ALU.add), [e1, e2], [e1])
                if li > 0:
                    E("dve", lambda e: e.tensor_scalar_add(e1[0:nt, :], e1[0:nt, :], float(li * 16384)), [e1], [e1])
                E("dve", lambda e: e.tensor_copy(idx[b][0:nt, :], e1[0:nt, :]), [e1], [idx[b]])
                yield

            def G(td, b, rgen):
                nt = td.nt; hfb = hf[b]; ib = idx[b]
                gflat = gte[b][:, :, :].rearrange("p a b -> p (a b)")
                for j in range(128):
                    s = j % NB
                    E("pool", lambda e, j=j, s=s: e.indirect_dma_start(out=gb[s][:, :], out_offset=None, in_=uvtab,
                                                                        in_offset=bass.IndirectOffsetOnAxis(ap=ib[:, j:j + 1], axis=0)),
                      [ib], [gb[s]], dsem=dg[s])
                    E("dve", lambda e, j=j, s=s: e.scalar_tensor_tensor(out=junk[:, :], in0=hfb[:, :], scalar=1.0, in1=gb[s][:, 0:1024],
                                                                         op0=ALU.mult, op1=ALU.mult, accum_out=dots[:, j:j + 1]),
                      [hfb, gb[s]], [junk, dots])
                    E("act", lambda e, j=j: e.activation(gel[0:nt, j:j + 1], dots[0:nt, j:j + 1], AF.Gelu), [dots], [gel])
                    E("act", lambda e, j=j: e.activation(wgt[0:nt, j:j + 1], gel[0:nt, j:j + 1], AF.Identity, scale=gflat[0:nt, j:j + 1]), [gel, gte[b]], [wgt])
                    dt_ = dgt[j % 3]
                    E("act", lambda e, j=j, dt_=dt_: e.activation(dt_[0:nt, 0:nt], self.ident[0:nt, 0:nt], AF.Identity, scale=wgt[0:nt, j:j + 1]),
                      [wgt, self.ident], [dt_])
                    for hv in range(2):
                        aq = self.accq[hv]
                        E("pe", lambda e, j=j, s=s, hv=hv, aq=aq, dt_=dt_: e.matmul(aq.ap[0:nt, :], lhsT=dt_[0:nt, 0:nt],
                                                                                   rhs=gb[s][0:nt, 1024 + hv * 512:1536 + hv * 512],
                                                                                   start=(j == 0), stop=(j == 127)), [dt_, gb[s]], [aq])
                    if rgen is not None and j >= 4:
                        next(rgen, None)
                if rgen is not None:
                    for _ in rgen:
                        pass

            def Fin(td):
                nt = td.nt
                for hv in range(2):
                    self.evac("act" if hv else "dve", acc[0:nt, hv * 512:(hv + 1) * 512], self.accq[hv].ap[0:nt, :], [self.accq[hv]], [acc])
                srcs = []
                for k in range(8):
                    q, o = self.transpose_to(acc[0:nt, k * 128:(k + 1) * 128], nt, 128, [acc])
                    srcs.append((o, q.r))
                self.resid_ln(td, li, 5, srcs, col[f"lnfg{li}"], col[f"lnfb{li}"])
                if last:
                    for k in range(8):
                        q, o = self.transpose_to(self.xr(td, k), 128, nt, [self.xrr[td.idx]])
                        self.evac(self.alt(), yst[0:nt, k * 128:(k + 1) * 128], o, [q], [yst])
                    dst_ = dr["o_yp"][td.tok0:td.tok0 + 128, :] if td.prompt else dr["o_ys"]
                    E("sp", lambda e, dst_=dst_: e.dma_start(out=dst_, in_=yst[0:nt, :]), [yst], [], dsem=dy, is_out=True)

            for _ in R(self.tds[0], 0):
                pass
            for td in self.tds:
                b = td.idx % 2
                rgen = R(self.tds[td.idx + 1], 1 - b) if td.idx + 1 < NTILES else None
                G(td, b, rgen)
                Fin(td)
            self.fw.barrier()


def build_nc():
    nc = bass.Bass("TRN2", target_bir_lowering=False)
    dr = {}

    def din(name, shape, dt=F32):
        dr[name] = nc.dram_tensor(name, list(shape), dt, kind="ExternalInput").ap()

    def dout(name, shape):
        dr[name] = nc.dram_tensor(name, list(shape), F32, kind="ExternalOutput").ap()
    din("xp", [2048, 1024]); din("xs", [64, 1024]); din("slh", [16, 1280]); din("slc", [16, 3, 1280])
    din("scc", [16, 30, 1024]); din("cc", [17, 1024])
    din("w_ada", [2, 1024, 6144]); din("b_ada", [2, 6144])
    for n in ("ln_mix_g", "ln_mix_b", "ln_ffn_g", "ln_ffn_b"):
        din(n, [2, 1024])
    din("lru_w_in", [1, 1024, 2560]); din("lru_conv_w", [1, 4, 1280]); din("lru_conv_b", [1, 1280])
    din("lru_w_a", [1, 10, 128, 128]); din("lru_b_a", [1, 1280]); din("lru_w_i", [1, 10, 128, 128]); din("lru_b_i", [1, 1280])
    din("lru_lambda", [1, 1280]); din("lru_w_out", [1, 1280, 1024])
    din("conf_w_pw1", [1, 1024, 2048]); din("conf_b_pw1", [1, 2048]); din("conf_dw_w", [1, 31, 1024]); din("conf_dw_b", [1, 1024])
    din("conf_ln_g", [1, 1024]); din("conf_ln_b", [1, 1024]); din("conf_w_pw2", [1, 1024, 1024]); din("conf_b_pw2", [1, 1024])
    din("peer_w_q", [2, 1024, 2048]); din("peer_keys", [2, 8, 2, 128, 128]); din("peer_uv", [2, 16384, 2048])
    dout("o_yp", [2048, 1024]); dout("o_ys", [64, 1024]); dout("o_hp", [10, 128]); dout("o_lcp", [30, 128]); dout("o_ccp", [240, 128])
    dout("o_hs", [160, 128]); dout("o_lcs", [480, 128]); dout("o_ccs", [3840, 128])
    with ExitStack() as es:
        fw = FW(nc, es)
        k = Kern(nc, fw, es, dr)
        k.onesb = k.sb(es, "onesb", [128, 1], F32)
        k.E("dve", lambda e: e.memset(k.onesb[:, :], 1.0), [], [k.onesb])
        k.run()
        fw.finish()
    return nc


_WNAMES = ["w_ada", "b_ada", "ln_mix_g", "ln_mix_b", "ln_ffn_g", "ln_ffn_b", "lru_w_in", "lru_conv_w", "lru_conv_b", "lru_w_a", "lru_b_a",
           "lru_w_i", "lru_b_i", "lru_lambda", "lru_w_out", "conf_w_pw1", "conf_b_pw1", "conf_dw_w", "conf_dw_b", "conf_ln_g", "conf_ln_b",
           "conf_w_pw2", "conf_b_pw2", "peer_w_q", "peer_keys"]


def kernel(**inp):
    f = lambda a: np.ascontiguousarray(np.asarray(a, dtype=np.float32))
    W = {n: f(inp[n]) for n in _WNAMES}
    W["peer_uv"] = np.ascontiguousarray(np.concatenate([f(inp["peer_u"]), f(inp["peer_v"])], axis=2))
    xp = f(inp["x_prompt"]); xs = f(inp["x_sample"])
    slh = f(inp["state_lru_h"]); slc = f(inp["state_lru_conv"]); scc = f(inp["state_conf_conv"])
    cp = f(inp["c_prompt"]); cs = f(inp["c_sample"])
    in_maps = []
    for i in range(NCORES):
        m = dict(W)
        sl = slice(16 * i, 16 * i + 16)
        m["xp"] = xp[i]; m["xs"] = np.ascontiguousarray(xs[sl].reshape(64, 1024))
        m["slh"] = np.ascontiguousarray(slh[0, sl]); m["slc"] = np.ascontiguousarray(slc[0, sl]); m["scc"] = np.ascontiguousarray(scc[0, sl])
        m["cc"] = np.ascontiguousarray(np.concatenate([cp[i:i + 1], cs[sl]], 0))
        in_maps.append(m)
    nc = build_nc()
    res = run_bass_kernel_spmd(nc, in_maps, core_ids=list(range(NCORES)))
    R = res.results
    y_p = np.stack([R[i]["o_yp"] for i in range(NCORES)], 0).astype(np.float32)
    y_s = np.concatenate([R[i]["o_ys"].reshape(16, 4, 1024) for i in range(NCORES)], 0).astype(np.float32)
    h_p = np.stack([R[i]["o_hp"].reshape(1280) for i in range(NCORES)], 0)[None].astype(np.float32)
    lc_p = np.stack([R[i]["o_lcp"].reshape(3, 1280) for i in range(NCORES)], 0)[None].astype(np.float32)
    cc_p = np.stack([R[i]["o_ccp"].reshape(30, 1024) for i in range(NCORES)], 0)[None].astype(np.float32)
    h_s = np.concatenate([R[i]["o_hs"].reshape(16, 1280) for i in range(NCORES)], 0)[None].astype(np.float32)
    lc_s = np.concatenate([R[i]["o_lcs"].reshape(16, 3, 1280) for i in range(NCORES)], 0)[None].astype(np.float32)
    cc_s = np.concatenate([R[i]["o_ccs"].reshape(16, 30, 1024) for i in range(NCORES)], 0)[None].astype(np.float32)
    return (y_p, y_s, h_p, lc_p, cc_p, h_s, lc_s, cc_s)
```

```python
import numpy as np
from contextlib import ExitStack
import concourse.bass as bass
import concourse.mybir as mybir
from concourse.bass_utils import run_bass_kernel_spmd

F32 = mybir.dt.float32; BF16 = mybir.dt.bfloat16; I32 = mybir.dt.int32; U32 = mybir.dt.uint32
ALU = mybir.AluOpType; AF = mybir.ActivationFunctionType; AX = mybir.AxisListType

NCORES = 8
D = 1024; KD = 8; DR = 1280; KR = 10
NTILES = 17; NTOK = 2112
ALPHA = 4.0 ** 0.25
EPS = 1e-5
NB = 6
STAGE = 99
NCH = 99


class Res:
    __slots__ = ("name", "lw", "rd", "excl")

    def __init__(self, name, excl=False):
        self.name = name; self.lw = None; self.rd = {}; self.excl = excl


class Tl:
    def __init__(self, t, name):
        self.t = t; self.r = Res(name)

    def __getitem__(self, k):
        return self.t[k]


class PQ:
    def __init__(self, ap, res):
        self.ap = ap; self.r = res


class FW:
    def __init__(self, nc, es):
        self.nc = nc; self.es = es
        self.q = {"pe": nc.tensor, "act": nc.scalar, "dve": nc.vector, "pool": nc.gpsimd, "sp": nc.sync}
        self.sem = {}; self.cnt = {}; self.waited = {k: {} for k in self.q}
        self.nsem = 0; self.dsems = []
        for k in self.q:
            self._newsem(k)
        self.out_evs = []

    def _alloc(self, name):
        self.nsem += 1
        return self.es.enter_context(self.nc.semaphore(f"{name}_{self.nsem}"))

    def _newsem(self, k):
        self.sem[k] = self._alloc("e" + k); self.cnt[k] = 0

    def dsem(self, name):
        d = [self._alloc("d" + name), 0, None]
        self.dsems.append(d)
        return d

    def _wait(self, eng, ev):
        if ev is None:
            return
        s, v = ev
        w = self.waited[eng]
        if w.get(id(s), -1) >= v:
            return
        self.q[eng].wait_ge(s, v); w[id(s)] = v

    def emit(self, eng, fn, reads=(), writes=(), dsem=None, is_out=False, serial=True):
        skip = None
        if eng == "pe":
            skip = id(self.sem["pe"])
        if dsem is not None and not serial:
            skip = id(dsem[0])

        def w8(ev):
            if ev is not None and id(ev[0]) != skip:
                self._wait(eng, ev)
        for r in reads:
            w8(r.lw)
        for w_ in writes:
            w8(w_.lw)
            for ev in w_.rd.values():
                w8(ev)
        if dsem is not None and serial:
            self._wait(eng, dsem[2])
        ins = fn(self.q[eng])
        if dsem is not None:
            if dsem[1] >= 30000:
                dsem[0] = self._alloc("dx"); dsem[1] = 0
            dsem[1] += 16
            ins.then_inc(dsem[0], 16)
            ev = (dsem[0], dsem[1]); dsem[2] = ev
            if is_out:
                self.out_evs.append(ev)
        else:
            if self.cnt[eng] >= 30000:
                self._newsem(eng)
            self.cnt[eng] += 1
            ins.then_inc(self.sem[eng], 1)
            ev = (self.sem[eng], self.cnt[eng])
        for r in reads:
            r.rd[id(ev[0])] = ev
        for w_ in writes:
            w_.lw = ev; w_.rd = {}
        return ev

    def barrier(self):
        evs = [(self.sem[k], self.cnt[k]) for k in self.q if self.cnt[k] > 0]
        evs += [d[2] for d in self.dsems if d[2] is not None]
        for k in self.q:
            for ev in evs:
                self._wait(k, ev)

    def finish(self):
        self.barrier()


class TD:
    def __init__(self, idx):
        self.idx = idx
        if idx < 16:
            self.nt = 128; self.nseq = 1; self.L = 128; self.tok0 = idx * 128; self.s0 = 0
        else:
            self.nt = 64; self.nseq = 16; self.L = 4; self.tok0 = 2048; self.s0 = 1
        self.prompt = idx < 16


class Kern:
    def __init__(self, nc, fw, es, dr):
        self.nc = nc; self.fw = fw; self.es = es; self.dr = dr
        self.pqi = 0

    def sb(self, es, name, shape, dt):
        self.nsb = getattr(self, "nsb", 0) + 1
        name = f"{name}_{self.nsb}"
        return Tl(es.enter_context(self.nc.sbuf_tensor(name, shape, dt)), name)

    def E(self, eng, fn, rd=(), wr=(), **kw):
        rs = [x if isinstance(x, Res) else x.r for x in rd]
        ws = [x if isinstance(x, Res) else x.r for x in wr]
        ws = ws + [r for r in rs if r.excl]
        rs = [r for r in rs if not r.excl]
        return self.fw.emit(eng, fn, reads=rs, writes=ws, **kw)

    def pq(self):
        q = self.pqs[self.pqi % 24]; self.pqi += 1
        return q

    def alt(self):
        self.alti = getattr(self, "alti", 0) + 1
        return "act" if self.alti % 2 else "dve"

    def evac(self, eng, out_ap, in_ap, rd, wr):
        if eng == "act":
            self.E("act", lambda e: e.copy(out_ap, in_ap), rd, wr)
        else:
            self.E(eng, lambda e: e.tensor_copy(out_ap, in_ap), rd, wr)

    def transpose_to(self, in_ap, npart, nfree, rd):
        q = self.pq()
        o = q.ap[0:nfree, 0:npart]
        self.E("pe", lambda e: e.transpose(o, in_ap, self.ident[0:npart, 0:npart]), list(rd) + [self.ident], [q])
        return q, o

    def load_fm(self, dram2d, C, col):
        st = self.vstg[self.vstg_i % 2]; d = self.vstg_d[self.vstg_i % 2]; self.vstg_i += 1
        self.E("sp", lambda e: e.dma_start(out=st[0:C, :], in_=dram2d), [], [st], dsem=d)
        q, o = self.transpose_to(st[0:C, :], C, 128, [st])
        self.evac(self.alt(), self.cv[:, col:col + C], o, [q], [self.cv])

    def load_w(self, dst, dram, KC, N, dsem):
        src = dram.rearrange("(k p) n -> p k n", p=128)
        for k in range(KC):
            for n0 in range(0, N, 1024):
                n1 = min(N, n0 + 1024)
                self.E("pool", lambda e, k=k, n0=n0, n1=n1: e.dma_start(out=dst[:, k, n0:n1], in_=src[:, k, n0:n1]),
                       [], [dst], dsem=dsem, serial=False)

    def modap(self, li, part, k, td):
        return self.modT[:, li, part * 8 + k, td.s0:td.s0 + td.nseq]

    def modulate(self, out_ap, in_ap, td, sc, sh, rd, wr):
        if td.nseq == 1:
            self.E("act", lambda e: e.activation(out_ap, in_ap, AF.Identity,
                                                 bias=(sh if sh is not None else 0.0),
                                                 scale=(sc if sc is not None else 1.0)), rd, wr)
        else:
            shape = [128, td.nseq, td.L]
            o3 = out_ap.rearrange("p (s l) -> p s l", l=td.L)
            i3 = in_ap.rearrange("p (s l) -> p s l", l=td.L)
            if sc is not None and sh is not None:
                t3 = self.mtmp[:, 0:td.nt].rearrange("p (s l) -> p s l", l=td.L)
                self.E("dve", lambda e: e.tensor_tensor(t3, i3, sc.unsqueeze(2).to_broadcast(shape), ALU.mult), rd, [self.mtmp])
                self.E("dve", lambda e: e.tensor_tensor(o3, t3, sh.unsqueeze(2).to_broadcast(shape), ALU.add), [self.mtmp], wr)
            elif sc is not None:
                self.E("dve", lambda e: e.tensor_tensor(o3, i3, sc.unsqueeze(2).to_broadcast(shape), ALU.mult), rd, wr)
            else:
                self.E("dve", lambda e: e.tensor_tensor(o3, i3, sh.unsqueeze(2).to_broadcast(shape), ALU.add), rd, wr)

    def xr(self, td, k):
        return self.xres[:, k, td.tok0:td.tok0 + td.nt]

    def resid_ln(self, td, li, gpart, srcs, lng_col, lnb_col):
        nt = td.nt; Tb = self.Tb; Sq = self.Sq
        q1 = self.pq(); q2 = self.pq()
        for j in range(8):
            ap, res = srcs[j]
            self.modulate(Tb[:, j, 0:nt], ap, td, self.modap(li, gpart, j, td), None, [res, self.modT], [Tb])
            self.E("dve", lambda e, j=j: e.scalar_tensor_tensor(out=Tb[:, j, 0:nt], in0=self.xr(td, j), scalar=ALPHA,
                                                                in1=Tb[:, j, 0:nt], op0=ALU.mult, op1=ALU.add),
                   [self.xrr[td.idx], Tb], [Tb])
            self.E("act", lambda e, j=j: e.activation(Sq[:, j, 0:nt], Tb[:, j, 0:nt], AF.Square), [Tb], [Sq])
        for j in range(8):
            self.E("pe", lambda e, j=j: e.matmul(q1.ap[:, 0:nt], lhsT=self.onesf[:, :], rhs=Tb[:, j, 0:nt], start=(j == 0), stop=(j == 7)),
                   [Tb, self.onesf], [q1])
        for j in range(8):
            self.E("pe", lambda e, j=j: e.matmul(q2.ap[:, 0:nt], lhsT=self.onesf[:, :], rhs=Sq[:, j, 0:nt], start=(j == 0), stop=(j == 7)),
                   [Sq, self.onesf], [q2])
        self.ln_finish(td, q1, q2, Tb, lng_col, lnb_col, lambda j: self.xr(td, j), [self.xrr[td.idx]], AF.Identity)

    def ln_finish(self, td, q1, q2, Tb, lng_col, lnb_col, outfn, outres, func):
        nt = td.nt; st = self.lnst
        self.evac("act", st[:, 0, 0:nt], q1.ap[:, 0:nt], [q1], [st])
        self.E("dve", lambda e: e.tensor_tensor(st[:, 2, 0:nt], st[:, 0, 0:nt], st[:, 0, 0:nt], ALU.mult), [st], [st])
        self.E("dve", lambda e: e.tensor_tensor(st[:, 1, 0:nt], q2.ap[:, 0:nt], st[:, 2, 0:nt], ALU.subtract), [q2, st], [st])
        self.E("act", lambda e: e.activation(st[:, 1, 0:nt], st[:, 1, 0:nt], AF.Sqrt, bias=self.epsb[:, 0:1], scale=1.0), [st, self.epsb], [st])
        self.E("dve", lambda e: e.reciprocal(st[:, 1, 0:nt], st[:, 1, 0:nt]), [st], [st])
        for j in range(8):
            self.E("dve", lambda e, j=j: e.tensor_tensor(Tb[:, j, 0:nt], Tb[:, j, 0:nt], st[:, 0, 0:nt], ALU.subtract), [Tb, st], [Tb])
            self.E("dve", lambda e, j=j: e.tensor_tensor(Tb[:, j, 0:nt], Tb[:, j, 0:nt], st[:, 1, 0:nt], ALU.mult), [Tb, st], [Tb])
            self.E("act", lambda e, j=j: e.activation(outfn(j), Tb[:, j, 0:nt], func, bias=self.cv[:, lnb_col + j:lnb_col + j + 1],
                                                      scale=self.cv[:, lng_col + j:lng_col + j + 1]), [Tb, self.cv], outres)

    def state_out(self, stg, ncols, grp, dram_rows, rd):
        for g in range(ncols // grp):
            q, o = self.transpose_to(stg[:, g * grp:(g + 1) * grp], 128, grp, rd)
            ost = self.ost[g % 2]
            self.evac(self.alt(), ost[0:grp, :], o, [q], [ost])
            self.E("sp", lambda e, g=g, ost=ost: e.dma_start(out=dram_rows[g * grp:(g + 1) * grp, :], in_=ost[0:grp, :]),
                   [ost], [], dsem=self.dout[g % 2], is_out=True)

    def run(self):
        nc = self.nc; es = self.es; dr = self.dr; E = self.E
        self.pbanks = [es.enter_context(nc.psum_tensor(f"pb{b}", [128, 512], F32)) for b in range(8)]
        self.bres = [Res(f"bank{b}", excl=True) for b in range(8)]
        self.pqs = [PQ(self.pbanks[i % 6][:, (i // 6) * 128:(i // 6 + 1) * 128], self.bres[i % 6]) for i in range(24)]
        self.accq = [PQ(self.pbanks[6 + i][:, :], self.bres[6 + i]) for i in range(2)]
        self.xres = self.sb(es, "xres", [128, 8, NTOK], F32)
        self.xrr = [Res(f"xr{t}") for t in range(NTILES)]
        self.ident = self.sb(es, "ident", [128, 128], F32)
        self.onesf = self.sb(es, "onesf", [128, 128], F32)
        self.iota16 = self.sb(es, "iota16", [128, 16], F32)
        self.epsb = self.sb(es, "epsb", [128, 1], F32)
        self.cv = self.sb(es, "cv", [128, 544], F32)
        self.modT = self.sb(es, "modT", [128, 2, 48, 18], F32)
        self.mtmp = self.sb(es, "mtmp", [128, 128], F32)
        self.Tb = self.sb(es, "Tb", [128, 8, 128], F32)
        self.Sq = self.sb(es, "Sq", [128, 8, 128], F32)
        self.lnst = self.sb(es, "lnst", [128, 3, 128], F32)
        self.ost = [self.sb(es, f"ost{i}", [128, 128], F32) for i in range(2)]
        self.vstg = [self.sb(es, f"vstg{i}", [128, 128], F32) for i in range(2)]
        self.vstg_d = [self.fw.dsem(f"vs{i}") for i in range(2)]; self.vstg_i = 0
        self.dout = [self.fw.dsem(f"out{i}") for i in range(2)]
        self.tds = [TD(i) for i in range(NTILES)]
        wk = self.mtmp
        E("pool", lambda e: e.iota(wk[:], pattern=[[1, 128]], base=0, channel_multiplier=-1, allow_small_or_imprecise_dtypes=True), [], [wk])
        E("dve", lambda e: e.tensor_single_scalar(self.ident[:], wk[:], 0.0, ALU.is_equal), [wk], [self.ident])
        E("dve", lambda e: e.memset(self.onesf[:], 1.0 / 1024.0), [], [self.onesf])
        E("dve", lambda e: e.memset(self.epsb[:], EPS), [], [self.epsb])
        E("pool", lambda e: e.iota(self.iota16[:], pattern=[[1, 16]], base=0, channel_multiplier=0, allow_small_or_imprecise_dtypes=True), [], [self.iota16])

        if STAGE < -3: return
        col = {}
        cur = [0]

        def vec(name, dram2d, C):
            col[name] = cur[0]
            for r0 in range(0, C, 128):
                r1 = min(C, r0 + 128)
                self.load_fm(dram2d[r0:r1, :], r1 - r0, cur[0] + r0)
            cur[0] += C
        v2 = lambda a: a.rearrange("(c p) -> c p", p=128)
        for li in range(2):
            vec(f"bada{li}", v2(dr["b_ada"][li]), 48)
            vec(f"lnmg{li}", v2(dr["ln_mix_g"][li]), 8); vec(f"lnmb{li}", v2(dr["ln_mix_b"][li]), 8)
            vec(f"lnfg{li}", v2(dr["ln_ffn_g"][li]), 8); vec(f"lnfb{li}", v2(dr["ln_ffn_b"][li]), 8)
        vec("lcw", dr["lru_conv_w"][0].rearrange("k (c p) -> (k c) p", p=128), 40)
        vec("lcb", v2(dr["lru_conv_b"][0]), 10); vec("lba", v2(dr["lru_b_a"][0]), 10)
        vec("lbi", v2(dr["lru_b_i"][0]), 10); vec("lam", v2(dr["lru_lambda"][0]), 10)
        vec("cb1", v2(dr["conf_b_pw1"][0]), 16)
        vec("cdw", dr["conf_dw_w"][0].rearrange("k (c p) -> (k c) p", p=128), 248)
        vec("cdb", v2(dr["conf_dw_b"][0]), 8); vec("clg", v2(dr["conf_ln_g"][0]), 8)
        vec("clb", v2(dr["conf_ln_b"][0]), 8); vec("cb2", v2(dr["conf_b_pw2"][0]), 8)
        self.col = col
        cv = self.cv
        for li in range(2):
            for part in (1, 4):
                c0 = col[f"bada{li}"] + part * 8
                E("dve", lambda e, c0=c0: e.tensor_scalar_add(cv[:, c0:c0 + 8], cv[:, c0:c0 + 8], 1.0), [cv], [cv])
        cl = col["lam"]
        E("act", lambda e: e.activation(cv[:, cl:cl + 10], cv[:, cl:cl + 10], AF.Exp, scale=-1.0), [cv], [cv])
        E("act", lambda e: e.activation(cv[:, cl:cl + 10], cv[:, cl:cl + 10], AF.Ln, bias=1.0, scale=1.0), [cv], [cv])
        E("dve", lambda e: e.tensor_scalar_mul(cv[:, cl:cl + 10], cv[:, cl:cl + 10], -8.0), [cv], [cv])

        if STAGE < -2: return
        with ExitStack() as pes:
            csb = self.sb(pes, "csb", [18, 1024], F32)
            E("dve", lambda e: e.memset(csb[:, :], 0.0), [], [csb])
            cT = self.sb(pes, "cT", [128, 8, 18], BF16)
            wab = [self.sb(pes, f"wab{i}", [128, 8, 512], BF16) for i in range(2)]
            dwa = [self.fw.dsem(f"wa{i}") for i in range(2)]
            dc = self.fw.dsem("c")
            E("sp", lambda e: e.dma_start(out=csb[0:17, :], in_=dr["cc"]), [], [csb], dsem=dc)
            E("act", lambda e: e.activation(csb[:, :], csb[:, :], AF.Silu), [csb], [csb])
            for k in range(8):
                q, o = self.transpose_to(csb[0:18, k * 128:(k + 1) * 128], 18, 128, [csb])
                self.evac(self.alt(), cT[:, k, :], o[:, 0:18], [q], [cT])
            it = 0
            for li in range(2):
                if STAGE < -1.7: break
                for n0 in range(0, 6144, 512):
                    if it >= NCH: break
                    if STAGE < -1.3 and it >= 1: break
                    if STAGE < -1.15 and it >= 2: break
                    if STAGE < -1.05 and it >= 3: break
                    wb = wab[it % 2]; dd = dwa[it % 2]; it += 1
                    src = dr["w_ada"][li][:, n0:n0 + 512].rearrange("(k p) n -> p k n", p=128)
                    for k in range(8):
                        E("pool", lambda e, wb=wb, src=src, k=k: e.dma_start(out=wb[:, k, :], in_=src[:, k, :]), [], [wb], dsem=dd, serial=(k == 0))
                    for s in range(4):
                        if STAGE < -1.5: break
                        mc = n0 // 128 + s
                        q = self.pq()
                        for k in range(8):
                            E("pe", lambda e, k=k, s=s, wb=wb, q=q: e.matmul(q.ap[:, 0:18], lhsT=wb[:, k, s * 128:(s + 1) * 128], rhs=cT[:, k, :],
                                                                          start=(k == 0), stop=(k == 7)), [wb, cT], [q])
                        bc = col[f"bada{li}"] + mc
                        E("act", lambda e, q=q, li=li, mc=mc, bc=bc: e.activation(self.modT[:, li, mc, :], q.ap[:, 0:18], AF.Identity,
                                                                               bias=cv[:, bc:bc + 1], scale=1.0), [q, cv], [self.modT])
            self.fw.barrier()

        if STAGE < -1: return
        with ExitStack() as pes:
            xin = [self.sb(pes, f"xin{i}", [128, 1024], F32) for i in range(2)]
            dx = [self.fw.dsem(f"x{i}") for i in range(2)]
            for td in self.tds:
                xi = xin[td.idx % 2]
                src = dr["xp"][td.tok0:td.tok0 + 128, :] if td.prompt else dr["xs"]
                E("sp", lambda e, xi=xi, src=src, td=td: e.dma_start(out=xi[0:td.nt, :], in_=src), [], [xi], dsem=dx[td.idx % 2])
                for k in range(8):
                    q, o = self.transpose_to(xi[0:td.nt, k * 128:(k + 1) * 128], td.nt, 128, [xi])
                    self.evac(self.alt(), self.xr(td, k), o, [q], [self.xrr[td.idx]])
            self.fw.barrier()

        if STAGE < 1: return
        self.phase_m0()
        if STAGE < 2: return
        self.phase_peer(0)
        if STAGE < 3: return
        self.phase_m1()
        if STAGE < 4: return
        self.phase_peer(1)

    def phase_m0(self):
        E = self.E; dr = self.dr; col = self.col; cv = self.cv
        with ExitStack() as pes:
            w_in = self.sb(pes, "w_in", [128, 8, 2560], BF16)
            w_a = self.sb(pes, "w_a", [128, 10, 128], BF16)
            w_i = self.sb(pes, "w_i", [128, 10, 128], BF16)
            w_out = self.sb(pes, "w_out", [128, 10, 1024], BF16)
            dw = self.fw.dsem("wm0")
            self.load_w(w_in, dr["lru_w_in"][0], 8, 2560, dw)
            E("pool", lambda e: e.dma_start(out=w_a[:, :, :], in_=dr["lru_w_a"][0].rearrange("h i j -> i h j")), [], [w_a], dsem=dw, serial=False)
            E("pool", lambda e: e.dma_start(out=w_i[:, :, :], in_=dr["lru_w_i"][0].rearrange("h i j -> i h j")), [], [w_i], dsem=dw, serial=False)
            self.load_w(w_out, dr["lru_w_out"][0], 10, 1024, dw)
            hmT = self.sb(pes, "hmT", [128, 8, 128], BF16)
            xp_p = self.sb(pes, "xp_p", [128, 10, 1, 131], F32)
            xp_s = self.sb(pes, "xp_s", [128, 10, 16, 7], F32)
            hst_p = self.sb(pes, "hst_p", [128, 10], F32)
            hst_s = self.sb(pes, "hst_s", [128, 10, 16], F32)
            gg = self.sb(pes, "gg", [128, 10, 128], F32)
            yg = self.sb(pes, "yg", [128, 10, 128], BF16)
            nbuf = 2
            xc = [self.sb(pes, f"xc{i}", [128, 128], F32) for i in range(nbuf)]
            xcb = [self.sb(pes, f"xcb{i}", [128, 128], BF16) for i in range(nbuf)]
            rr = [self.sb(pes, f"rr{i}", [128, 128], F32) for i in range(nbuf)]
            ig = [self.sb(pes, f"ig{i}", [128, 128], F32) for i in range(nbuf)]
            aa = [self.sb(pes, f"aa{i}", [128, 128], F32) for i in range(nbuf)]
            a2 = [self.sb(pes, f"a2{i}", [128, 128], F32) for i in range(nbuf)]
            hh = [self.sb(pes, f"hh{i}", [128, 128], F32) for i in range(nbuf)]
            stg = self.sb(pes, "stg0", [128, 1280], F32)
            dst = self.fw.dsem("st0")
            E("dve", lambda e: e.memset(xp_p[:, :, :, :], 0.0), [], [xp_p])
            E("dve", lambda e: e.memset(hst_p[:, :], 0.0), [], [hst_p])
            E("sp", lambda e: e.dma_start(out=stg[0:16, :], in_=dr["slh"]), [], [stg], dsem=dst)
            for c in range(10):
                q, o = self.transpose_to(stg[0:16, c * 128:(c + 1) * 128], 16, 128, [stg])
                self.evac(self.alt(), hst_s[:, c, :], o, [q], [hst_s])
            E("sp", lambda e: e.dma_start(out=stg[0:48, :], in_=dr["slc"].rearrange("s k d -> (s k) d")), [], [stg], dsem=dst)
            for c in range(10):
                q, o = self.transpose_to(stg[0:48, c * 128:(c + 1) * 128], 48, 128, [stg])
                self.evac(self.alt(), xp_s[:, c, :, 0:3], o.rearrange("p (s k) -> p s k", k=3), [q], [xp_s])

            for td in self.tds:
                nt = td.nt; L = td.L; ns = td.nseq
                xp = xp_p if td.prompt else xp_s
                for k in range(8):
                    self.modulate(hmT[:, k, 0:nt], self.xr(td, k), td, self.modap(0, 1, k, td), self.modap(0, 0, k, td),
                                  [self.xrr[td.idx], self.modT], [hmT])
                for c in range(20):
                    q = self.pq()
                    for k in range(8):
                        E("pe", lambda e, k=k, c=c, q=q: e.matmul(q.ap[:, 0:nt], lhsT=w_in[:, k, c * 128:(c + 1) * 128], rhs=hmT[:, k, 0:nt],
                                                                 start=(k == 0), stop=(k == 7)), [w_in, hmT], [q])
                    if c < 10:
                        o3 = xp[:, c, :, 3:3 + L]
                        self.evac("dve", o3, q.ap[:, 0:nt].rearrange("p (s l) -> p s l", l=L), [q], [xp])
                    else:
                        E("act", lambda e, c=c, q=q: e.activation(gg[:, c - 10, 0:nt], q.ap[:, 0:nt], AF.Gelu), [q], [gg])
                for c in range(10):
                    b = c % nbuf
                    xc3 = xc[b][:, 0:nt].rearrange("p (s l) -> p s l", l=L)
                    lw = col["lcw"]
                    E("dve", lambda e, c=c, xc3=xc3: e.tensor_scalar(xc3, xp[:, c, :, 0:L], cv[:, lw + c:lw + c + 1],
                                                                     cv[:, col["lcb"] + c:col["lcb"] + c + 1], op0=ALU.mult, op1=ALU.add),
                      [xp, cv], [xc[b]])
                    for kk in range(1, 4):
                        E("dve", lambda e, c=c, kk=kk, xc3=xc3: e.scalar_tensor_tensor(out=xc3, in0=xp[:, c, :, kk:kk + L],
                                                                                        scalar=cv[:, lw + kk * 10 + c:lw + kk * 10 + c + 1],
                                                                                        in1=xc3, op0=ALU.mult, op1=ALU.add), [xp, cv, xc[b]], [xc[b]])
                    if td.prompt:
                        E("pool", lambda e, c=c: e.tensor_copy(xp[:, c, :, 0:3], xp[:, c, :, L:L + 3]), [xp], [xp])
                    E("act", lambda e, b=b: e.copy(xcb[b][:, 0:nt], xc[b][:, 0:nt]), [xc[b]], [xcb[b]])
                    qa = self.pq(); qi = self.pq()
                    E("pe", lambda e, c=c, b=b, qa=qa: e.matmul(qa.ap[:, 0:nt], lhsT=w_a[:, c, :], rhs=xcb[b][:, 0:nt], start=True, stop=True), [w_a, xcb[b]], [qa])
                    E("pe", lambda e, c=c, b=b, qi=qi: e.matmul(qi.ap[:, 0:nt], lhsT=w_i[:, c, :], rhs=xcb[b][:, 0:nt], start=True, stop=True), [w_i, xcb[b]], [qi])
                    E("act", lambda e, c=c, b=b, qa=qa: e.activation(rr[b][:, 0:nt], qa.ap[:, 0:nt], AF.Sigmoid,
                                                                     bias=cv[:, col["lba"] + c:col["lba"] + c + 1], scale=1.0), [qa, cv], [rr[b]])
                    E("act", lambda e, c=c, b=b, qi=qi: e.activation(ig[b][:, 0:nt], qi.ap[:, 0:nt], AF.Sigmoid,
                                                                     bias=cv[:, col["lbi"] + c:col["lbi"] + c + 1], scale=1.0), [qi, cv], [ig[b]])
                    E("act", lambda e, c=c, b=b: e.activation(aa[b][:, 0:nt], rr[b][:, 0:nt], AF.Exp,
                                                              scale=cv[:, col["lam"] + c:col["lam"] + c + 1]), [rr[b], cv], [aa[b]])
                    E("dve", lambda e, b=b: e.tensor_tensor(a2[b][:, 0:nt], aa[b][:, 0:nt], aa[b][:, 0:nt], ALU.mult), [aa[b]], [a2[b]])
                    E("act", lambda e, b=b: e.activation(a2[b][:, 0:nt], a2[b][:, 0:nt], AF.Sqrt, bias=self.onesb[:, 0:1], scale=-1.0), [a2[b], self.onesb], [a2[b]])
                    E("dve", lambda e, b=b: e.tensor_tensor(ig[b][:, 0:nt], ig[b][:, 0:nt], xc[b][:, 0:nt], ALU.mult), [ig[b], xc[b]], [ig[b]])
                    E("dve", lambda e, b=b: e.tensor_tensor(ig[b][:, 0:nt], ig[b][:, 0:nt], a2[b][:, 0:nt], ALU.mult), [ig[b], a2[b]], [ig[b]])
                    if td.prompt:
                        E("dve", lambda e, c=c, b=b: e.tensor_tensor_scan(hh[b][:, 0:nt], aa[b][:, 0:nt], ig[b][:, 0:nt], hst_p[:, c:c + 1],
                                                                           op0=ALU.mult, op1=ALU.add), [aa[b], ig[b], hst_p], [hh[b]])
                        E("dve", lambda e, c=c, b=b: e.tensor_copy(hst_p[:, c:c + 1], hh[b][:, nt - 1:nt]), [hh[b]], [hst_p])
                    else:
                        h3 = hh[b][:, 0:nt].rearrange("p (s l) -> p s l", l=L)
                        a3 = aa[b][:, 0:nt].rearrange("p (s l) -> p s l", l=L)
                        b3 = ig[b][:, 0:nt].rearrange("p (s l) -> p s l", l=L)
                        for t in range(L):
                            prev = hst_s[:, c, :] if t == 0 else h3[:, :, t - 1]
                            E("dve", lambda e, t=t, prev=prev, h3=h3, a3=a3: e.tensor_tensor(h3[:, :, t], a3[:, :, t], prev, ALU.mult),
                              [aa[b], hst_s, hh[b]], [hh[b]])
                            E("dve", lambda e, t=t, h3=h3, b3=b3: e.tensor_tensor(h3[:, :, t], h3[:, :, t], b3[:, :, t], ALU.add),
                              [ig[b], hh[b]], [hh[b]])
                        E("dve", lambda e, c=c, h3=h3: e.tensor_copy(hst_s[:, c, :], h3[:, :, L - 1]), [hh[b]], [hst_s])
                    E("dve", lambda e, c=c, b=b: e.tensor_tensor(yg[:, c, 0:nt], hh[b][:, 0:nt], gg[:, c, 0:nt], ALU.mult), [hh[b], gg], [yg])
                srcs = []
                for j in range(8):
                    q = self.pq()
                    for k in range(10):
                        E("pe", lambda e, k=k, j=j, q=q: e.matmul(q.ap[:, 0:nt], lhsT=w_out[:, k, j * 128:(j + 1) * 128], rhs=yg[:, k, 0:nt],
                                                                 start=(k == 0), stop=(k == 9)), [w_out, yg], [q])
                    srcs.append((q.ap[:, 0:nt], q.r))
                self.resid_ln(td, 0, 2, srcs, col["lnmg0"], col["lnmb0"])
                if td.idx == 15:
                    self.state_out(hst_p, 10, 10, self.dr["o_hp"], [hst_p])
                    E("dve", lambda e: e.tensor_copy(stg[:, 0:30].rearrange("p (k c) -> p k c", c=10),
                                                     xp_p[:, :, 0, 0:3].rearrange("p c k -> p k c")), [xp_p], [stg])
                    self.state_out(stg, 30, 30, self.dr["o_lcp"], [stg])
                if td.idx == 16:
                    E("dve", lambda e: e.tensor_copy(stg[:, 0:160].rearrange("p (s c) -> p s c", c=10),
                                                     hst_s[:, :, :].rearrange("p c s -> p s c")), [hst_s], [stg])
                    self.state_out(stg, 160, 80, self.dr["o_hs"], [stg])
                    for s in range(16):
                        E("dve", lambda e, s=s: e.tensor_copy(stg[:, 160 + s * 30:160 + (s + 1) * 30].rearrange("p (k c) -> p k c", c=10),
                                                              xp_s[:, :, s, 4:7].rearrange("p c k -> p k c")), [xp_s], [stg])
                    self.state_out(stg[:, 160:640], 480, 120, self.dr["o_lcs"], [stg])
            self.fw.barrier()

    def phase_m1(self):
        E = self.E; dr = self.dr; col = self.col; cv = self.cv
        with ExitStack() as pes:
            w1 = self.sb(pes, "w_pw1", [128, 8, 2048], BF16)
            w2 = self.sb(pes, "w_pw2", [128, 8, 1024], BF16)
            dw = self.fw.dsem("wm1")
            self.load_w(w1, dr["conf_w_pw1"][0], 8, 2048, dw)
            self.load_w(w2, dr["conf_w_pw2"][0], 8, 1024, dw)
            hmT = self.sb(pes, "hmT1", [128, 8, 128], BF16)
            sg = self.sb(pes, "sg", [128, 8, 128], F32)
            up_p = self.sb(pes, "up_p", [128, 8, 1, 158], F32)
            up_s = self.sb(pes, "up_s", [128, 8, 16, 34], F32)
            dd = self.sb(pes, "dd", [128, 8, 128], F32)
            dd2 = self.sb(pes, "dd2", [128, 8, 128], F32)
            dsT = self.sb(pes, "dsT", [128, 8, 128], BF16)
            osb = self.sb(pes, "osb", [128, 8, 128], F32)
            stg = self.sb(pes, "stg1", [128, 1024], F32)
            dst = self.fw.dsem("st1")
            E("dve", lambda e: e.memset(up_p[:, :, :, :], 0.0), [], [up_p])
            src = dr["scc"].rearrange("s k d -> (s k) d")
            for g in range(4):
                E("sp", lambda e, g=g: e.dma_start(out=stg[0:120, :], in_=src[g * 120:(g + 1) * 120, :]), [], [stg], dsem=dst)
                for c in range(8):
                    q, o = self.transpose_to(stg[0:120, c * 128:(c + 1) * 128], 120, 128, [stg])
                    self.evac(self.alt(), up_s[:, c, g * 4:(g + 1) * 4, 0:30], o.rearrange("p (s k) -> p s k", k=30), [q], [up_s])
            for td in self.tds:
                nt = td.nt; L = td.L
                up = up_p if td.prompt else up_s
                for k in range(8):
                    self.modulate(hmT[:, k, 0:nt], self.xr(td, k), td, self.modap(1, 1, k, td), self.modap(1, 0, k, td),
                                  [self.xrr[td.idx], self.modT], [hmT])
                for c in list(range(8, 16)) + list(range(8)):
                    q = self.pq()
                    for k in range(8):
                        E("pe", lambda e, k=k, c=c, q=q: e.matmul(q.ap[:, 0:nt], lhsT=w1[:, k, c * 128:(c + 1) * 128], rhs=hmT[:, k, 0:nt],
                                                                 start=(k == 0), stop=(k == 7)), [w1, hmT], [q])
                    bcol = col["cb1"] + c
                    if c >= 8:
                        E("act", lambda e, c=c, q=q, bcol=bcol: e.activation(sg[:, c - 8, 0:nt], q.ap[:, 0:nt], AF.Sigmoid,
                                                                            bias=cv[:, bcol:bcol + 1], scale=1.0), [q, cv], [sg])
                    else:
                        E("dve", lambda e, c=c, q=q, bcol=bcol: e.scalar_tensor_tensor(
                            out=up[:, c, :, 30:30 + L], in0=q.ap[:, 0:nt].rearrange("p (s l) -> p s l", l=L), scalar=cv[:, bcol:bcol + 1],
                            in1=sg[:, c, 0:nt].rearrange("p (s l) -> p s l", l=L), op0=ALU.add, op1=ALU.mult), [q, cv, sg], [up])
                for c in range(8):
                    d3 = dd[:, c, 0:nt].rearrange("p (s l) -> p s l", l=L)
                    d3b = dd2[:, c, 0:nt].rearrange("p (s l) -> p s l", l=L)
                    cw = col["cdw"]
                    E("dve", lambda e, c=c, d3=d3: e.tensor_scalar(d3, up[:, c, :, 0:L], cv[:, cw + c:cw + c + 1],
                                                                   cv[:, col["cdb"] + c:col["cdb"] + c + 1], op0=ALU.mult, op1=ALU.add), [up, cv], [dd])
                    E("dve", lambda e, c=c, d3b=d3b: e.tensor_scalar(d3b, up[:, c, :, 1:1 + L], cv[:, cw + 8 + c:cw + 8 + c + 1], None, op0=ALU.mult), [up, cv], [dd2])
                    for kk in range(2, 31):
                        acc_ = d3 if kk % 2 == 0 else d3b
                        accr = dd if kk % 2 == 0 else dd2
                        E("dve", lambda e, c=c, kk=kk, acc_=acc_: e.scalar_tensor_tensor(out=acc_, in0=up[:, c, :, kk:kk + L],
                                                                                          scalar=cv[:, cw + kk * 8 + c:cw + kk * 8 + c + 1],
                                                                                          in1=acc_, op0=ALU.mult, op1=ALU.add), [up, cv, accr], [accr])
                    E("dve", lambda e, d3=d3, d3b=d3b: e.tensor_tensor(d3, d3, d3b, ALU.add), [dd, dd2], [dd])
                    if td.prompt:
                        E("pool", lambda e, c=c: e.tensor_copy(up[:, c, :, 0:30], up[:, c, :, L:L + 30]), [up], [up])
                Sq = self.Sq
                q1 = self.pq(); q2 = self.pq()
                for j in range(8):
                    E("act", lambda e, j=j: e.activation(Sq[:, j, 0:nt], dd[:, j, 0:nt], AF.Square), [dd], [Sq])
                for j in range(8):
                    E("pe", lambda e, j=j: e.matmul(q1.ap[:, 0:nt], lhsT=self.onesf[:, :], rhs=dd[:, j, 0:nt], start=(j == 0), stop=(j == 7)), [dd, self.onesf], [q1])
                for j in range(8):
                    E("pe", lambda e, j=j: e.matmul(q2.ap[:, 0:nt], lhsT=self.onesf[:, :], rhs=Sq[:, j, 0:nt], start=(j == 0), stop=(j == 7)), [Sq, self.onesf], [q2])
                self.ln_finish(td, q1, q2, dd, col["clg"], col["clb"], lambda j: dsT[:, j, 0:nt], [dsT], AF.Silu)
                srcs = []
                for j in range(8):
                    q = self.pq()
                    for k in range(8):
                        E("pe", lambda e, k=k, j=j, q=q: e.matmul(q.ap[:, 0:nt], lhsT=w2[:, k, j * 128:(j + 1) * 128], rhs=dsT[:, k, 0:nt],
                                                                 start=(k == 0), stop=(k == 7)), [w2, dsT], [q])
                    bcol = col["cb2"] + j
                    E("act", lambda e, j=j, q=q, bcol=bcol: e.activation(osb[:, j, 0:nt], q.ap[:, 0:nt], AF.Identity, bias=cv[:, bcol:bcol + 1], scale=1.0),
                      [q, cv], [osb])
                    srcs.append((osb[:, j, 0:nt], osb.r))
                self.resid_ln(td, 1, 2, srcs, col["lnmg1"], col["lnmb1"])
                if td.idx == 15:
                    E("dve", lambda e: e.tensor_copy(stg[:, 0:240].rearrange("p (k c) -> p k c", c=8),
                                                     up_p[:, :, 0, 0:30].rearrange("p c k -> p k c")), [up_p], [stg])
                    self.state_out(stg, 240, 120, self.dr["o_ccp"], [stg])
                if td.idx == 16:
                    for s in range(16):
                        E("dve", lambda e, s=s: e.tensor_copy(stg[:, 0:240].rearrange("p (k c) -> p k c", c=8),
                                                              up_s[:, :, s, 4:34].rearrange("p c k -> p k c")), [up_s], [stg])
                        self.state_out(stg, 240, 120, self.dr["o_ccs"][s * 240:(s + 1) * 240, :], [stg])
            self.fw.barrier()

    def phase_peer(self, li):
        E = self.E; dr = self.dr; col = self.col; cv = self.cv
        last = (li == 1)
        with ExitStack() as pes:
            wq = self.sb(pes, "wq", [128, 8, 2048], BF16)
            keysT = self.sb(pes, "keysT", [128, 16, 128], BF16)
            dw = self.fw.dsem(f"wq{li}")
            self.load_w(wq, dr["peer_w_q"][li], 8, 2048, dw)
            hrot = [self.sb(pes, f"hrot{i}", [128, 128], F32) for i in range(2)]
            hfT = self.sb(pes, "hfT", [128, 8, 128], BF16)
            hf = [self.sb(pes, f"hf{i}", [128, 1024], F32) for i in range(2)]
            qTc = [self.sb(pes, f"qTc{i}", [128, 128], BF16) for i in range(3)]
            S = self.sb(pes, "S", [128, 8, 2, 128], F32)
            Sflat = S.t[:, :, :, :].rearrange("p a b c -> p (a b c)")
            S2 = self.sb(pes, "S2", [128, 128], F32)
            vv = self.sb(pes, "vv", [128, 8, 2, 16], F32)
            iu = self.sb(pes, "iu", [128, 8, 2, 16], U32)
            iff = self.sb(pes, "iff", [128, 8, 2, 16], F32)
            cand = Tl(Sflat[:, 0:1024].rearrange("p (h i j) -> p h i j", h=4, i=16), "x"); cand.r = S.r
            cand2 = Tl(self.Sq.t[:, :, :].rearrange("p a b -> p (a b)").rearrange("p (h i j) -> p h i j", h=4, i=16), "x"); cand2.r = self.Sq.r
            yst = Tl(Sflat[:, 1024:2048], "x"); yst.r = S.r
            ts = self.sb(pes, "ts", [128, 8, 16], F32)
            pu = self.sb(pes, "pu", [128, 8, 16], U32)
            p1u = self.sb(pes, "p1u", [128, 8, 16], U32)
            p1f = self.sb(pes, "p1f", [128, 8, 16], F32)
            p2f = self.sb(pes, "p2f", [128, 8, 16], F32)
            gte = [self.sb(pes, f"gte{i}", [128, 8, 16], F32) for i in range(2)]
            ssum = self.sb(pes, "ssum", [128, 8], F32)
            e1 = self.sb(pes, "e1", [128, 128], F32)
            e2 = self.sb(pes, "e2", [128, 128], F32)
            idx = [self.sb(pes, f"idx{i}", [128, 128], I32) for i in range(2)]
            dots = self.sb(pes, "dots", [128, 128], F32)
            wgt = self.sb(pes, "wgt", [128, 128], F32)
            gel = self.sb(pes, "gel", [128, 128], F32)
            gb = [self.sb(pes, f"gb{i}", [128, 2048], F32) for i in range(NB)]
            dg = [self.fw.dsem(f"g{li}_{i}") for i in range(NB)]
            junk = self.sb(pes, "junk", [128, 1024], BF16)
            acc = Tl(self.Tb.t[:, :, :].rearrange("p a b -> p (a b)"), "x"); acc.r = self.Tb.r
            dgt = [self.sb(pes, f"dgt{i}", [128, 128], F32) for i in range(3)]
            dy = self.fw.dsem(f"y{li}")
            dk = self.fw.dsem(f"k{li}")
            kst = S2
            for b in range(2):
                E("pool", lambda e, b=b: e.memset(idx[b][:, :], 0), [], [idx[b]])
                E("dve", lambda e, b=b: e.memset(hf[b][:, :], 0.0), [], [hf[b]])
                E("dve", lambda e, b=b: e.memset(gte[b][:, :, :], 0.0), [], [gte[b]])
            ksrc = dr["peer_keys"][li].rearrange("h p n d -> (h p n) d")
            for c in range(16):
                E("sp", lambda e, c=c: e.dma_start(out=kst[:, :], in_=ksrc[c * 128:(c + 1) * 128, :]), [], [kst], dsem=dk)
                q, o = self.transpose_to(kst[:, :], 128, 128, [kst])
                self.evac(self.alt(), keysT[:, c, :], o, [q], [keysT])
            uvtab = dr["peer_uv"].rearrange("l e d -> (l e) d")

            def R(td, b):
                nt = td.nt; hfb = hf[b]; gt = gte[b]
                for k in range(8):
                    h32 = hrot[k % 2]
                    self.modulate(h32[:, 0:nt], self.xr(td, k), td, self.modap(li, 4, k, td), self.modap(li, 3, k, td),
                                  [self.xrr[td.idx], self.modT], [h32])
                    E("act", lambda e, k=k, h32=h32: e.copy(hfT[:, k, 0:nt], h32[:, 0:nt]), [h32], [hfT])
                    q, o = self.transpose_to(h32[:, 0:nt], 128, nt, [h32])
                    self.evac("dve", hfb[0:nt, k * 128:(k + 1) * 128], o, [q], [hfb])
                    yield
                def score(c):
                    qc = qTc[c % 3]
                    q2 = self.pq()
                    E("pe", lambda e, c=c, q2=q2, qc=qc: e.matmul(q2.ap[0:nt, :], lhsT=qc[:, 0:nt], rhs=keysT[:, c, :], start=True, stop=True), [qc, keysT], [q2])
                    self.evac("act", S[0:nt, c // 2, c % 2, :], q2.ap[0:nt, :], [q2], [S])
                for c in range(16):
                    q = self.pq(); qc = qTc[c % 3]
                    for k in range(8):
                        E("pe", lambda e, k=k, c=c, q=q: e.matmul(q.ap[:, 0:nt], lhsT=wq[:, k, c * 128:(c + 1) * 128], rhs=hfT[:, k, 0:nt],
                                                                 start=(k == 0), stop=(k == 7)), [wq, hfT], [q])
                    self.evac("act", qc[:, 0:nt], q.ap[:, 0:nt], [q], [qc])
                    if c > 0:
                        score(c - 1)
                    yield
                score(15)
                yield
                for c in range(16):
                    h, p = c // 2, c % 2
                    sc_ = S[0:nt, h, p, :]
                    E("dve", lambda e, h=h, p=p, sc_=sc_: e.max(out=vv[0:nt, h, p, 0:8], in_=sc_), [S], [vv])
                    E("dve", lambda e, h=h, p=p, sc_=sc_: e.max_index(iu[0:nt, h, p, 0:8], vv[0:nt, h, p, 0:8], sc_), [S, vv], [iu])
                    E("dve", lambda e, h=h, p=p, sc_=sc_: e.match_replace(out=S2[0:nt, :], in_to_replace=vv[0:nt, h, p, 0:8], in_values=sc_, imm_value=-1e30), [S, vv], [S2])
                    yield
                    E("dve", lambda e, h=h, p=p: e.max(out=vv[0:nt, h, p, 8:16], in_=S2[0:nt, :]), [S2], [vv])
                    E("dve", lambda e, h=h, p=p: e.max_index(iu[0:nt, h, p, 8:16], vv[0:nt, h, p, 8:16], S2[0:nt, :]), [S2, vv], [iu])
                    yield
                E("dve", lambda e: e.tensor_copy(iff[0:nt], iu[0:nt]), [iu], [iff])
                for hh_ in range(2):
                    h0 = hh_ * 4
                    E("dve", lambda e, h0=h0: e.tensor_tensor(cand[0:nt], vv[0:nt, h0:h0 + 4, 0, :].unsqueeze(3).to_broadcast([nt, 4, 16, 16]),
                                                              vv[0:nt, h0:h0 + 4, 1, :].unsqueeze(2).to_broadcast([nt, 4, 16, 16]), ALU.add), [vv], [cand])
                    yield
                    for hl in range(4):
                        h = h0 + hl
                        cf = cand[0:nt, hl].rearrange("p a b -> p (a b)")
                        c2 = cand2[0:nt, hl].rearrange("p a b -> p (a b)")
                        E("dve", lambda e, h=h, cf=cf: e.max(out=ts[0:nt, h, 0:8], in_=cf), [cand], [ts])
                        E("dve", lambda e, h=h, cf=cf: e.max_index(pu[0:nt, h, 0:8], ts[0:nt, h, 0:8], cf), [cand, ts], [pu])
                        E("dve", lambda e, h=h, cf=cf, c2=c2: e.match_replace(out=c2, in_to_replace=ts[0:nt, h, 0:8], in_values=cf, imm_value=-1e30), [cand, ts], [cand2])
                        yield
                        E("dve", lambda e, h=h, c2=c2: e.max(out=ts[0:nt, h, 8:16], in_=c2), [cand2], [ts])
                        E("dve", lambda e, h=h, c2=c2: e.max_index(pu[0:nt, h, 8:16], ts[0:nt, h, 8:16], c2), [cand2, ts], [pu])
                        yield
                E("dve", lambda e: e.tensor_tensor(gt[0:nt], ts[0:nt], ts[0:nt, :, 0:1].to_broadcast([nt, 8, 16]), ALU.subtract), [ts], [gt])
                E("act", lambda e: e.activation(gt[0:nt], gt[0:nt], AF.Exp), [gt], [gt])
                E("dve", lambda e: e.tensor_reduce(ssum[0:nt], gt[0:nt], axis=AX.X, op=ALU.add), [gt], [ssum])
                E("dve", lambda e: e.reciprocal(ssum[0:nt], ssum[0:nt]), [ssum], [ssum])
                E("dve", lambda e: e.tensor_tensor(gt[0:nt], gt[0:nt], ssum[0:nt].unsqueeze(2).to_broadcast([nt, 8, 16]), ALU.mult), [gt, ssum], [gt])
                yield
                E("dve", lambda e: e.tensor_single_scalar(p1u[0:nt], pu[0:nt], 4, ALU.logical_shift_right), [pu], [p1u])
                E("dve", lambda e: e.tensor_copy(p1f[0:nt], p1u[0:nt]), [p1u], [p1f])
                E("dve", lambda e: e.tensor_single_scalar(p1u[0:nt], pu[0:nt], 15, ALU.bitwise_and), [pu, p1f], [p1u])
                E("dve", lambda e: e.tensor_copy(p2f[0:nt], p1u[0:nt]), [p1u], [p2f])
                yield
                for hh_ in range(2):
                    h0 = hh_ * 4
                    for (pf, half, eo) in ((p1f, 0, e1), (p2f, 1, e2)):
                        oh = cand[0:nt].rearrange("p a b c -> p (a b) c")
                        E("dve", lambda e, pf=pf, h0=h0, oh=oh: e.tensor_tensor(
                            oh, pf[0:nt, h0:h0 + 4, :].rearrange("p a b -> p (a b)").unsqueeze(2).to_broadcast([nt, 64, 16]),
                            self.iota16[0:nt, :].unsqueeze(1).to_broadcast([nt, 64, 16]), ALU.is_equal), [pf, self.iota16], [cand])
                        E("dve", lambda e, half=half, h0=h0: e.tensor_tensor(
                            cand[0:nt], cand[0:nt], iff[0:nt, h0:h0 + 4, half, :].unsqueeze(2).to_broadcast([nt, 4, 16, 16]), ALU.mult), [cand, iff], [cand])
                        E("dve", lambda e, eo=eo, h0=h0, oh=oh: e.tensor_reduce(eo[0:nt, h0 * 16:(h0 + 4) * 16], oh, axis=AX.X, op=ALU.add), [cand], [eo])
                        yield
                E("dve", lambda e: e.scalar_tensor_tensor(out=e1[0:nt, :], in0=e1[0:nt, :], scalar=128.0, in1=e2[0:nt, :], op0=ALU.mult, op1=ALU.add), [e1, e2], [e1])
                if li > 0:
                    E("dve", lambda e: e.tensor_scalar_add(e1[0:nt, :], e1[0:nt, :], float(li * 16384)), [e1], [e1])
                E("dve", lambda e: e.tensor_copy(idx[b][0:nt, :], e1[0:nt, :]), [e1], [idx[b]])
                yield

            def G(td, b, rgen):
                nt = td.nt; hfb = hf[b]; ib = idx[b]
                gflat = gte[b][:, :, :].rearrange("p a b -> p (a b)")
                for j in range(128):
                    s = j % NB
                    E("pool", lambda e, j=j, s=s: e.indirect_dma_start(out=gb[s][:, :], out_offset=None, in_=uvtab,
                                                                        in_offset=bass.IndirectOffsetOnAxis(ap=ib[:, j:j + 1], axis=0)),
                      [ib], [gb[s]], dsem=dg[s])
                    E("dve", lambda e, j=j, s=s: e.scalar_tensor_tensor(out=junk[:, :], in0=hfb[:, :], scalar=1.0, in1=gb[s][:, 0:1024],
                                                                         op0=ALU.mult, op1=ALU.mult, accum_out=dots[:, j:j + 1]),
                      [hfb, gb[s]], [junk, dots])
                    E("act", lambda e, j=j: e.activation(gel[0:nt, j:j + 1], dots[0:nt, j:j + 1], AF.Gelu), [dots], [gel])
                    E("act", lambda e, j=j: e.activation(wgt[0:nt, j:j + 1], gel[0:nt, j:j + 1], AF.Identity, scale=gflat[0:nt, j:j + 1]), [gel, gte[b]], [wgt])
                    dt_ = dgt[j % 3]
                    E("act", lambda e, j=j, dt_=dt_: e.activation(dt_[0:nt, 0:nt], self.ident[0:nt, 0:nt], AF.Identity, scale=wgt[0:nt, j:j + 1]),
                      [wgt, self.ident], [dt_])
                    for hv in range(2):
                        aq = self.accq[hv]
                        E("pe", lambda e, j=j, s=s, hv=hv, aq=aq, dt_=dt_: e.matmul(aq.ap[0:nt, :], lhsT=dt_[0:nt, 0:nt],
                                                                                   rhs=gb[s][0:nt, 1024 + hv * 512:1536 + hv * 512],
                                                                                   start=(j == 0), stop=(j == 127)), [dt_, gb[s]], [aq])
                    if rgen is not None and j >= 4:
                        next(rgen, None)
                if rgen is not None:
                    for _ in rgen:
                        pass

            def Fin(td):
                nt = td.nt
                for hv in range(2):
                    self.evac("act" if hv else "dve", acc[0:nt, hv * 512:(hv + 1) * 512], self.accq[hv].ap[0:nt, :], [self.accq[hv]], [acc])
                srcs = []
                for k in range(8):
                    q, o = self.transpose_to(acc[0:nt, k * 128:(k + 1) * 128], nt, 128, [acc])
                    srcs.append((o, q.r))
                self.resid_ln(td, li, 5, srcs, col[f"lnfg{li}"], col[f"lnfb{li}"])
                if last:
                    for k in range(8):
                        q, o = self.transpose_to(self.xr(td, k), 128, nt, [self.xrr[td.idx]])
                        self.evac(self.alt(), yst[0:nt, k * 128:(k + 1) * 128], o, [q], [yst])
                    dst_ = dr["o_yp"][td.tok0:td.tok0 + 128, :] if td.prompt else dr["o_ys"]
                    E("sp", lambda e, dst_=dst_: e.dma_start(out=dst_, in_=yst[0:nt, :]), [yst], [], dsem=dy, is_out=True)

            for _ in R(self.tds[0], 0):
                pass
            for td in self.tds:
                b = td.idx % 2
                rgen = R(self.tds[td.idx + 1], 1 - b) if td.idx + 1 < NTILES else None
                G(td, b, rgen)
                Fin(td)
            self.fw.barrier()


def build_nc():
    nc = bass.Bass("TRN2", target_bir_lowering=False)
    dr = {}

    def din(name, shape, dt=F32):
        dr[name] = nc.dram_tensor(name, list(shape), dt, kind="ExternalInput").ap()

    def dout(name, shape):
        dr[name] = nc.dram_tensor(name, list(shape), F32, kind="ExternalOutput").ap()
    din("xp", [2048, 1024]); din("xs", [64, 1024]); din("slh", [16, 1280]); din("slc", [16, 3, 1280])
    din("scc", [16, 30, 1024]); din("cc", [17, 1024])
    din("w_ada", [2, 1024, 6144]); din("b_ada", [2, 6144])
    for n in ("ln_mix_g", "ln_mix_b", "ln_ffn_g", "ln_ffn_b"):
        din(n, [2, 1024])
    din("lru_w_in", [1, 1024, 2560]); din("lru_conv_w", [1, 4, 1280]); din("lru_conv_b", [1, 1280])
    din("lru_w_a", [1, 10, 128, 128]); din("lru_b_a", [1, 1280]); din("lru_w_i", [1, 10, 128, 128]); din("lru_b_i", [1, 1280])
    din("lru_lambda", [1, 1280]); din("lru_w_out", [1, 1280, 1024])
    din("conf_w_pw1", [1, 1024, 2048]); din("conf_b_pw1", [1, 2048]); din("conf_dw_w", [1, 31, 1024]); din("conf_dw_b", [1, 1024])
    din("conf_ln_g", [1, 1024]); din("conf_ln_b", [1, 1024]); din("conf_w_pw2", [1, 1024, 1024]); din("conf_b_pw2", [1, 1024])
    din("peer_w_q", [2, 1024, 2048]); din("peer_keys", [2, 8, 2, 128, 128]); din("peer_uv", [2, 16384, 2048])
    dout("o_yp", [2048, 1024]); dout("o_ys", [64, 1024]); dout("o_hp", [10, 128]); dout("o_lcp", [30, 128]); dout("o_ccp", [240, 128])
    dout("o_hs", [160, 128]); dout("o_lcs", [480, 128]); dout("o_ccs", [3840, 128])
    with ExitStack() as es:
        fw = FW(nc, es)
        k = Kern(nc, fw, es, dr)
        k.onesb = k.sb(es, "onesb", [128, 1], F32)
        k.E("dve", lambda e: e.memset(k.onesb[:, :], 1.0), [], [k.onesb])
        k.run()
        fw.finish()
    return nc


_WNAMES = ["w_ada", "b_ada", "ln_mix_g", "ln_mix_b", "ln_ffn_g", "ln_ffn_b", "lru_w_in", "lru_conv_w", "lru_conv_b", "lru_w_a", "lru_b_a",
           "lru_w_i", "lru_b_i", "lru_lambda", "lru_w_out", "conf_w_pw1", "conf_b_pw1", "conf_dw_w", "conf_dw_b", "conf_ln_g", "conf_ln_b",
           "conf_w_pw2", "conf_b_pw2", "peer_w_q", "peer_keys"]


def kernel(**inp):
    f = lambda a: np.ascontiguousarray(np.asarray(a, dtype=np.float32))
    W = {n: f(inp[n]) for n in _WNAMES}
    W["peer_uv"] = np.ascontiguousarray(np.concatenate([f(inp["peer_u"]), f(inp["peer_v"])], axis=2))
    xp = f(inp["x_prompt"]); xs = f(inp["x_sample"])
    slh = f(inp["state_lru_h"]); slc = f(inp["state_lru_conv"]); scc = f(inp["state_conf_conv"])
    cp = f(inp["c_prompt"]); cs = f(inp["c_sample"])
    in_maps = []
    for i in range(NCORES):
        m = dict(W)
        sl = slice(16 * i, 16 * i + 16)
        m["xp"] = xp[i]; m["xs"] = np.ascontiguousarray(xs[sl].reshape(64, 1024))
        m["slh"] = np.ascontiguousarray(slh[0, sl]); m["slc"] = np.ascontiguousarray(slc[0, sl]); m["scc"] = np.ascontiguousarray(scc[0, sl])
        m["cc"] = np.ascontiguousarray(np.concatenate([cp[i:i + 1], cs[sl]], 0))
        in_maps.append(m)
    nc = build_nc()
    res = run_bass_kernel_spmd(nc, in_maps, core_ids=list(range(NCORES)))
    R = res.results
    y_p = np.stack([R[i]["o_yp"] for i in range(NCORES)], 0).astype(np.float32)
    y_s = np.concatenate([R[i]["o_ys"].reshape(16, 4, 1024) for i in range(NCORES)], 0).astype(np.float32)
    h_p = np.stack([R[i]["o_hp"].reshape(1280) for i in range(NCORES)], 0)[None].astype(np.float32)
    lc_p = np.stack([R[i]["o_lcp"].reshape(3, 1280) for i in range(NCORES)], 0)[None].astype(np.float32)
    cc_p = np.stack([R[i]["o_ccp"].reshape(30, 1024) for i in range(NCORES)], 0)[None].astype(np.float32)
    h_s = np.concatenate([R[i]["o_hs"].reshape(16, 1280) for i in range(NCORES)], 0)[None].astype(np.float32)
    lc_s = np.concatenate([R[i]["o_lcs"].reshape(16, 3, 1280) for i in range(NCORES)], 0)[None].astype(np.float32)
    cc_s = np.concatenate([R[i]["o_ccs"].reshape(16, 30, 1024) for i in range(NCORES)], 0)[None].astype(np.float32)
    return (y_p, y_s, h_p, lc_p, cc_p, h_s, lc_s, cc_s)
```
